# Optimizing a Trainium2 kernel written in Bass

```python
import jax, jax.numpy as jnp
from jax import lax
import numpy as np

D_MODEL = 2048
BATCH = 2
SEQ = 16384
DEPTH = 2

HEAD_DIM = 128
ATTN_GROUPS = ((128, 1), (512, 4), (2048, 16))
ATTN_HEADS_PER_GROUP = 4
N_ATTN_GROUPS = len(ATTN_GROUPS)
N_ATTN_HEADS = N_ATTN_GROUPS * ATTN_HEADS_PER_GROUP
ATTN_WIDTH = N_ATTN_HEADS * HEAD_DIM
ATTN_OUT_WIDTH = ATTN_HEADS_PER_GROUP * HEAD_DIM
BLOCK = 128
ROPE_THETA = 10000.0
NEG_INF = -1e30

N_RET_HEADS = D_MODEL // 256
RET_QK_DIM = 256
RET_V_DIM = 512
RET_QK_WIDTH = N_RET_HEADS * RET_QK_DIM
RET_V_WIDTH = N_RET_HEADS * RET_V_DIM
CHUNK = 128
GN_EPS = 1e-5

D_FF = 5632
RMS_EPS = 1e-6

IN_SPLITS = (ATTN_WIDTH, ATTN_WIDTH, ATTN_WIDTH,
             RET_QK_WIDTH, RET_QK_WIDTH, RET_V_WIDTH, RET_V_WIDTH,
             D_MODEL, D_MODEL)
IN_COLS = sum(IN_SPLITS)

kernel_name = "hybrid_dilated_attn_retention_macaron"


def rms_norm(x, g):
    xf = x.astype(jnp.float32)
    y = xf * lax.rsqrt(jnp.mean(xf * xf, axis=-1, keepdims=True) + RMS_EPS)
    return (y * g.astype(jnp.float32)).astype(x.dtype)


def swiglu(h, w_gate, w_up, w_down):
    return (jax.nn.silu(h @ w_gate) * (h @ w_up)) @ w_down


def rope_tables(seq, dim):
    inv_freq = 1.0 / (ROPE_THETA ** (jnp.arange(0, dim, 2, dtype=jnp.float32) / dim))
    ang = jnp.arange(seq, dtype=jnp.float32)[:, None] * inv_freq[None, :]
    return jnp.cos(ang), jnp.sin(ang)


def apply_rotary(x, cos, sin):
    xf = x.astype(jnp.float32)
    half = xf.shape[-1] // 2
    x1, x2 = xf[..., :half], xf[..., half:]
    c, s = cos[:, None, :], sin[:, None, :]
    return jnp.concatenate([x1 * c - x2 * s, x2 * c + x1 * s], axis=-1)


def to_strided(x, dilation, nb):
    b, rest = x.shape[0], x.shape[2:]
    return x.reshape(b, nb * BLOCK, dilation, *rest).swapaxes(1, 2).reshape(b, dilation, nb, BLOCK, *rest)


def from_strided(x, seq):
    b, d, nb, blk = x.shape[:4]
    rest = x.shape[4:]
    return x.reshape(b, d, nb * blk, *rest).swapaxes(1, 2).reshape(b, nb * blk * d, *rest)[:, :seq]


def dilated_group_attention(q, k, v, window, dilation):
    steps = window // dilation
    b, s, h, dh = q.shape
    sub_len = -(-s // dilation)
    nb = -(-sub_len // BLOCK)
    sp = nb * BLOCK * dilation
    pad = ((0, 0), (0, sp - s), (0, 0), (0, 0))
    qs = to_strided(jnp.pad(q, pad), dilation, nb)
    ks = to_strided(jnp.pad(k, pad), dilation, nb)
    vs = to_strided(jnp.pad(v.astype(jnp.float32), pad), dilation, nb)
    prev = lambda t: jnp.pad(t, ((0, 0), (0, 0), (1, 0), (0, 0), (0, 0), (0, 0)))[:, :, :-1]
    kc = jnp.concatenate([prev(ks), ks], axis=3)
    vc = jnp.concatenate([prev(vs), vs], axis=3)
    scores = jnp.einsum('brnqhd,brnkhd->brnhqk', qs, kc)
    qi = jnp.arange(BLOCK)[:, None]
    kj = jnp.arange(2 * BLOCK)[None, :]
    blk = jnp.arange(nb)[:, None, None]
    dist = BLOCK + qi - kj
    valid = (dist >= 0) & (dist <= steps) & (blk * BLOCK + kj - BLOCK >= 0)
    valid = valid[None, None, :, None]
    scores = jnp.where(valid, scores, NEG_INF)
    m = jnp.max(scores, axis=-1, keepdims=True)
    p = jnp.where(valid, jnp.exp(scores - m), 0.0)
    l = jnp.sum(p, axis=-1)
    l_t = jnp.swapaxes(l, -1, -2)
    o = jnp.einsum('brnhqk,brnkhe->brnqhe', p, vc) / l_t[..., None]
    o = from_strided(o, s)
    m = from_strided(jnp.swapaxes(m[..., 0], -1, -2), s)
    l = from_strided(l_t, s)
    return o, m, l


def dilated_attention_mixer(q, k, v, cos, sin):
    b, s, _ = q.shape
    q = apply_rotary(q.reshape(b, s, N_ATTN_HEADS, HEAD_DIM), cos, sin) * (HEAD_DIM ** -0.5)
    k = apply_rotary(k.reshape(b, s, N_ATTN_HEADS, HEAD_DIM), cos, sin)
    v = v.reshape(b, s, N_ATTN_HEADS, HEAD_DIM)
    q = q.reshape(b, s, N_ATTN_GROUPS, ATTN_HEADS_PER_GROUP, HEAD_DIM)
    k = k.reshape(b, s, N_ATTN_GROUPS, ATTN_HEADS_PER_GROUP, HEAD_DIM)
    v = v.reshape(b, s, N_ATTN_GROUPS, ATTN_HEADS_PER_GROUP, HEAD_DIM)
    outs, maxes, dens = [], [], []
    for g, (window, dilation) in enumerate(ATTN_GROUPS):
        o, m, l = dilated_group_attention(q[:, :, g], k[:, :, g], v[:, :, g], window, dilation)
        outs.append(o)
        maxes.append(m)
        dens.append(l)
    o = jnp.stack(outs)
    m = jnp.stack(maxes)
    l = jnp.stack(dens)
    weight = l * jnp.exp(m - jnp.max(m, axis=0, keepdims=True))
    o = jnp.sum(weight[..., None] * o, axis=0) / jnp.sum(weight, axis=0)[..., None]
    return o.reshape(b, s, ATTN_OUT_WIDTH)


def retention_mixer(q, k, v, g, cos, sin):
    b, s, _ = q.shape
    n = s // CHUNK
    log_gamma = jnp.log1p(-jnp.exp2(-5.0 - jnp.arange(N_RET_HEADS, dtype=jnp.float32)))
    q = apply_rotary(q.reshape(b, s, N_RET_HEADS, RET_QK_DIM), cos, sin)
    k = apply_rotary(k.reshape(b, s, N_RET_HEADS, RET_QK_DIM), cos, sin) * (RET_QK_DIM ** -0.5)
    v = v.astype(jnp.float32).reshape(b, s, N_RET_HEADS, RET_V_DIM)
    qc = q.reshape(b, n, CHUNK, N_RET_HEADS, RET_QK_DIM)
    kc = k.reshape(b, n, CHUNK, N_RET_HEADS, RET_QK_DIM)
    vc = v.reshape(b, n, CHUNK, N_RET_HEADS, RET_V_DIM)
    idx = jnp.arange(CHUNK, dtype=jnp.float32)
    rel = idx[:, None] - idx[None, :]
    decay_mask = jnp.where(rel >= 0, jnp.exp(log_gamma[:, None, None] * jnp.maximum(rel, 0.0)), 0.0)
    inner = jnp.einsum('bnihd,bnjhd->bnhij', qc, kc) * decay_mask
    o_inner = jnp.einsum('bnhij,bnjhe->bnihe', inner, vc)
    q_dec = qc * jnp.exp(log_gamma[None, :] * (idx + 1.0)[:, None])[:, :, None]
    k_dec = kc * jnp.exp(log_gamma[None, :] * (CHUNK - 1.0 - idx)[:, None])[:, :, None]
    chunk_decay = jnp.exp(log_gamma * CHUNK)[None, :, None, None]

    def step(state, xs):
        qn, kn, vn = xs
        out = jnp.einsum('bihd,bhde->bihe', qn, state)
        state = chunk_decay * state + jnp.einsum('bjhd,bjhe->bhde', kn, vn)
        return state, out

    state0 = jnp.zeros((b, N_RET_HEADS, RET_QK_DIM, RET_V_DIM), jnp.float32)
    _, o_cross = lax.scan(step, state0, (q_dec.swapaxes(0, 1), k_dec.swapaxes(0, 1), vc.swapaxes(0, 1)))
    o = (o_inner + o_cross.swapaxes(0, 1)).reshape(b, s, N_RET_HEADS, RET_V_DIM)
    mu = jnp.mean(o, axis=-1, keepdims=True)
    var = jnp.mean(jnp.square(o - mu), axis=-1, keepdims=True)
    o = (o - mu) * lax.rsqrt(var + GN_EPS)
    gate = jax.nn.silu(g.astype(jnp.float32)).reshape(b, s, N_RET_HEADS, RET_V_DIM)
    return (o * gate).reshape(b, s, RET_V_WIDTH)


def setup_inputs(seed: int = 0) -> dict:
    key = jax.random.key(seed)
    ks = jax.random.split(key, 16)

    def w(k, shape, fan_in):
        return jax.random.normal(k, shape, jnp.float32) * (fan_in ** -0.5)

    def gain(k, shape):
        return 1.0 + 0.01 * jax.random.normal(k, shape, jnp.float32)

    return {
        "x": jax.random.normal(ks[0], (BATCH, SEQ, D_MODEL), jnp.float32),
        "ffn1_norm": gain(ks[1], (DEPTH, D_MODEL)),
        "ffn1_w_gate": w(ks[2], (DEPTH, D_MODEL, D_FF), D_MODEL),
        "ffn1_w_up": w(ks[3], (DEPTH, D_MODEL, D_FF), D_MODEL),
        "ffn1_w_down": w(ks[4], (DEPTH, D_FF, D_MODEL), D_FF),
        "mix_norm": gain(ks[5], (DEPTH, D_MODEL)),
        "w_in": w(ks[6], (DEPTH, D_MODEL, IN_COLS), D_MODEL),
        "w_proj_attn": w(ks[7], (DEPTH, ATTN_OUT_WIDTH, D_MODEL), ATTN_OUT_WIDTH),
        "w_proj_ret": w(ks[8], (DEPTH, RET_V_WIDTH, D_MODEL), RET_V_WIDTH),
        "w_out": w(ks[9], (DEPTH, D_MODEL, D_MODEL), D_MODEL),
        "ffn2_norm": gain(ks[10], (DEPTH, D_MODEL)),
        "ffn2_w_gate": w(ks[11], (DEPTH, D_MODEL, D_FF), D_MODEL),
        "ffn2_w_up": w(ks[12], (DEPTH, D_MODEL, D_FF), D_MODEL),
        "ffn2_w_down": w(ks[13], (DEPTH, D_FF, D_MODEL), D_FF),
        "final_norm": gain(ks[14], (D_MODEL,)),
    }


def reference(x, ffn1_norm, ffn1_w_gate, ffn1_w_up, ffn1_w_down, mix_norm, w_in,
              w_proj_attn, w_proj_ret, w_out, ffn2_norm, ffn2_w_gate, ffn2_w_up,
              ffn2_w_down, final_norm):
    seq = x.shape[1]
    cos_a, sin_a = rope_tables(seq, HEAD_DIM)
    cos_r, sin_r = rope_tables(seq, RET_QK_DIM)
    offsets = [int(o) for o in np.cumsum(IN_SPLITS)[:-1]]
    for layer in range(DEPTH):
        x = x + 0.5 * swiglu(rms_norm(x, ffn1_norm[layer]), ffn1_w_gate[layer],
                             ffn1_w_up[layer], ffn1_w_down[layer])
        h = rms_norm(x, mix_norm[layer])
        proj = h @ w_in[layer]
        q_a, k_a, v_a, q_r, k_r, v_r, g_r, gate_a, gate_b = jnp.split(proj, offsets, axis=-1)
        o_a = dilated_attention_mixer(q_a, k_a, v_a, cos_a, sin_a).astype(x.dtype)
        o_r = retention_mixer(q_r, k_r, v_r, g_r, cos_r, sin_r).astype(x.dtype)
        merged = (jax.nn.sigmoid(gate_a) * (o_a @ w_proj_attn[layer])
                  + jax.nn.sigmoid(gate_b) * (o_r @ w_proj_ret[layer]))
        x = x + merged @ w_out[layer]
        x = x + 0.5 * swiglu(rms_norm(x, ffn2_norm[layer]), ffn2_w_gate[layer],
                             ffn2_w_up[layer], ffn2_w_down[layer])
    return rms_norm(x, final_norm)
```

```python
import os
from contextlib import ExitStack, contextmanager
import numpy as np
import concourse.bass as bass
import concourse.mybir as mybir
from concourse.bass_utils import run_bass_kernel_spmd

F32 = mybir.dt.float32
BF16 = mybir.dt.bfloat16
ALU = mybir.AluOpType
AF = mybir.ActivationFunctionType
AX = mybir.AxisListType

NCORES = 8
T = 4096
D = 2048
DFF = 5632
DEPTH = 2
INC = 20992
NEG = -1.0e30
RMS_EPS = 1e-6
GN_EPS = 1e-5
GAMMA = [1.0 - 2.0 ** (-5.0 - h) for h in range(8)]
O_QA, O_KA, O_VA, O_QR, O_KR, O_VR, O_GR, O_GA, O_GB = 0, 1536, 3072, 4608, 6656, 8704, 12800, 16896, 18944
C_ID = 0
C_RMASK = 128
C_RDQ = C_RMASK + 1024
C_RDK = C_RDQ + 1024
C_COEF = C_RDK + 8
C_SEL = C_COEF + 32
C_AMASK = C_SEL + 4
C_GAIN = C_AMASK + 512
C_DKABS = C_GAIN + 112
CW = C_DKABS + 256

STOP = os.environ.get("KSTOP", "")
SKIP = os.environ.get("KSKIP", "").split(",")
USED_W = {}
DBG = [s for s in os.environ.get("KDBG", "").split(",") if s]


class Buf:
    __slots__ = ("name", "w", "r", "ds")

    def __init__(self, name):
        self.name = name
        self.w = {}
        self.r = {}
        self.ds = None


def _merge(d, src):
    for k, (sem, v) in src.items():
        if k not in d or d[k][1] < v:
            d[k] = (sem, v)


class Tl:
    def __init__(self, h, name, psum=False):
        self.h = h
        self.b = Buf(name)
        self.psum = psum

    def __getitem__(self, k):
        return self.h[k]


class Cx:
    def __init__(self, nc):
        self.nc = nc
        self.eng = {"pe": nc.tensor, "act": nc.scalar, "dve": nc.vector, "pool": nc.gpsimd, "sp": nc.sync}
        self.psem = {}
        self.cnt = {}
        for e in ["pe", "act", "dve", "pool"]:
            self.psem[e] = nc.alloc_semaphore("s_" + e)
            self.cnt[e] = 0
        self.known = {e: {} for e in self.eng}
        self.dpool = [[nc.alloc_semaphore("d%d" % i), 0, "d%d" % i] for i in range(44)]
        self.dfree = list(range(44))
        self.dused = []
        self.uid = 0
        self.st = None
        self.ninst = 0

    def sb(self, shape, dtype, name="t"):
        self.uid += 1
        nm = "%s_%d" % (name, self.uid)
        h = self.st.enter_context(self.nc.sbuf_tensor(nm, list(shape), dtype))
        return Tl(h, nm)

    def ps(self, shape, dtype, name="p"):
        self.uid += 1
        nm = "%s_%d" % (name, self.uid)
        h = self.st.enter_context(self.nc.psum_tensor(nm, list(shape), dtype))
        return Tl(h, nm, psum=True)

    def _dsem(self, buf):
        if buf.ds is None:
            i = self.dfree.pop(0)
            buf.ds = self.dpool[i]
            self.dused.append((i, buf))
        return buf.ds

    def _wait(self, e, toks):
        for name, (sem, val) in toks.items():
            if e == "pe" and name == "s_pe":
                continue
            if self.known[e].get(name, 0) < val:
                self.eng[e].wait_ge(sem, val)
                self.known[e][name] = val
                self.ninst += 1

    def _deps(self, reads, writes):
        deps = {}
        for b in reads:
            _merge(deps, b.w)
        for b in writes:
            _merge(deps, b.w)
            _merge(deps, b.r)
        return deps

    def _mark(self, tokname, tok, reads, writes):
        t = {tokname: tok}
        for b in writes:
            _merge(b.w, t)
            b.r = {}
        for b in reads:
            _merge(b.r, t)

    def op(self, e, fn, reads=(), writes=()):
        writes = list(writes) + [x for x in reads if isinstance(x, Tl) and x.psum]
        reads = [x.b if isinstance(x, Tl) else x for x in reads if not (isinstance(x, Tl) and x.psum)]
        writes = [x.b if isinstance(x, Tl) else x for x in writes]
        self._wait(e, self._deps(reads, writes))
        ins = fn(self.eng[e])
        self.cnt[e] += 1
        ins.then_inc(self.psem[e], 1)
        self.ninst += 1
        self._mark("s_" + e, (self.psem[e], self.cnt[e]), reads, writes)

    def dma(self, e, out, in_, sbuf, reads=(), writes=()):
        reads = [x.b if isinstance(x, Tl) else x for x in reads]
        writes = [x.b if isinstance(x, Tl) else x for x in writes]
        sb = sbuf.b if isinstance(sbuf, Tl) else sbuf
        self._wait(e, self._deps(reads, writes))
        ds = self._dsem(sb)
        ins = self.eng[e].dma_start(out=out, in_=in_)
        ds[1] += 16
        ins.then_inc(ds[0], 16)
        self.ninst += 1
        self._mark(ds[2], (ds[0], ds[1]), reads, writes)

    def barrier(self):
        allt = {}
        for e in self.psem:
            if self.cnt[e] > 0:
                allt["s_" + e] = (self.psem[e], self.cnt[e])
        for s, c, n in self.dpool:
            if c > 0:
                allt[n] = (s, c)
        for e in self.eng:
            self._wait(e, allt)
        for i, b in self.dused:
            b.ds = None
            self.dfree.append(i)
        self.dused = []

    @contextmanager
    def phase(self, name):
        st = ExitStack()
        self.st = st
        try:
            yield st
        finally:
            self.barrier()
            st.close()


def linear_phase(cx, G, TB, NS, groups, xsrc=None, norm=None, need=()):
    nc = cx.nc
    KC = 16 if norm is not None else sum(a.shape[0] for a in xsrc) // 128
    NSUB = TB // 512
    X = cx.sb([128, KC, TB], BF16, "X")
    maxk = max(max(w.shape[0] // 128 for (w, _) in g["outs"]) for g in groups)
    nouts = max(len(g["outs"]) for g in groups)
    slabs = [[cx.sb([128, maxk, 256], BF16, "slab") for _ in range(nouts)] for _ in range(2)]
    NPB = 3
    banks = [[cx.ps([128, 512], F32, "bk") for _ in range(2)] for _ in range(NPB)]
    stgA = [cx.sb([128, TB], BF16, "stgA") for _ in range(2)] if "stgA" in need else None
    stgB = [cx.sb([128, TB], BF16, "stgB") for _ in range(2)] if "stgB" in need else None
    tmp = [cx.sb([128, 512], F32, "tmp") for _ in range(4)] if "tmp" in need else None
    xin = [cx.sb([128, TB], F32, "xin") for _ in range(2)] if "res" in need else None
    xo = [cx.sb([128, TB], F32, "xo") for _ in range(2)] if "res" in need else None
    gta = [cx.sb([128, TB], BF16, "gta") for _ in range(2)] if "gate" in need else None
    gtb = [cx.sb([128, TB], BF16, "gtb") for _ in range(2)] if "gate" in need else None
    rtab = cx.sb([128, 4, TB], F32, "rtab") if "rope" in need else None
    if norm is not None:
        xs32 = cx.sb([128, 16, NS], F32, "xs32")
        sq = cx.sb([128, 16, NS], BF16, "sq")
        psn = cx.ps([128, 512], F32, "psn")
        rs0 = cx.sb([128, NS], F32, "rs0")
        rstd = cx.sb([128, NS], F32, "rstd")
    rot = {"bank": 0, "stg": 0, "tmp": 0, "res": 0, "gate": 0}

    def load_slabs(gi):
        g = groups[gi]
        for oi, (w, kc0) in enumerate(g["outs"]):
            sl = slabs[gi % 2][oi]
            nk = w.shape[0] // 128
            cx.dma("pool", sl[:, 0:nk, :], w.rearrange("(c p) f -> p c f", p=128), sl, writes=[sl])

    for tb in range(T // TB):
        t0 = tb * TB
        if norm is not None:
            x_ap, gidx = norm
            for s in range(TB // NS):
                ts0 = t0 + s * NS
                cx.dma("sp", xs32[:], x_ap.rearrange("(c p) t -> p c t", p=128)[:, :, ts0:ts0 + NS], xs32, writes=[xs32])
                cx.op("act", lambda a: a.activation(out=sq[:], in_=xs32[:], func=AF.Square), reads=[xs32], writes=[sq])
                for c in range(16):
                    cx.op("pe", lambda p, c=c: p.matmul(psn[:, 0:NS], lhsT=G["ones"][:], rhs=sq[:, c, :], start=(c == 0), stop=(c == 15)),
                          reads=[sq, G["onesb"]], writes=[psn])
                cx.op("act", lambda a: a.activation(out=rs0[:], in_=psn[:, 0:NS], func=AF.Sqrt, scale=1.0 / D, bias=G["eps"][:, 0:1]),
                      reads=[psn, G["cstb"]], writes=[rs0])
                cx.op("dve", lambda v: v.reciprocal(out=rstd[:], in_=rs0[:]), reads=[rs0], writes=[rstd])
                for c in range(16):
                    cx.op("dve", lambda v, c=c: v.scalar_tensor_tensor(
                        out=X[:, c, s * NS:(s + 1) * NS], in0=xs32[:, c, :],
                        scalar=G["cst"][:, C_GAIN + gidx * 16 + c:C_GAIN + gidx * 16 + c + 1],
                        in1=rstd[:], op0=ALU.mult, op1=ALU.mult), reads=[xs32, rstd, G["cstb"]], writes=[X])
        else:
            kc = 0
            for a in xsrc:
                n = a.shape[0] // 128
                av = a.rearrange("(c p) t -> p c t", p=128)
                for c0 in range(0, n, 8):
                    c1 = min(n, c0 + 8)
                    cx.dma("sp", X[:, kc + c0:kc + c1, :], av[:, c0:c1, t0:t0 + TB], X, writes=[X])
                kc += n
        if rtab is not None:
            ra, rr = G["ropeA"], G["ropeR"]
            cx.dma("sp", rtab[:, 0:2, :], ra.rearrange("k p t -> p k t")[:, :, t0:t0 + TB], rtab, writes=[rtab])
            cx.dma("sp", rtab[:, 2:4, :], rr.rearrange("k p t -> p k t")[:, :, t0:t0 + TB], rtab, writes=[rtab])
        load_slabs(0)
        for gi, g in enumerate(groups):
            if gi + 1 < len(groups):
                load_slabs(gi + 1)
            kind = g["kind"]
            sl = slabs[gi % 2]
            for ui, unit in enumerate(g["units"]):
                if kind in ("copy", "silu", "sigmoid", "swiglu", "ropeA", "ropeR", "gate"):
                    sA = stgA[rot["stg"] % 2]
                    sB = stgB[rot["stg"] % 2] if stgB is not None else None
                    rot["stg"] += 1
                if kind == "res":
                    xi = xin[rot["res"] % 2]
                    xoo = xo[rot["res"] % 2]
                    rot["res"] += 1
                    r0 = g["row0"] + ui * 128
                    cx.dma("sp", xi[:], g["xin"][r0:r0 + 128, t0:t0 + TB], xi, writes=[xi])
                if kind == "gate":
                    ga_t = gta[rot["gate"] % 2]
                    gb_t = gtb[rot["gate"] % 2]
                    rot["gate"] += 1
                    r0 = g["row0"] + ui * 128
                    cx.dma("sp", ga_t[:], g["ga"][r0:r0 + 128, t0:t0 + TB], ga_t, writes=[ga_t])
                    cx.dma("sp", gb_t[:], g["gb"][r0:r0 + 128, t0:t0 + TB], gb_t, writes=[gb_t])
                for ts in range(NSUB):
                    bk = banks[rot["bank"] % NPB]
                    rot["bank"] += 1
                    cs = slice(ts * 512, (ts + 1) * 512)
                    for j, (oi, col) in enumerate(unit):
                        w, kc0 = g["outs"][oi]
                        nk = w.shape[0] // 128
                        for kc in range(nk):
                            cx.op("pe", lambda p, j=j, oi=oi, col=col, kc=kc, kc0=kc0, nk=nk: p.matmul(
                                bk[j][:], lhsT=sl[oi][:, kc, col:col + 128], rhs=X[:, kc0 + kc, cs],
                                start=(kc == 0), stop=(kc == nk - 1)), reads=[sl[oi], X], writes=[bk[j]])
                    if kind in ("copy", "silu", "sigmoid"):
                        fn = {"copy": AF.Copy, "silu": AF.Silu, "sigmoid": AF.Sigmoid}[kind]
                        cx.op("act", lambda a, fn=fn: a.activation(out=sA[:, cs], in_=bk[0][:], func=fn), reads=[bk[0]], writes=[sA])
                    elif kind == "swiglu":
                        tm = tmp[rot["tmp"] % 4]
                        rot["tmp"] += 1
                        cx.op("act", lambda a: a.activation(out=tm[:], in_=bk[0][:], func=AF.Silu), reads=[bk[0]], writes=[tm])
                        cx.op("dve", lambda v: v.tensor_tensor(out=sA[:, cs], in0=bk[1][:], in1=tm[:], op=ALU.mult), reads=[bk[1], tm], writes=[sA])
                    elif kind in ("ropeA", "ropeR"):
                        k0 = 0 if kind == "ropeA" else 2
                        cc = rtab[:, k0, cs]
                        ss = rtab[:, k0 + 1, cs]
                        t1, t2, t3, t4 = [tmp[(rot["tmp"] + i) % 4] for i in range(4)]
                        rot["tmp"] += 4
                        A, B = bk[0], bk[1]
                        cx.op("dve", lambda v: v.tensor_tensor(out=t1[:], in0=A[:], in1=cc, op=ALU.mult), reads=[A, rtab], writes=[t1])
                        cx.op("dve", lambda v: v.tensor_tensor(out=t2[:], in0=B[:], in1=ss, op=ALU.mult), reads=[B, rtab], writes=[t2])
                        cx.op("pool", lambda v: v.tensor_tensor(out=sA[:, cs], in0=t1[:], in1=t2[:], op=ALU.subtract), reads=[t1, t2], writes=[sA])
                        cx.op("dve", lambda v: v.tensor_tensor(out=t3[:], in0=B[:], in1=cc, op=ALU.mult), reads=[B, rtab], writes=[t3])
                        cx.op("dve", lambda v: v.tensor_tensor(out=t4[:], in0=A[:], in1=ss, op=ALU.mult), reads=[A, rtab], writes=[t4])
                        cx.op("pool", lambda v: v.tensor_tensor(out=sB[:, cs], in0=t3[:], in1=t4[:], op=ALU.add), reads=[t3, t4], writes=[sB])
                    elif kind == "res":
                        sc = g["scale"]
                        cx.op("dve", lambda v: v.scalar_tensor_tensor(out=xoo[:, cs], in0=bk[0][:], scalar=sc, in1=xi[:, cs],
                                                                      op0=ALU.mult, op1=ALU.add), reads=[bk[0], xi], writes=[xoo])
                    elif kind == "gate":
                        t1, t2 = [tmp[(rot["tmp"] + i) % 4] for i in range(2)]
                        rot["tmp"] += 2
                        cx.op("dve", lambda v: v.tensor_tensor(out=t1[:], in0=bk[0][:], in1=ga_t[:, cs], op=ALU.mult), reads=[bk[0], ga_t], writes=[t1])
                        cx.op("dve", lambda v: v.tensor_tensor(out=t2[:], in0=bk[1][:], in1=gb_t[:, cs], op=ALU.mult), reads=[bk[1], gb_t], writes=[t2])
                        cx.op("pool", lambda v: v.tensor_tensor(out=sA[:, cs], in0=t1[:], in1=t2[:], op=ALU.add), reads=[t1, t2], writes=[sA])
                tsl = slice(t0, t0 + TB)
                if kind in ("copy", "silu", "sigmoid", "swiglu", "gate"):
                    r0 = g["row0"] + ui * 128
                    cx.dma("sp", g["dst"][r0:r0 + 128, tsl], sA[:], sA, reads=[sA])
                elif kind == "ropeR":
                    r0 = g["row0"]
                    cx.dma("sp", g["dst"][r0:r0 + 128, tsl], sA[:], sA, reads=[sA])
                    cx.dma("sp", g["dst"][r0 + 128:r0 + 256, tsl], sB[:], sB, reads=[sB])
                elif kind == "ropeA":
                    r0 = g["row0"]
                    cx.dma("sp", g["dst"][r0:r0 + 64, tsl], sA[0:64, :], sA, reads=[sA])
                    cx.dma("sp", g["dst"][r0 + 128:r0 + 192, tsl], sA[64:128, :], sA, reads=[sA])
                    cx.dma("sp", g["dst"][r0 + 64:r0 + 128, tsl], sB[0:64, :], sB, reads=[sB])
                    cx.dma("sp", g["dst"][r0 + 192:r0 + 256, tsl], sB[64:128, :], sB, reads=[sB])
                elif kind == "res":
                    r0 = g["row0"] + ui * 128
                    cx.dma("sp", g["dst"][r0:r0 + 128, tsl], xoo[:], xoo, reads=[xoo])


def skew(n, stages):
    md = max(d for _, d in stages)
    for t in range(n + md):
        for fn, d in stages:
            i = t - d
            if 0 <= i < n:
                fn(i)


def simple_groups(w, ncols, kind, dst, col0=0, row0=0, **kw):
    gs = []
    for s in range(ncols // 256):
        g = dict(outs=[(w[:, col0 + s * 256:col0 + (s + 1) * 256], 0)], units=[[(0, 0)], [(0, 128)]], kind=kind, dst=dst,
                 row0=row0 + s * 256)
        g.update(kw)
        gs.append(g)
    return gs


def final_norm_phase(cx, G, x_ap, gidx, y_ap):
    NS = 256
    xs32 = [cx.sb([128, 16, NS], F32, "xs32") for _ in range(2)]
    sq = cx.sb([128, 16, NS], BF16, "sq")
    psn = cx.ps([128, 512], F32, "psn")
    rs0 = cx.sb([128, NS], F32, "rs0")
    rstd = cx.sb([128, NS], F32, "rstd")
    yo = [cx.sb([128, 16, NS], F32, "yo") for _ in range(2)]
    for s in range(T // NS):
        ts0 = s * NS
        x3 = xs32[s % 2]
        y3 = yo[s % 2]
        cx.dma("sp", x3[:], x_ap.rearrange("(c p) t -> p c t", p=128)[:, :, ts0:ts0 + NS], x3, writes=[x3])
        cx.op("act", lambda a: a.activation(out=sq[:], in_=x3[:], func=AF.Square), reads=[x3], writes=[sq])
        for c in range(16):
            cx.op("pe", lambda p, c=c: p.matmul(psn[:, 0:NS], lhsT=G["ones"][:], rhs=sq[:, c, :], start=(c == 0), stop=(c == 15)),
                  reads=[sq, G["onesb"]], writes=[psn])
        cx.op("act", lambda a: a.activation(out=rs0[:], in_=psn[:, 0:NS], func=AF.Sqrt, scale=1.0 / D, bias=G["eps"][:, 0:1]),
              reads=[psn, G["cstb"]], writes=[rs0])
        cx.op("dve", lambda v: v.reciprocal(out=rstd[:], in_=rs0[:]), reads=[rs0], writes=[rstd])
        for c in range(16):
            cx.op("dve", lambda v, c=c: v.scalar_tensor_tensor(
                out=y3[:, c, :], in0=x3[:, c, :], scalar=G["cst"][:, C_GAIN + gidx * 16 + c:C_GAIN + gidx * 16 + c + 1],
                in1=rstd[:], op0=ALU.mult, op1=ALU.mult), reads=[x3, rstd, G["cstb"]], writes=[y3])
        cx.dma("sp", y_ap.rearrange("(c p) t -> p c t", p=128)[:, :, ts0:ts0 + NS], y3[:], y3, reads=[y3])


def retention_pass(cx, G, S, first):
    nc = cx.nc
    BT = 128
    cst = G["cst"]
    qB = [cx.sb([128, 16, BT], BF16, "rq") for _ in range(2)]
    kB = [cx.sb([128, 16, BT], BF16, "rk") for _ in range(2)]
    vB = [cx.sb([128, 32, BT], BF16, "rv") for _ in range(2)]
    sgB = [cx.sb([128, 32, BT], BF16, "rsg") for _ in range(2)]
    ostB = [cx.sb([128, 32, BT], BF16, "rost") for _ in range(2)]
    Sf, Sb = S
    psA = cx.ps([128, 512], F32, "psA")
    psO2 = [cx.ps([128, 512], F32, "psO") for _ in range(2)]
    psOT = cx.ps([128, 1024], BF16, "psOT")
    psT = [cx.ps([128, 1024], BF16, "psT") for _ in range(2)]
    psS = [cx.ps([128, 512], F32, "psS") for _ in range(2)]
    ident = G["ident"]
    assert not first
    atm = [cx.sb([128, 128], BF16, "atm") for _ in range(3)]
    qdec = [cx.sb([128, 2, 128], BF16, "qdec") for _ in range(3)]
    on = [cx.sb([128, 512], BF16, "on") for _ in range(4)]
    st6 = [cx.sb([128, 6], F32, "st6") for _ in range(4)]
    mv = [cx.sb([128, 2], F32, "mv") for _ in range(4)]
    sd = [cx.sb([128, 1], F32, "sd") for _ in range(4)]
    rs = [cx.sb([128, 1], F32, "rs") for _ in range(4)]
    nb = [cx.sb([128, 1], F32, "nb") for _ in range(4)]
    kdec = [cx.sb([128, 256], BF16, "kdec") for _ in range(3)]
    vtok = [cx.sb([128, 512], BF16, "vtok") for _ in range(3)]
    NBLK = T // BT

    def load_block(lb):
        tsl = slice(lb * BT, (lb + 1) * BT)
        q, k, v, sg = qB[lb % 2], kB[lb % 2], vB[lb % 2], sgB[lb % 2]
        cx.dma("sp", k[:], G["krT"].rearrange("(c p) t -> p c t", p=128)[:, :, tsl], k, writes=[k])
        cx.dma("sp", q[:], G["qrT"].rearrange("(c p) t -> p c t", p=128)[:, :, tsl], q, writes=[q])
        cx.dma("sp", v[:, 0:16, :], G["vrT"].rearrange("(c p) t -> p c t", p=128)[:, 0:16, tsl], v, writes=[v])
        cx.dma("sp", v[:, 16:32, :], G["vrT"].rearrange("(c p) t -> p c t", p=128)[:, 16:32, tsl], v, writes=[v])
        cx.dma("sp", sg[:, 0:16, :], G["sgT"].rearrange("(c p) t -> p c t", p=128)[:, 0:16, tsl], sg, writes=[sg])
        cx.dma("sp", sg[:, 16:32, :], G["sgT"].rearrange("(c p) t -> p c t", p=128)[:, 16:32, tsl], sg, writes=[sg])

    def stA(i):
        blk, h = divmod(i, 8)
        if i == 0:
            load_block(0)
        if h == 4 and blk + 1 < NBLK:
            load_block(blk + 1)
        q, k, v = qB[blk % 2], kB[blk % 2], vB[blk % 2]
        pT = psT[i % 2]
        for dc in range(2):
            cx.op("pe", lambda p, dc=dc: p.transpose(out=pT[:, dc * 128:(dc + 1) * 128], in_=k[:, 2 * h + dc, :], identity=ident[:]),
                  reads=[k, G["identb"]], writes=[pT])
        for ec in range(4):
            cx.op("pe", lambda p, ec=ec: p.transpose(out=pT[:, 256 + ec * 128:256 + (ec + 1) * 128], in_=v[:, 4 * h + ec, :], identity=ident[:]),
                  reads=[v, G["identb"]], writes=[pT])
        for dc in range(2):
            cx.op("pe", lambda p, dc=dc: p.matmul(psA[:, 0:128], lhsT=k[:, 2 * h + dc, :], rhs=q[:, 2 * h + dc, :], start=(dc == 0), stop=(dc == 1)),
                  reads=[k, q], writes=[psA])

    def stB(i):
        blk, h = divmod(i, 8)
        q = qB[blk % 2]
        pT, kd_, vt_, at_, qd_ = psT[i % 2], kdec[i % 3], vtok[i % 3], atm[i % 3], qdec[i % 3]
        cx.op("dve", lambda e: e.tensor_scalar(out=kd_[:], in0=pT[:, 0:256], scalar1=cst[:, C_RDK + h:C_RDK + h + 1], scalar2=None, op0=ALU.mult),
              reads=[pT, G["cstb"]], writes=[kd_])
        cx.op("act", lambda a: a.activation(out=vt_[:], in_=pT[:, 256:768], func=AF.Copy), reads=[pT], writes=[vt_])
        cx.op("dve", lambda e: e.tensor_tensor(out=at_[:], in0=psA[:, 0:128], in1=cst[:, C_RMASK + h * 128:C_RMASK + (h + 1) * 128], op=ALU.mult),
              reads=[psA, G["cstb"]], writes=[at_])
        for dc in range(2):
            cx.op("pool", lambda e, dc=dc: e.tensor_tensor(out=qd_[:, dc, :], in0=q[:, 2 * h + dc, :], in1=cst[:, C_RDQ + h * 128:C_RDQ + (h + 1) * 128], op=ALU.mult),
                  reads=[q, G["cstb"]], writes=[qd_])

    def stC(i):
        blk, h = divmod(i, 8)
        kd_, vt_, at_, qd_ = kdec[i % 3], vtok[i % 3], atm[i % 3], qdec[i % 3]
        psO = psO2[i % 2]
        for dc in range(2):
            cx.op("pe", lambda p, dc=dc: p.matmul(psO[:], lhsT=qd_[:, dc, :], rhs=Sb[:, h, dc, :], start=(dc == 0), stop=False),
                  reads=[qd_, Sb], writes=[psO])
        cx.op("pe", lambda p: p.matmul(psO[:], lhsT=at_[:], rhs=vt_[:], start=False, stop=True), reads=[at_, vt_], writes=[psO])
        for dc in range(2):
            cx.op("pe", lambda p, dc=dc: p.matmul(psS[dc][:], lhsT=kd_[:, dc * 128:(dc + 1) * 128], rhs=vt_[:], start=True, stop=True),
                  reads=[kd_, vt_], writes=[psS[dc]])

    def stD(i):
        blk, h = divmod(i, 8)
        psO = psO2[i % 2]
        on_ = on[i % 4]
        cd = GAMMA[h] ** 128
        for dc in range(2):
            cx.op("dve", lambda e, dc=dc: e.scalar_tensor_tensor(out=Sf[:, h, dc, :], in0=Sf[:, h, dc, :], scalar=cd, in1=psS[dc][:], op0=ALU.mult, op1=ALU.add),
                  reads=[psS[dc]], writes=[Sf])
        for dc in range(2):
            cx.op("pool", lambda e, dc=dc: e.tensor_copy(out=Sb[:, h, dc, :], in_=Sf[:, h, dc, :]), reads=[Sf], writes=[Sb])
        s6, mv_, sd_, rs_, nb_ = st6[i % 4], mv[i % 4], sd[i % 4], rs[i % 4], nb[i % 4]
        cx.op("dve", lambda e: e.bn_stats(out=s6[:], in_=psO[:]), reads=[psO], writes=[s6])
        cx.op("dve", lambda e: e.bn_aggr(out=mv_[:], in_=s6[:]), reads=[s6], writes=[mv_])
        cx.op("act", lambda a: a.activation(out=sd_[:], in_=mv_[:, 1:2], func=AF.Sqrt, bias=G["eps"][:, 1:2]), reads=[mv_, G["cstb"]], writes=[sd_])
        cx.op("dve", lambda e: e.reciprocal(out=rs_[:], in_=sd_[:]), reads=[sd_], writes=[rs_])
        cx.op("dve", lambda e: e.scalar_tensor_tensor(out=nb_[:], in0=mv_[:, 0:1], scalar=-1.0, in1=rs_[:], op0=ALU.mult, op1=ALU.mult),
              reads=[mv_, rs_], writes=[nb_])
        cx.op("act", lambda a: a.activation(out=on_[:], in_=psO[:], func=AF.Identity, scale=rs_[:, 0:1], bias=nb_[:, 0:1]),
              reads=[psO, rs_, nb_], writes=[on_])

    def stE(i):
        on_ = on[i % 4]
        for ec in range(4):
            cx.op("pe", lambda p, ec=ec: p.transpose(out=psOT[:, ec * 128:(ec + 1) * 128], in_=on_[:, ec * 128:(ec + 1) * 128], identity=ident[:]),
                  reads=[on_, G["identb"]], writes=[psOT])

    def stF(i):
        blk, h = divmod(i, 8)
        sg, ost = sgB[blk % 2], ostB[blk % 2]
        cx.op("dve", lambda e: e.tensor_tensor(out=ost[:, 4 * h:4 * h + 4, :], in0=psOT[:, 0:512].rearrange("p (e t) -> p e t", e=4),
                                               in1=sg[:, 4 * h:4 * h + 4, :], op=ALU.mult), reads=[psOT, sg], writes=[ost])
        if h == 7:
            tsl = slice(blk * BT, (blk + 1) * BT)
            cx.dma("sp", G["orT"].rearrange("(c p) t -> p c t", p=128)[:, 0:16, tsl], ost[:, 0:16, :], ost, reads=[ost])
            cx.dma("sp", G["orT"].rearrange("(c p) t -> p c t", p=128)[:, 16:32, tsl], ost[:, 16:32, :], ost, reads=[ost])

    skew(8 * NBLK, [(stA, 0), (stB, 0), (stC, 1), (stD, 1), (stE, 3), (stF, 3)])
    return
    if first:
        for i in range(4):
            cx.dma("sp", G["exS_in"][i].rearrange("(h c p) e -> p h c e", p=128, c=2), Sf[:, 2 * i:2 * i + 2], Sf, reads=[Sf])


def retention_local_state(cx, G):
    BT = 256
    cst = G["cst"]
    ident = G["ident"]
    kb = [cx.sb([128, 4, BT], BF16, "lk") for _ in range(2)]
    vb = [cx.sb([128, 8, BT], BF16, "lv") for _ in range(2)]
    kdec = [cx.sb([128, 256], BF16, "lkd") for _ in range(3)]
    vtok = [cx.sb([128, 512], BF16, "lvt") for _ in range(3)]
    psT = [cx.ps([128, 1024], BF16, "lpT") for _ in range(3)]
    acc = [[cx.ps([128, 512], F32, "lacc") for _ in range(2)] for _ in range(2)]
    stg = cx.sb([128, 2, 2, 512], F32, "lstg")
    for hp in range(4):
        NI = (T // BT) * (BT // 128) * 2

        def dec(i):
            blk, rem = divmod(i, (BT // 128) * 2)
            ch, hh = divmod(rem, 2)
            return blk, ch, hh

        def stA(i):
            blk, ch, hh = dec(i)
            k, v = kb[blk % 2], vb[blk % 2]
            if ch == 0 and hh == 0:
                for lb in ([0, 1] if blk == 0 else [blk + 1]):
                    if lb < T // BT:
                        tsl = slice(lb * BT, (lb + 1) * BT)
                        k2, v2 = kb[lb % 2], vb[lb % 2]
                        cx.dma("sp", k2[:], G["krT"][hp * 512:(hp + 1) * 512, :].rearrange("(c p) t -> p c t", p=128)[:, :, tsl], k2, writes=[k2])
                        cx.dma("sp", v2[:], G["vrT"][hp * 1024:(hp + 1) * 1024, :].rearrange("(c p) t -> p c t", p=128)[:, :, tsl], v2, writes=[v2])
            cs = slice(ch * 128, (ch + 1) * 128)
            pT = psT[i % 3]
            for dc in range(2):
                cx.op("pe", lambda p, dc=dc: p.transpose(out=pT[:, dc * 128:(dc + 1) * 128], in_=k[:, 2 * hh + dc, cs], identity=ident[:]),
                      reads=[k, G["identb"]], writes=[pT])
            for ec in range(4):
                cx.op("pe", lambda p, ec=ec: p.transpose(out=pT[:, 256 + ec * 128:256 + (ec + 1) * 128], in_=v[:, 4 * hh + ec, cs], identity=ident[:]),
                      reads=[v, G["identb"]], writes=[pT])

        def stB(i):
            blk, ch, hh = dec(i)
            n = blk * (BT // 128) + ch
            h = 2 * hp + hh
            pT, kd_, vt_ = psT[i % 3], kdec[i % 3], vtok[i % 3]
            col = C_DKABS + n * 8 + h
            cx.op("dve", lambda e: e.tensor_scalar(out=kd_[:], in0=pT[:, 0:256], scalar1=cst[:, col:col + 1], scalar2=None, op0=ALU.mult),
                  reads=[pT, G["cstb"]], writes=[kd_])
            cx.op("act", lambda a: a.activation(out=vt_[:], in_=pT[:, 256:768], func=AF.Copy), reads=[pT], writes=[vt_])

        def stC(i):
            blk, ch, hh = dec(i)
            n = blk * (BT // 128) + ch
            kd_, vt_ = kdec[i % 3], vtok[i % 3]
            for dc in range(2):
                cx.op("pe", lambda p, dc=dc: p.matmul(acc[hh][dc][:], lhsT=kd_[:, dc * 128:(dc + 1) * 128], rhs=vt_[:], start=(n == 0), stop=(n == 31)),
                      reads=[kd_, vt_], writes=[acc[hh][dc]])

        skew(NI, [(stA, 0), (stB, 0), (stC, 1)])
        for hh in range(2):
            for dc in range(2):
                if dc == 0:
                    cx.op("act", lambda a: a.activation(out=stg[:, hh, dc, :], in_=acc[hh][dc][:], func=AF.Copy), reads=[acc[hh][dc]], writes=[stg])
                else:
                    cx.op("dve", lambda e: e.tensor_copy(out=stg[:, hh, dc, :], in_=acc[hh][dc][:]), reads=[acc[hh][dc]], writes=[stg])
        cx.dma("sp", G["exS_in"][hp].rearrange("(h c p) e -> p h c e", p=128, c=2), stg[:], stg, reads=[stg])


def state_init(cx, G, S, zero):
    Sf, Sb = S
    cst = G["cst"]
    if zero:
        cx.op("pool", lambda e: e.memset(Sf[:], 0.0), writes=[Sf])
        cx.op("pool", lambda e: e.memset(Sb[:], 0.0), writes=[Sb])
        return
    gt = [cx.sb([128, 8, 2, 512], F32, "gst") for _ in range(1)]
    g0 = gt[0]
    for r in range(3):
        for i in range(4):
            src = G["exS_out"][i][r * 512:(r + 1) * 512, :].rearrange("(h c p) e -> p h c e", p=128, c=2)
            cx.dma("sp", g0[:, 2 * i:2 * i + 2], src, g0, writes=[g0])
        for h in range(8):
            col = cst[:, C_COEF + r * 8 + h:C_COEF + r * 8 + h + 1]
            if r == 0:
                cx.op("dve", lambda e, h=h, col=col: e.tensor_scalar(out=Sf[:, h], in0=g0[:, h], scalar1=col, scalar2=None, op0=ALU.mult),
                      reads=[g0, G["cstb"]], writes=[Sf])
            else:
                cx.op("dve", lambda e, h=h, col=col: e.scalar_tensor_tensor(out=Sf[:, h], in0=g0[:, h], scalar=col, in1=Sf[:, h], op0=ALU.mult, op1=ALU.add),
                      reads=[g0, G["cstb"]], writes=[Sf])
    cx.op("pool", lambda e: e.tensor_copy(out=Sb[:], in_=Sf[:]), reads=[Sf], writes=[Sb])


ATT_GROUPS = ((128, 1), (512, 4), (2048, 16))
XOVR = None
USE_POW = False


def attention_phase(cx, G):
    cst = G["cst"]
    ident = G["ident"]
    HMAX = 2048
    qh = cx.sb([128, T], BF16, "aq")
    kh = cx.sb([128, HMAX + T], BF16, "ak")
    vh = cx.sb([128, HMAX + T], BF16, "av")
    qd = cx.sb([128, T], BF16, "aqd")
    kd = cx.sb([128, HMAX + T], BF16, "akd")
    vd = cx.sb([128, HMAX + T], BF16, "avd")
    hk = [cx.sb([128, HMAX], BF16, "hk") for _ in range(3)]
    hv = [cx.sb([128, HMAX], BF16, "hv") for _ in range(3)]
    NSL = 6
    s_sb = [cx.sb([128, 256], F32, "s") for _ in range(NSL)]
    p_sb = [cx.sb([128, 256], BF16, "p") for _ in range(NSL)]
    pT_sb = [cx.sb([128, 256], BF16, "pT") for _ in range(NSL)]
    NVT = 10
    vt = [cx.sb([128, 128], BF16, "vt") for _ in range(NVT)]
    ost = [cx.sb([128, 128], F32, "ao") for _ in range(NSL)]
    mlst = [cx.sb([128, 2], F32, "ml") for _ in range(NSL)]
    negm = [cx.sb([128, 1], F32, "negm") for _ in range(NSL)]
    bank = [cx.ps([128, 512], F32, "abk") for _ in range(NSL)]
    psV = [cx.ps([128, 1024], BF16, "apV") for _ in range(2)]
    scale = 128.0 ** -0.5
    it = 0
    vti = 0
    for g, (window, d) in enumerate(ATT_GROUPS):
        HALO = 128 * d
        L = T // d
        LK = L + 128
        nbl = L // 128
        for hh in range(4):
            H = 4 * g + hh
            rows = slice(H * 128, (H + 1) * 128)
            cx.dma("sp", qh[:], G["qaT"][rows, :], qh, writes=[qh])
            cx.dma("sp", kh[:, HALO:HALO + T], G["kaT"][rows, :], kh, writes=[kh])
            cx.dma("sp", vh[:, HALO:HALO + T], G["vaT"][rows, :], vh, writes=[vh])
            for r in range(3):
                if g == 2:
                    ksrc = G["exT2o"][hh // 2][r * 256 + (hh % 2) * 128:r * 256 + (hh % 2) * 128 + 128, :]
                    vsrc = G["exT2o"][2 + hh // 2][r * 256 + (hh % 2) * 128:r * 256 + (hh % 2) * 128 + 128, :]
                elif g == 1:
                    ksrc = G["exT1o"][r * 1024 + hh * 128:r * 1024 + hh * 128 + 128, :]
                    vsrc = G["exT1o"][r * 1024 + 512 + hh * 128:r * 1024 + 512 + hh * 128 + 128, :]
                else:
                    ksrc = G["exT0o"][r * 1024 + hh * 128:r * 1024 + hh * 128 + 128, :]
                    vsrc = G["exT0o"][r * 1024 + 512 + hh * 128:r * 1024 + 512 + hh * 128 + 128, :]
                cx.dma("sp", hk[r][:, 0:HALO], ksrc, hk[r], writes=[hk[r]])
                cx.dma("sp", hv[r][:, 0:HALO], vsrc, hv[r], writes=[hv[r]])
            for (dst, srcs) in ((kh, hk), (vh, hv)):
                cx.op("dve", lambda e, dst=dst, srcs=srcs: e.tensor_scalar(out=dst[:, 0:HALO], in0=srcs[0][:, 0:HALO], scalar1=cst[:, C_SEL:C_SEL + 1], scalar2=None, op0=ALU.mult),
                      reads=[srcs[0], G["cstb"]], writes=[dst])
                for r in (1, 2):
                    cx.op("dve", lambda e, dst=dst, srcs=srcs, r=r: e.scalar_tensor_tensor(out=dst[:, 0:HALO], in0=srcs[r][:, 0:HALO], scalar=cst[:, C_SEL + r:C_SEL + r + 1],
                                                                                          in1=dst[:, 0:HALO], op0=ALU.mult, op1=ALU.add),
                          reads=[srcs[r], G["cstb"]], writes=[dst])
            if d > 1:
                cx.op("pool", lambda e: e.tensor_copy(out=qd[:, 0:T].rearrange("p (r l) -> p r l", r=d), in_=qh[:, 0:T].rearrange("p (l r) -> p r l", r=d)),
                      reads=[qh], writes=[qd])
                cx.op("act", lambda e: e.activation(out=kd[:, 0:HALO + T].rearrange("p (r l) -> p r l", r=d), in_=kh[:, 0:HALO + T].rearrange("p (l r) -> p r l", r=d), func=AF.Copy),
                      reads=[kh], writes=[kd])
                cx.op("pool", lambda e: e.tensor_copy(out=vd[:, 0:HALO + T].rearrange("p (r l) -> p r l", r=d), in_=vh[:, 0:HALO + T].rearrange("p (l r) -> p r l", r=d)),
                      reads=[vh], writes=[vd])
                Q, K, V = qd, kd, vd
            else:
                Q, K, V = qh, kh, vh
            items = [(r, b) for r in range(d) for b in range(nbl)]
            vtile = {}

            def vtrans(r, bi, Vt=None):
                nonlocal vti
                V_ = Vt
                tl = vt[vti % NVT]
                pv = psV[vti % 2]
                vti += 1
                kb = r * LK
                cx.op("pe", lambda p: p.transpose(out=pv[:, 0:128], in_=V_[:, kb + bi * 128:kb + (bi + 1) * 128], identity=ident[:]),
                      reads=[V_, G["identb"]], writes=[pv])
                cx.op("act", lambda a: a.activation(out=tl[:], in_=pv[:, 0:128], func=AF.Copy), reads=[pv], writes=[tl])
                vtile[(r, bi)] = tl

            def stA(i, Q=Q, K=K, V=V):
                r, b = items[i]
                if b == 0:
                    vtrans(r, 0, V)
                vtrans(r, b + 1, V)
                bk_ = bank[i % NSL]
                qb, kb = r * L, r * LK
                cx.op("pe", lambda p: p.matmul(bk_[:, 0:256], lhsT=Q[:, qb + b * 128:qb + (b + 1) * 128], rhs=K[:, kb + b * 128:kb + b * 128 + 256], start=True, stop=True),
                      reads=[Q, K], writes=[bk_])

            def stB(i):
                r, b = items[i]
                sl_ = i % NSL
                bk_, s_, p_, ml_, nm_ = bank[sl_], s_sb[sl_], p_sb[sl_], mlst[sl_], negm[sl_]
                mcol = C_AMASK + (256 if b == 0 else 0)
                cx.op("dve", lambda e: e.scalar_tensor_tensor(out=s_[:], in0=bk_[:, 0:256], scalar=scale, in1=cst[:, mcol:mcol + 256], op0=ALU.mult, op1=ALU.add),
                      reads=[bk_, G["cstb"]], writes=[s_])
                cx.op("dve", lambda e: e.reduce_max(out=ml_[:, 0:1], in_=s_[:], axis=AX.X), reads=[s_], writes=[ml_])
                cx.op("dve", lambda e: e.tensor_scalar(out=nm_[:], in0=ml_[:, 0:1], scalar1=-1.0, scalar2=None, op0=ALU.mult), reads=[ml_], writes=[nm_])
                cx.op("act", lambda a: a.activation(out=p_[:], in_=s_[:], func=AF.Exp, bias=nm_[:, 0:1], accum_out=ml_[:, 1:2]),
                      reads=[s_, nm_], writes=[p_, ml_])

            def stC(i):
                sl_ = i % NSL
                bk_, p_, pT_ = bank[sl_], p_sb[sl_], pT_sb[sl_]
                pPTv = bk_[:, 256:384].bitcast(BF16)
                for half in range(2):
                    cx.op("pe", lambda p, half=half: p.transpose(out=pPTv[:, half * 128:(half + 1) * 128], in_=p_[:, half * 128:(half + 1) * 128], identity=ident[:]),
                          reads=[p_, G["identb"]], writes=[bk_])
                cx.op("dve", lambda e: e.tensor_copy(out=pT_[:], in_=pPTv[:, 0:256]), reads=[bk_], writes=[pT_])

            def stD(i, g=g, hh=hh, d=d):
                r, b = items[i]
                sl_ = i % NSL
                bk_, pT_, o_, ml_ = bank[sl_], pT_sb[sl_], ost[sl_], mlst[sl_]
                vprev, vcur = vtile[(r, b)], vtile[(r, b + 1)]
                cx.op("pe", lambda p: p.matmul(bk_[:, 384:512], lhsT=pT_[:, 0:128], rhs=vprev[:], start=True, stop=False), reads=[pT_, vprev], writes=[bk_])
                cx.op("pe", lambda p: p.matmul(bk_[:, 384:512], lhsT=pT_[:, 128:256], rhs=vcur[:], start=False, stop=True), reads=[pT_, vcur], writes=[bk_])
                cx.op("act", lambda a: a.activation(out=o_[:], in_=bk_[:, 384:512], func=AF.Copy), reads=[bk_], writes=[o_])
                orows = G["oacc"][g].rearrange("(l r) f -> r l f", r=d)[r, b * 128:(b + 1) * 128, hh * 128:(hh + 1) * 128]
                cx.dma("sp", orows, o_[:], o_, reads=[o_])
                mrows = G["mlacc"].rearrange("(l r) f -> r l f", r=d)[r, b * 128:(b + 1) * 128, (g * 4 + hh) * 2:(g * 4 + hh) * 2 + 2]
                cx.dma("sp", mrows, ml_[:], ml_, reads=[ml_])

            skew(len(items), [(stA, 0), (stB, 0), (stC, 2), (stD, 4)])


def attn_merge_phase(cx, G):
    ident = G["ident"]
    NT_ = T // 128
    o3 = [cx.sb([128, 3, 512], F32, "mo") for _ in range(2)]
    ml = [cx.sb([128, 3, 4, 2], F32, "mml") for _ in range(2)]
    M = cx.sb([128, 4], F32, "mM")
    df = cx.sb([128, 3, 4], F32, "mdf")
    w = cx.sb([128, 3, 4], F32, "mw")
    wl = cx.sb([128, 3, 4], F32, "mwl")
    den = cx.sb([128, 4], F32, "mden")
    rden = cx.sb([128, 4], F32, "mrden")
    wn = cx.sb([128, 3, 4], F32, "mwn")
    acc = cx.sb([128, 512], F32, "macc")
    ob = [cx.sb([128, 512], BF16, "mob") for _ in range(2)]
    stg = [cx.sb([128, 4, 512], BF16, "mstg") for _ in range(2)]
    psT = [cx.ps([128, 1024], BF16, "mpT") for _ in range(2)]
    for tt in range(NT_):
        i2 = tt % 2
        o_, ml_, ob_, pT = o3[i2], ml[i2], ob[i2], psT[i2]
        sg = stg[(tt // 4) % 2]
        rows = slice(tt * 128, (tt + 1) * 128)
        for g in range(3):
            cx.dma("sp", o_[:, g, :], G["oacc"][g][rows, :], o_, writes=[o_])
        cx.dma("sp", ml_[:], G["mlacc"][rows, :].rearrange("t (g h k) -> t g h k", g=3, h=4), ml_, writes=[ml_])
        cx.op("dve", lambda e: e.tensor_tensor(out=M[:], in0=ml_[:, 0, :, 0], in1=ml_[:, 1, :, 0], op=ALU.max), reads=[ml_], writes=[M])
        cx.op("dve", lambda e: e.tensor_tensor(out=M[:], in0=M[:], in1=ml_[:, 2, :, 0], op=ALU.max), reads=[ml_], writes=[M])
        for g in range(3):
            cx.op("dve", lambda e, g=g: e.tensor_tensor(out=df[:, g, :], in0=ml_[:, g, :, 0], in1=M[:], op=ALU.subtract), reads=[ml_, M], writes=[df])
        cx.op("act", lambda a: a.activation(out=w[:], in_=df[:], func=AF.Exp), reads=[df], writes=[w])
        cx.op("dve", lambda e: e.tensor_tensor(out=wl[:], in0=w[:], in1=ml_[:, :, :, 1], op=ALU.mult), reads=[w, ml_], writes=[wl])
        cx.op("dve", lambda e: e.tensor_tensor(out=den[:], in0=wl[:, 0, :], in1=wl[:, 1, :], op=ALU.add), reads=[wl], writes=[den])
        cx.op("dve", lambda e: e.tensor_tensor(out=den[:], in0=den[:], in1=wl[:, 2, :], op=ALU.add), reads=[wl], writes=[den])
        cx.op("dve", lambda e: e.reciprocal(out=rden[:], in_=den[:]), reads=[den], writes=[rden])
        for g in range(3):
            cx.op("dve", lambda e, g=g: e.tensor_tensor(out=wn[:, g, :], in0=w[:, g, :], in1=rden[:], op=ALU.mult), reads=[w, rden], writes=[wn])
        for hh in range(4):
            hs = slice(hh * 128, (hh + 1) * 128)
            cx.op("dve", lambda e, hh=hh, hs=hs: e.tensor_scalar(out=acc[:, hs], in0=o_[:, 0, hs], scalar1=wn[:, 0, hh:hh + 1], scalar2=None, op0=ALU.mult),
                  reads=[o_, wn], writes=[acc])
            cx.op("dve", lambda e, hh=hh, hs=hs: e.scalar_tensor_tensor(out=acc[:, hs], in0=o_[:, 1, hs], scalar=wn[:, 1, hh:hh + 1], in1=acc[:, hs], op0=ALU.mult, op1=ALU.add),
                  reads=[o_, wn], writes=[acc])
            cx.op("dve", lambda e, hh=hh, hs=hs: e.scalar_tensor_tensor(out=ob_[:, hs], in0=o_[:, 2, hs], scalar=wn[:, 2, hh:hh + 1], in1=acc[:, hs], op0=ALU.mult, op1=ALU.add),
                  reads=[o_, wn, acc], writes=[ob_])
        for hh in range(4):
            cx.op("pe", lambda p, hh=hh: p.transpose(out=pT[:, hh * 128:(hh + 1) * 128], in_=ob_[:, hh * 128:(hh + 1) * 128], identity=ident[:]),
                  reads=[ob_, G["identb"]], writes=[pT])
        q4 = tt % 4
        cx.op("act", lambda a: a.activation(out=sg[:, :, q4 * 128:(q4 + 1) * 128], in_=pT[:, 0:512].rearrange("p (h t) -> p h t", h=4), func=AF.Copy),
              reads=[pT], writes=[sg])
        if q4 == 3:
            t0 = (tt - 3) * 128
            cx.dma("sp", G["oaT"].rearrange("(c p) t -> p c t", p=128)[:, :, t0:t0 + 512], sg[:], sg, reads=[sg])


def build():
    nc = bass.Bass("TRN2", target_bir_lowering=False)
    cx = Cx(nc)
    G = {}

    def din(name, shape, dt=F32):
        return nc.dram_tensor(name, list(shape), dt, kind="ExternalInput").ap()

    DBGT = {}

    def dscr(name, shape, dt):
        a = nc.dram_tensor(name, list(shape), dt, kind="Internal").ap()
        if name in DBG:
            DBGT[name] = (a, nc.dram_tensor("dbg_" + name, [shape[0], 256], dt, kind="ExternalOutput").ap())
        return a

    def finish():
        db = Buf("dbgcopy")
        for name, (a, o) in DBGT.items():
            cx.dma("sp", o, a[:, 0:256], db)
        cx.barrier()
        return nc

    xT = din("xT", [D, T])
    WSHAPE = {"ffn1_w_gate": [D, DFF], "ffn1_w_up": [D, DFF], "ffn1_w_down": [DFF, D], "w_in": [D, INC],
              "w_proj_attn": [512, D], "w_proj_ret": [4096, D], "w_out": [D, D],
              "ffn2_w_gate": [D, DFF], "ffn2_w_up": [D, DFF], "ffn2_w_down": [DFF, D]}
    USED_W.clear()

    class _WL:
        def __init__(self, nm):
            self.nm = nm

        def __getitem__(self, l):
            key = "%s_%d" % (self.nm, l)
            if key not in USED_W:
                USED_W[key] = (self.nm, l, din(key, WSHAPE[self.nm]))
            return USED_W[key][2]

    W = {nm: _WL(nm) for nm in WSHAPE}
    cst_d = din("cst", [128, CW])
    G["ropeA"] = din("ropeA", [2, 128, T])
    G["ropeR"] = din("ropeR", [2, 128, T])
    yT = nc.dram_tensor("yT", [D, T], F32, kind="ExternalOutput").ap()

    XA = dscr("XA", [D, T], F32)
    XB = dscr("XB", [D, T], F32)
    uT = dscr("uT", [DFF, T], BF16)
    for nm, rows in (("qaT", 1536), ("kaT", 1536), ("vaT", 1536), ("qrT", 2048), ("krT", 2048), ("vrT", 4096), ("sgT", 4096),
                     ("gaT", 2048), ("gbT", 2048), ("oaT", 512), ("orT", 4096), ("mT", 2048)):
        G[nm] = dscr(nm, [rows, T], BF16)
    G["oacc"] = [dscr("oacc%d" % g, [T, 512], F32) for g in range(3)]
    G["mlacc"] = dscr("mlacc", [T, 24], F32)
    EX = []
    G["exS_in"], G["exS_out"] = [], []
    for i in range(4):
        a = nc.dram_tensor("exS_in%d" % i, [512, 512], F32)
        b = nc.dram_tensor("exS_out%d" % i, [4 * 512, 512], F32)
        EX.append((a, b))
        G["exS_in"].append(a.ap())
        G["exS_out"].append(b.ap())
    G["exT2"], G["exT2o"] = [], []
    for j in range(4):
        a = nc.dram_tensor("exT2_%d" % j, [256, 2048], BF16)
        b = nc.dram_tensor("exT2o_%d" % j, [4 * 256, 2048], BF16)
        EX.append((a, b))
        G["exT2"].append(a.ap())
        G["exT2o"].append(b.ap())
    a = nc.dram_tensor("exT1", [1024, 512], BF16)
    b = nc.dram_tensor("exT1o", [4 * 1024, 512], BF16)
    EX.append((a, b))
    G["exT1"], G["exT1o"] = a.ap(), b.ap()
    a = nc.dram_tensor("exT0", [1024, 128], BF16)
    b = nc.dram_tensor("exT0o", [4 * 1024, 128], BF16)
    EX.append((a, b))
    G["exT0"], G["exT0o"] = a.ap(), b.ap()
    cc_sems = [nc.alloc_semaphore("cc%d" % i) for i in range(len(EX) * DEPTH)]
    ccn = [0]

    cst = nc.alloc_sbuf_tensor("cst_sb", [128, CW], F32)
    identb = nc.alloc_sbuf_tensor("identb", [128, 128], BF16)
    ones = nc.alloc_sbuf_tensor("ones", [128, 128], BF16)
    eps = nc.alloc_sbuf_tensor("eps", [128, 2], F32)
    G["cst"], G["ident"], G["ones"], G["eps"] = cst, identb, ones, eps
    G["cstb"], G["identb"], G["onesb"] = Buf("cst"), Buf("ident"), Buf("ones")
    cx.dma("sp", cst[:], cst_d, G["cstb"], writes=[G["cstb"]])
    cx.op("dve", lambda e: e.tensor_copy(out=identb[:], in_=cst[:, C_ID:C_ID + 128]), reads=[G["cstb"]], writes=[G["identb"]])
    cx.op("pool", lambda e: e.memset(ones[:], 1.0), writes=[G["onesb"]])
    cx.op("pool", lambda e: e.memset(eps[:, 0:1], RMS_EPS), writes=[G["cstb"]])
    cx.op("pool", lambda e: e.memset(eps[:, 1:2], GN_EPS), writes=[G["cstb"]])
    cx.barrier()

    def ffn(l, which, xi, xo_, gidx):
        wg, wu, wd = W["ffn%d_w_gate" % which][l], W["ffn%d_w_up" % which][l], W["ffn%d_w_down" % which][l]
        groups = []
        for s in range(DFF // 256):
            groups.append(dict(outs=[(wg[:, s * 256:(s + 1) * 256], 0), (wu[:, s * 256:(s + 1) * 256], 0)],
                               units=[[(0, 0), (1, 0)], [(0, 128), (1, 128)]], kind="swiglu", dst=uT, row0=s * 256))
        with cx.phase("ffn_up"):
            linear_phase(cx, G, 2048, 256, groups, norm=(xi, gidx), need=("stgA", "tmp"))
        if STOP == "ffn_up":
            return True
        with cx.phase("ffn_down"):
            linear_phase(cx, G, 1024, 0, simple_groups(wd, D, "res", xo_, xin=xi, scale=0.5), xsrc=[uT], need=("res",))
        return False

    def exchange():
        nonlocal_ = ccn
        for (i_, o_) in EX:
            ins = nc.gpsimd.collective_compute("AllGather", ALU.bypass, replica_groups=[[0, 1, 2, 3], [4, 5, 6, 7]],
                                               ins=[i_.ap().opt()], outs=[o_.ap().opt()])
            sem = cc_sems[nonlocal_[0]]
            nonlocal_[0] += 1
            ins.then_inc(sem)
            nc.gpsimd.wait_ge(sem, 1)
            for e in ("pe", "act", "dve", "sp"):
                cx.eng[e].wait_ge(sem, 1)

    xcur, xnxt = xT, XA

    def done():
        return nc

    for l in range(DEPTH):
        if not ("ffn1" in SKIP and l == 0):
            if ffn(l, 1, xcur, xnxt, l * 3 + 0):
                return finish()
            xcur, xnxt = xnxt, (XB if xnxt is XA else XA)
        if STOP == "ffn1":
            return finish()
        wi = W["w_in"][l]
        groups = []
        for i in range(6):
            groups.append(dict(outs=[(wi[:, O_QA + i * 256:O_QA + (i + 1) * 256], 0)], units=[[(0, 0), (0, 128)]], kind="ropeA", dst=G["qaT"], row0=i * 256))
        for i in range(6):
            groups.append(dict(outs=[(wi[:, O_KA + i * 256:O_KA + (i + 1) * 256], 0)], units=[[(0, 0), (0, 128)]], kind="ropeA", dst=G["kaT"], row0=i * 256))
        for i in range(8):
            groups.append(dict(outs=[(wi[:, O_QR + i * 256:O_QR + (i + 1) * 256], 0)], units=[[(0, 0), (0, 128)]], kind="ropeR", dst=G["qrT"], row0=i * 256))
        for i in range(8):
            groups.append(dict(outs=[(wi[:, O_KR + i * 256:O_KR + (i + 1) * 256], 0)], units=[[(0, 0), (0, 128)]], kind="ropeR", dst=G["krT"], row0=i * 256))
        groups += simple_groups(wi, 1536, "copy", G["vaT"], col0=O_VA)
        groups += simple_groups(wi, 4096, "copy", G["vrT"], col0=O_VR)
        groups += simple_groups(wi, 4096, "silu", G["sgT"], col0=O_GR)
        groups += simple_groups(wi, 2048, "sigmoid", G["gaT"], col0=O_GA)
        groups += simple_groups(wi, 2048, "sigmoid", G["gbT"], col0=O_GB)
        if "w_in" not in SKIP:
            with cx.phase("w_in"):
                linear_phase(cx, G, 2048, 256, groups, norm=(xcur, l * 3 + 1), need=("stgA", "stgB", "tmp", "rope"))
        if STOP == "w_in":
            return finish()
        with cx.phase("ret1"):
            retention_local_state(cx, G)
            tb_ = Buf("tails")
            for j in range(2):
                cx.dma("sp", G["exT2"][j], G["kaT"][1024 + j * 256:1024 + (j + 1) * 256, T - 2048:T], tb_)
                cx.dma("sp", G["exT2"][2 + j], G["vaT"][1024 + j * 256:1024 + (j + 1) * 256, T - 2048:T], tb_)
            cx.dma("sp", G["exT1"][0:512, :], G["kaT"][512:1024, T - 512:T], tb_)
            cx.dma("sp", G["exT1"][512:1024, :], G["vaT"][512:1024, T - 512:T], tb_)
            cx.dma("sp", G["exT0"][0:512, :], G["kaT"][0:512, T - 128:T], tb_)
            cx.dma("sp", G["exT0"][512:1024, :], G["vaT"][0:512, T - 128:T], tb_)
        if STOP == "ret1":
            return finish()
        if "exch" not in SKIP:
            exchange()
        if STOP == "exch":
            return finish()
        with cx.phase("ret2"):
            Sf = cx.sb([128, 8, 2, 512], F32, "Sf")
            Sb = cx.sb([128, 8, 2, 512], BF16, "Sb")
            with ExitStack() as st2:
                old = cx.st
                cx.st = st2
                state_init(cx, G, (Sf, Sb), False)
                cx.barrier()
                cx.st = old
            retention_pass(cx, G, (Sf, Sb), False)
        if STOP == "ret":
            return finish()
        with cx.phase("attn"):
            attention_phase(cx, G)
        with cx.phase("merge"):
            attn_merge_phase(cx, G)
        if STOP == "attn":
            return finish()
        pa, pr = W["w_proj_attn"][l], W["w_proj_ret"][l]
        groups = []
        for s in range(D // 256):
            groups.append(dict(outs=[(pa[:, s * 256:(s + 1) * 256], 0), (pr[:, s * 256:(s + 1) * 256], 4)],
                               units=[[(0, 0), (1, 0)], [(0, 128), (1, 128)]], kind="gate", dst=G["mT"], row0=s * 256, ga=G["gaT"], gb=G["gbT"]))
        with cx.phase("proj"):
            linear_phase(cx, G, 1024, 0, groups, xsrc=[G["oaT"], G["orT"]], need=("stgA", "tmp", "gate"))
        if STOP == "proj":
            return finish()
        with cx.phase("w_out"):
            linear_phase(cx, G, 2048, 0, simple_groups(W["w_out"][l], D, "res", xnxt, xin=xcur, scale=1.0), xsrc=[G["mT"]], need=("res",))
        xcur, xnxt = xnxt, (XB if xnxt is XA else XA)
        if STOP == "w_out":
            return finish()
        if ffn(l, 2, xcur, xnxt, l * 3 + 2):
            return finish()
        xcur, xnxt = xnxt, (XB if xnxt is XA else XA)
        if STOP == "layer0":
            return finish()
    with cx.phase("final"):
        final_norm_phase(cx, G, xcur, 6, yT)
    print("kernel build: instructions ~", cx.ninst, "counts", cx.cnt)
    return nc


def _consts(qd):
    c = np.zeros((128, CW), np.float64)
    c[:, C_ID:C_ID + 128] = np.eye(128)
    i = np.arange(128)
    for h in range(8):
        gm = GAMMA[h]
        rel = i[None, :] - i[:, None]
        m = np.where(rel >= 0, gm ** np.maximum(rel, 0), 0.0) * (256.0 ** -0.5)
        c[:, C_RMASK + h * 128:C_RMASK + (h + 1) * 128] = m
        c[:, C_RDQ + h * 128:C_RDQ + (h + 1) * 128] = (gm ** (i + 1.0))[None, :]
        c[:, C_RDK + h] = gm ** (127.0 - i) * (256.0 ** -0.5)
        for n in range(32):
            dk = gm ** (T - 1.0 - (n * 128 + i)) * (256.0 ** -0.5)
            c[:, C_DKABS + n * 8 + h] = np.where(dk < 1e-30, 0.0, dk)
        for r in range(4):
            c[:, C_COEF + r * 8 + h] = (gm ** (4096.0 * (qd - r - 1))) if r < qd else 0.0
    for r in range(4):
        c[:, C_SEL + r] = 1.0 if r == qd - 1 else 0.0
    qi = i[:, None]
    kj = np.arange(256)[None, :]
    band = np.where((kj >= qi) & (kj <= qi + 128), 0.0, NEG)
    c[:, C_AMASK:C_AMASK + 256] = band
    first = band.copy()
    if qd == 0:
        first[:, 0:128] = NEG
    c[:, C_AMASK + 256:C_AMASK + 512] = first
    return c


def _rope(qd, dim, reps):
    inv = (1.0 / (10000.0 ** (np.arange(0, dim, 2, dtype=np.float32) / np.float32(dim)))).astype(np.float32)
    pos = (np.arange(T, dtype=np.float32) + np.float32(qd * T)).astype(np.float32)
    ang = (pos[None, :] * inv[:, None]).astype(np.float32)
    cs = np.stack([np.cos(ang), np.sin(ang)]).astype(np.float32)
    return np.ascontiguousarray(np.tile(cs, (1, reps, 1)))


def _perm_w_in(w_in):
    w = np.array(w_in, copy=True)
    for base in (0, 1536):
        blk = w_in[:, :, base:base + 1536].reshape(DEPTH, D, 6, 2, 2, 64)
        w[:, :, base:base + 1536] = blk.transpose(0, 1, 2, 4, 3, 5).reshape(DEPTH, D, 1536)
    return w


def kernel(x, ffn1_norm, ffn1_w_gate, ffn1_w_up, ffn1_w_down, mix_norm, w_in, w_proj_attn, w_proj_ret, w_out,
           ffn2_norm, ffn2_w_gate, ffn2_w_up, ffn2_w_down, final_norm):
    x = np.asarray(x, np.float32)
    gains = [ffn1_norm[0], mix_norm[0], ffn2_norm[0], ffn1_norm[1], mix_norm[1], ffn2_norm[1], final_norm]
    gcols = np.concatenate([np.asarray(g, np.float32).reshape(16, 128).T for g in gains], axis=1)
    wperm = _perm_w_in(np.asarray(w_in, np.float32))
    allw = {"ffn1_w_gate": ffn1_w_gate, "ffn1_w_up": ffn1_w_up, "ffn1_w_down": ffn1_w_down, "w_in": wperm,
            "w_proj_attn": w_proj_attn, "w_proj_ret": w_proj_ret, "w_out": w_out,
            "ffn2_w_gate": ffn2_w_gate, "ffn2_w_up": ffn2_w_up, "ffn2_w_down": ffn2_w_down}
    nc = build()
    shared = {key: np.ascontiguousarray(np.asarray(allw[nm][l], np.float32)) for key, (nm, l, _) in USED_W.items()}
    in_maps = []
    for c in range(NCORES):
        b, qd = c // 4, c % 4
        m = dict(shared)
        m["xT"] = np.ascontiguousarray(x[b, qd * T:(qd + 1) * T, :].T)
        if XOVR is not None and c == 0:
            m["xT"][:, :XOVR.shape[0]] = XOVR.T
        cc = _consts(qd)
        cc[:, C_GAIN:C_GAIN + 112] = gcols
        m["cst"] = cc.astype(np.float32)
        m["ropeA"] = _rope(qd, 128, 2)
        m["ropeR"] = _rope(qd, 256, 1)
        in_maps.append(m)
    res = run_bass_kernel_spmd(nc, in_maps, core_ids=list(range(NCORES)))
    kernel.last = res
    out = np.empty((2, 4 * T, D), np.float32)
    for c in range(NCORES):
        b, qd = c // 4, c % 4
        out[b, qd * T:(qd + 1) * T, :] = res.results[c]["yT"].T
    return out
```

```python
import os
from contextlib import ExitStack, contextmanager
import numpy as np
import concourse.bass as bass
import concourse.mybir as mybir
from concourse.bass_utils import run_bass_kernel_spmd

F32 = mybir.dt.float32
BF16 = mybir.dt.bfloat16
ALU = mybir.AluOpType
AF = mybir.ActivationFunctionType
AX = mybir.AxisListType

NCORES = 8
T = 4096
D = 2048
DFF = 5632
DEPTH = 2
INC = 20992
NEG = -1.0e30
RMS_EPS = 1e-6
GN_EPS = 1e-5
GAMMA = [1.0 - 2.0 ** (-5.0 - h) for h in range(8)]
O_QA, O_KA, O_VA, O_QR, O_KR, O_VR, O_GR, O_GA, O_GB = 0, 1536, 3072, 4608, 6656, 8704, 12800, 16896, 18944
C_ID = 0
C_RMASK = 128
C_RDQ = C_RMASK + 1024
C_RDK = C_RDQ + 1024
C_COEF = C_RDK + 8
C_SEL = C_COEF + 32
C_AMASK = C_SEL + 4
C_GAIN = C_AMASK + 512
C_DKABS = C_GAIN + 112
CW = C_DKABS + 256

STOP = os.environ.get("KSTOP", "")
SKIP = os.environ.get("KSKIP", "").split(",")
USED_W = {}
DBG = [s for s in os.environ.get("KDBG", "").split(",") if s]


class Buf:
    __slots__ = ("name", "w", "r", "ds")

    def __init__(self, name):
        self.name = name
        self.w = {}
        self.r = {}
        self.ds = None


def _merge(d, src):
    for k, (sem, v) in src.items():
        if k not in d or d[k][1] < v:
            d[k] = (sem, v)


class Tl:
    def __init__(self, h, name, psum=False):
        self.h = h
        self.b = Buf(name)
        self.psum = psum

    def __getitem__(self, k):
        return self.h[k]


class Cx:
    def __init__(self, nc):
        self.nc = nc
        self.eng = {"pe": nc.tensor, "act": nc.scalar, "dve": nc.vector, "pool": nc.gpsimd, "sp": nc.sync}
        self.psem = {}
        self.cnt = {}
        for e in ["pe", "act", "dve", "pool"]:
            self.psem[e] = nc.alloc_semaphore("s_" + e)
            self.cnt[e] = 0
        self.known = {e: {} for e in self.eng}
        self.dpool = [[nc.alloc_semaphore("d%d" % i), 0, "d%d" % i] for i in range(44)]
        self.dfree = list(range(44))
        self.dused = []
        self.uid = 0
        self.st = None
        self.ninst = 0

    def sb(self, shape, dtype, name="t"):
        self.uid += 1
        nm = "%s_%d" % (name, self.uid)
        h = self.st.enter_context(self.nc.sbuf_tensor(nm, list(shape), dtype))
        return Tl(h, nm)

    def ps(self, shape, dtype, name="p"):
        self.uid += 1
        nm = "%s_%d" % (name, self.uid)
        h = self.st.enter_context(self.nc.psum_tensor(nm, list(shape), dtype))
        return Tl(h, nm, psum=True)

    def _dsem(self, buf):
        if buf.ds is None:
            i = self.dfree.pop(0)
            buf.ds = self.dpool[i]
            self.dused.append((i, buf))
        return buf.ds

    def _wait(self, e, toks):
        for name, (sem, val) in toks.items():
            if e == "pe" and name == "s_pe":
                continue
            if self.known[e].get(name, 0) < val:
                self.eng[e].wait_ge(sem, val)
                self.known[e][name] = val
                self.ninst += 1

    def _deps(self, reads, writes):
        deps = {}
        for b in reads:
            _merge(deps, b.w)
        for b in writes:
            _merge(deps, b.w)
            _merge(deps, b.r)
        return deps

    def _mark(self, tokname, tok, reads, writes):
        t = {tokname: tok}
        for b in writes:
            _merge(b.w, t)
            b.r = {}
        for b in reads:
            _merge(b.r, t)

    def op(self, e, fn, reads=(), writes=()):
        writes = list(writes) + [x for x in reads if isinstance(x, Tl) and x.psum]
        reads = [x.b if isinstance(x, Tl) else x for x in reads if not (isinstance(x, Tl) and x.psum)]
        writes = [x.b if isinstance(x, Tl) else x for x in writes]
        self._wait(e, self._deps(reads, writes))
        ins = fn(self.eng[e])
        self.cnt[e] += 1
        ins.then_inc(self.psem[e], 1)
        self.ninst += 1
        self._mark("s_" + e, (self.psem[e], self.cnt[e]), reads, writes)

    def dma(self, e, out, in_, sbuf, reads=(), writes=()):
        reads = [x.b if isinstance(x, Tl) else x for x in reads]
        writes = [x.b if isinstance(x, Tl) else x for x in writes]
        sb = sbuf.b if isinstance(sbuf, Tl) else sbuf
        self._wait(e, self._deps(reads, writes))
        ds = self._dsem(sb)
        ins = self.eng[e].dma_start(out=out, in_=in_)
        ds[1] += 16
        ins.then_inc(ds[0], 16)
        self.ninst += 1
        self._mark(ds[2], (ds[0], ds[1]), reads, writes)

    def barrier(self):
        allt = {}
        for e in self.psem:
            if self.cnt[e] > 0:
                allt["s_" + e] = (self.psem[e], self.cnt[e])
        for s, c, n in self.dpool:
            if c > 0:
                allt[n] = (s, c)
        for e in self.eng:
            self._wait(e, allt)
        for i, b in self.dused:
            b.ds = None
            self.dfree.append(i)
        self.dused = []

    @contextmanager
    def phase(self, name):
        st = ExitStack()
        self.st = st
        try:
            yield st
        finally:
            self.barrier()
            st.close()


def linear_phase(cx, G, TB, NS, groups, xsrc=None, norm=None, need=()):
    nc = cx.nc
    KC = 16 if norm is not None else sum(a.shape[0] for a in xsrc) // 128
    NSUB = TB // 512
    X = cx.sb([128, KC, TB], BF16, "X")
    maxk = max(max(w.shape[0] // 128 for (w, _) in g["outs"]) for g in groups)
    nouts = max(len(g["outs"]) for g in groups)
    slabs = [[cx.sb([128, maxk, 256], BF16, "slab") for _ in range(nouts)] for _ in range(2)]
    NPB = 3
    banks = [[cx.ps([128, 512], F32, "bk") for _ in range(2)] for _ in range(NPB)]
    stgA = [cx.sb([128, TB], BF16, "stgA") for _ in range(2)] if "stgA" in need else None
    stgB = [cx.sb([128, TB], BF16, "stgB") for _ in range(2)] if "stgB" in need else None
    tmp = [cx.sb([128, 512], F32, "tmp") for _ in range(4)] if "tmp" in need else None
    xin = [cx.sb([128, TB], F32, "xin") for _ in range(2)] if "res" in need else None
    xo = [cx.sb([128, TB], F32, "xo") for _ in range(2)] if "res" in need else None
    gta = [cx.sb([128, TB], BF16, "gta") for _ in range(2)] if "gate" in need else None
    gtb = [cx.sb([128, TB], BF16, "gtb") for _ in range(2)] if "gate" in need else None
    rtab = cx.sb([128, 4, TB], F32, "rtab") if "rope" in need else None
    if norm is not None:
        xs32 = cx.sb([128, 16, NS], F32, "xs32")
        sq = cx.sb([128, 16, NS], BF16, "sq")
        psn = cx.ps([128, 512], F32, "psn")
        rs0 = cx.sb([128, NS], F32, "rs0")
        rstd = cx.sb([128, NS], F32, "rstd")
    rot = {"bank": 0, "stg": 0, "tmp": 0, "res": 0, "gate": 0}

    def load_slabs(gi):
        g = groups[gi]
        for oi, (w, kc0) in enumerate(g["outs"]):
            sl = slabs[gi % 2][oi]
            nk = w.shape[0] // 128
            cx.dma("pool", sl[:, 0:nk, :], w.rearrange("(c p) f -> p c f", p=128), sl, writes=[sl])

    for tb in range(T // TB):
        t0 = tb * TB
        if norm is not None:
            x_ap, gidx = norm
            for s in range(TB // NS):
                ts0 = t0 + s * NS
                cx.dma("sp", xs32[:], x_ap.rearrange("(c p) t -> p c t", p=128)[:, :, ts0:ts0 + NS], xs32, writes=[xs32])
                cx.op("act", lambda a: a.activation(out=sq[:], in_=xs32[:], func=AF.Square), reads=[xs32], writes=[sq])
                for c in range(16):
                    cx.op("pe", lambda p, c=c: p.matmul(psn[:, 0:NS], lhsT=G["ones"][:], rhs=sq[:, c, :], start=(c == 0), stop=(c == 15)),
                          reads=[sq, G["onesb"]], writes=[psn])
                cx.op("act", lambda a: a.activation(out=rs0[:], in_=psn[:, 0:NS], func=AF.Sqrt, scale=1.0 / D, bias=G["eps"][:, 0:1]),
                      reads=[psn, G["cstb"]], writes=[rs0])
                cx.op("dve", lambda v: v.reciprocal(out=rstd[:], in_=rs0[:]), reads=[rs0], writes=[rstd])
                for c in range(16):
                    cx.op("dve", lambda v, c=c: v.scalar_tensor_tensor(
                        out=X[:, c, s * NS:(s + 1) * NS], in0=xs32[:, c, :],
                        scalar=G["cst"][:, C_GAIN + gidx * 16 + c:C_GAIN + gidx * 16 + c + 1],
                        in1=rstd[:], op0=ALU.mult, op1=ALU.mult), reads=[xs32, rstd, G["cstb"]], writes=[X])
        else:
            kc = 0
            for a in xsrc:
                n = a.shape[0] // 128
                av = a.rearrange("(c p) t -> p c t", p=128)
                for c0 in range(0, n, 8):
                    c1 = min(n, c0 + 8)
                    cx.dma("sp", X[:, kc + c0:kc + c1, :], av[:, c0:c1, t0:t0 + TB], X, writes=[X])
                kc += n
        if rtab is not None:
            ra, rr = G["ropeA"], G["ropeR"]
            cx.dma("sp", rtab[:, 0:2, :], ra.rearrange("k p t -> p k t")[:, :, t0:t0 + TB], rtab, writes=[rtab])
            cx.dma("sp", rtab[:, 2:4, :], rr.rearrange("k p t -> p k t")[:, :, t0:t0 + TB], rtab, writes=[rtab])
        load_slabs(0)
        for gi, g in enumerate(groups):
            if gi + 1 < len(groups):
                load_slabs(gi + 1)
            kind = g["kind"]
            sl = slabs[gi % 2]
            for ui, unit in enumerate(g["units"]):
                if kind in ("copy", "silu", "sigmoid", "swiglu", "ropeA", "ropeR", "gate"):
                    sA = stgA[rot["stg"] % 2]
                    sB = stgB[rot["stg"] % 2] if stgB is not None else None
                    rot["stg"] += 1
                if kind == "res":
                    xi = xin[rot["res"] % 2]
                    xoo = xo[rot["res"] % 2]
                    rot["res"] += 1
                    r0 = g["row0"] + ui * 128
                    cx.dma("sp", xi[:], g["xin"][r0:r0 + 128, t0:t0 + TB], xi, writes=[xi])
                if kind == "gate":
                    ga_t = gta[rot["gate"] % 2]
                    gb_t = gtb[rot["gate"] % 2]
                    rot["gate"] += 1
                    r0 = g["row0"] + ui * 128
                    cx.dma("sp", ga_t[:], g["ga"][r0:r0 + 128, t0:t0 + TB], ga_t, writes=[ga_t])
                    cx.dma("sp", gb_t[:], g["gb"][r0:r0 + 128, t0:t0 + TB], gb_t, writes=[gb_t])
                for ts in range(NSUB):
                    bk = banks[rot["bank"] % NPB]
                    rot["bank"] += 1
                    cs = slice(ts * 512, (ts + 1) * 512)
                    for j, (oi, col) in enumerate(unit):
                        w, kc0 = g["outs"][oi]
                        nk = w.shape[0] // 128
                        for kc in range(nk):
                            cx.op("pe", lambda p, j=j, oi=oi, col=col, kc=kc, kc0=kc0, nk=nk: p.matmul(
                                bk[j][:], lhsT=sl[oi][:, kc, col:col + 128], rhs=X[:, kc0 + kc, cs],
                                start=(kc == 0), stop=(kc == nk - 1)), reads=[sl[oi], X], writes=[bk[j]])
                    if kind in ("copy", "silu", "sigmoid"):
                        fn = {"copy": AF.Copy, "silu": AF.Silu, "sigmoid": AF.Sigmoid}[kind]
                        cx.op("act", lambda a, fn=fn: a.activation(out=sA[:, cs], in_=bk[0][:], func=fn), reads=[bk[0]], writes=[sA])
                    elif kind == "swiglu":
                        tm = tmp[rot["tmp"] % 4]
                        rot["tmp"] += 1
                        cx.op("act", lambda a: a.activation(out=tm[:], in_=bk[0][:], func=AF.Silu), reads=[bk[0]], writes=[tm])
                        cx.op("dve", lambda v: v.tensor_tensor(out=sA[:, cs], in0=bk[1][:], in1=tm[:], op=ALU.mult), reads=[bk[1], tm], writes=[sA])
                    elif kind in ("ropeA", "ropeR"):
                        k0 = 0 if kind == "ropeA" else 2
                        cc = rtab[:, k0, cs]
                        ss = rtab[:, k0 + 1, cs]
                        t1, t2, t3, t4 = [tmp[(rot["tmp"] + i) % 4] for i in range(4)]
                        rot["tmp"] += 4
                        A, B = bk[0], bk[1]
                        cx.op("dve", lambda v: v.tensor_tensor(out=t1[:], in0=A[:], in1=cc, op=ALU.mult), reads=[A, rtab], writes=[t1])
                        cx.op("dve", lambda v: v.tensor_tensor(out=t2[:], in0=B[:], in1=ss, op=ALU.mult), reads=[B, rtab], writes=[t2])
                        cx.op("pool", lambda v: v.tensor_tensor(out=sA[:, cs], in0=t1[:], in1=t2[:], op=ALU.subtract), reads=[t1, t2], writes=[sA])
                        cx.op("dve", lambda v: v.tensor_tensor(out=t3[:], in0=B[:], in1=cc, op=ALU.mult), reads=[B, rtab], writes=[t3])
                        cx.op("dve", lambda v: v.tensor_tensor(out=t4[:], in0=A[:], in1=ss, op=ALU.mult), reads=[A, rtab], writes=[t4])
                        cx.op("pool", lambda v: v.tensor_tensor(out=sB[:, cs], in0=t3[:], in1=t4[:], op=ALU.add), reads=[t3, t4], writes=[sB])
                    elif kind == "res":
                        sc = g["scale"]
                        cx.op("dve", lambda v: v.scalar_tensor_tensor(out=xoo[:, cs], in0=bk[0][:], scalar=sc, in1=xi[:, cs],
                                                                      op0=ALU.mult, op1=ALU.add), reads=[bk[0], xi], writes=[xoo])
                    elif kind == "gate":
                        t1, t2 = [tmp[(rot["tmp"] + i) % 4] for i in range(2)]
                        rot["tmp"] += 2
                        cx.op("dve", lambda v: v.tensor_tensor(out=t1[:], in0=bk[0][:], in1=ga_t[:, cs], op=ALU.mult), reads=[bk[0], ga_t], writes=[t1])
                        cx.op("dve", lambda v: v.tensor_tensor(out=t2[:], in0=bk[1][:], in1=gb_t[:, cs], op=ALU.mult), reads=[bk[1], gb_t], writes=[t2])
                        cx.op("pool", lambda v: v.tensor_tensor(out=sA[:, cs], in0=t1[:], in1=t2[:], op=ALU.add), reads=[t1, t2], writes=[sA])
                tsl = slice(t0, t0 + TB)
                if kind in ("copy", "silu", "sigmoid", "swiglu", "gate"):
                    r0 = g["row0"] + ui * 128
                    cx.dma("sp", g["dst"][r0:r0 + 128, tsl], sA[:], sA, reads=[sA])
                elif kind == "ropeR":
                    r0 = g["row0"]
                    cx.dma("sp", g["dst"][r0:r0 + 128, tsl], sA[:], sA, reads=[sA])
                    cx.dma("sp", g["dst"][r0 + 128:r0 + 256, tsl], sB[:], sB, reads=[sB])
                elif kind == "ropeA":
                    r0 = g["row0"]
                    cx.dma("sp", g["dst"][r0:r0 + 64, tsl], sA[0:64, :], sA, reads=[sA])
                    cx.dma("sp", g["dst"][r0 + 128:r0 + 192, tsl], sA[64:128, :], sA, reads=[sA])
                    cx.dma("sp", g["dst"][r0 + 64:r0 + 128, tsl], sB[0:64, :], sB, reads=[sB])
                    cx.dma("sp", g["dst"][r0 + 192:r0 + 256, tsl], sB[64:128, :], sB, reads=[sB])
                elif kind == "res":
                    r0 = g["row0"] + ui * 128
                    cx.dma("sp", g["dst"][r0:r0 + 128, tsl], xoo[:], xoo, reads=[xoo])


def skew(n, stages):
    md = max(d for _, d in stages)
    for t in range(n + md):
        for fn, d in stages:
            i = t - d
            if 0 <= i < n:
                fn(i)


def simple_groups(w, ncols, kind, dst, col0=0, row0=0, **kw):
    gs = []
    for s in range(ncols // 256):
        g = dict(outs=[(w[:, col0 + s * 256:col0 + (s + 1) * 256], 0)], units=[[(0, 0)], [(0, 128)]], kind=kind, dst=dst,
                 row0=row0 + s * 256)
        g.update(kw)
        gs.append(g)
    return gs


def final_norm_phase(cx, G, x_ap, gidx, y_ap):
    NS = 256
    xs32 = [cx.sb([128, 16, NS], F32, "xs32") for _ in range(2)]
    sq = cx.sb([128, 16, NS], BF16, "sq")
    psn = cx.ps([128, 512], F32, "psn")
    rs0 = cx.sb([128, NS], F32, "rs0")
    rstd = cx.sb([128, NS], F32, "rstd")
    yo = [cx.sb([128, 16, NS], F32, "yo") for _ in range(2)]
    for s in range(T // NS):
        ts0 = s * NS
        x3 = xs32[s % 2]
        y3 = yo[s % 2]
        cx.dma("sp", x3[:], x_ap.rearrange("(c p) t -> p c t", p=128)[:, :, ts0:ts0 + NS], x3, writes=[x3])
        cx.op("act", lambda a: a.activation(out=sq[:], in_=x3[:], func=AF.Square), reads=[x3], writes=[sq])
        for c in range(16):
            cx.op("pe", lambda p, c=c: p.matmul(psn[:, 0:NS], lhsT=G["ones"][:], rhs=sq[:, c, :], start=(c == 0), stop=(c == 15)),
                  reads=[sq, G["onesb"]], writes=[psn])
        cx.op("act", lambda a: a.activation(out=rs0[:], in_=psn[:, 0:NS], func=AF.Sqrt, scale=1.0 / D, bias=G["eps"][:, 0:1]),
              reads=[psn, G["cstb"]], writes=[rs0])
        cx.op("dve", lambda v: v.reciprocal(out=rstd[:], in_=rs0[:]), reads=[rs0], writes=[rstd])
        for c in range(16):
            cx.op("dve", lambda v, c=c: v.scalar_tensor_tensor(
                out=y3[:, c, :], in0=x3[:, c, :], scalar=G["cst"][:, C_GAIN + gidx * 16 + c:C_GAIN + gidx * 16 + c + 1],
                in1=rstd[:], op0=ALU.mult, op1=ALU.mult), reads=[x3, rstd, G["cstb"]], writes=[y3])
        cx.dma("sp", y_ap.rearrange("(c p) t -> p c t", p=128)[:, :, ts0:ts0 + NS], y3[:], y3, reads=[y3])


def retention_pass(cx, G, S, first):
    nc = cx.nc
    BT = 128
    cst = G["cst"]
    qB = [cx.sb([128, 16, BT], BF16, "rq") for _ in range(2)]
    kB = [cx.sb([128, 16, BT], BF16, "rk") for _ in range(2)]
    vB = [cx.sb([128, 32, BT], BF16, "rv") for _ in range(2)]
    sgB = [cx.sb([128, 32, BT], BF16, "rsg") for _ in range(2)]
    ostB = [cx.sb([128, 32, BT], BF16, "rost") for _ in range(2)]
    Sf, Sb = S
    psA = cx.ps([128, 512], F32, "psA")
    psO2 = [cx.ps([128, 512], F32, "psO") for _ in range(2)]
    psOT = cx.ps([128, 1024], BF16, "psOT")
    psT = [cx.ps([128, 1024], BF16, "psT") for _ in range(2)]
    psS = [cx.ps([128, 512], F32, "psS") for _ in range(2)]
    ident = G["ident"]
    assert not first
    atm = [cx.sb([128, 128], BF16, "atm") for _ in range(3)]
    qdec = [cx.sb([128, 2, 128], BF16, "qdec") for _ in range(3)]
    on = [cx.sb([128, 512], BF16, "on") for _ in range(4)]
    st6 = [cx.sb([128, 6], F32, "st6") for _ in range(4)]
    mv = [cx.sb([128, 2], F32, "mv") for _ in range(4)]
    sd = [cx.sb([128, 1], F32, "sd") for _ in range(4)]
    rs = [cx.sb([128, 1], F32, "rs") for _ in range(4)]
    nb = [cx.sb([128, 1], F32, "nb") for _ in range(4)]
    kdec = [cx.sb([128, 256], BF16, "kdec") for _ in range(3)]
    vtok = [cx.sb([128, 512], BF16, "vtok") for _ in range(3)]
    NBLK = T // BT

    def load_block(lb):
        tsl = slice(lb * BT, (lb + 1) * BT)
        q, k, v, sg = qB[lb % 2], kB[lb % 2], vB[lb % 2], sgB[lb % 2]
        cx.dma("sp", k[:], G["krT"].rearrange("(c p) t -> p c t", p=128)[:, :, tsl], k, writes=[k])
        cx.dma("sp", q[:], G["qrT"].rearrange("(c p) t -> p c t", p=128)[:, :, tsl], q, writes=[q])
        cx.dma("sp", v[:, 0:16, :], G["vrT"].rearrange("(c p) t -> p c t", p=128)[:, 0:16, tsl], v, writes=[v])
        cx.dma("sp", v[:, 16:32, :], G["vrT"].rearrange("(c p) t -> p c t", p=128)[:, 16:32, tsl], v, writes=[v])
        cx.dma("sp", sg[:, 0:16, :], G["sgT"].rearrange("(c p) t -> p c t", p=128)[:, 0:16, tsl], sg, writes=[sg])
        cx.dma("sp", sg[:, 16:32, :], G["sgT"].rearrange("(c p) t -> p c t", p=128)[:, 16:32, tsl], sg, writes=[sg])

    def stA(i):
        blk, h = divmod(i, 8)
        if i == 0:
            load_block(0)
        if h == 4 and blk + 1 < NBLK:
            load_block(blk + 1)
        q, k, v = qB[blk % 2], kB[blk % 2], vB[blk % 2]
        pT = psT[i % 2]
        for dc in range(2):
            cx.op("pe", lambda p, dc=dc: p.transpose(out=pT[:, dc * 128:(dc + 1) * 128], in_=k[:, 2 * h + dc, :], identity=ident[:]),
                  reads=[k, G["identb"]], writes=[pT])
        for ec in range(4):
            cx.op("pe", lambda p, ec=ec: p.transpose(out=pT[:, 256 + ec * 128:256 + (ec + 1) * 128], in_=v[:, 4 * h + ec, :], identity=ident[:]),
                  reads=[v, G["identb"]], writes=[pT])
        for dc in range(2):
            cx.op("pe", lambda p, dc=dc: p.matmul(psA[:, 0:128], lhsT=k[:, 2 * h + dc, :], rhs=q[:, 2 * h + dc, :], start=(dc == 0), stop=(dc == 1)),
                  reads=[k, q], writes=[psA])

    def stB(i):
        blk, h = divmod(i, 8)
        q = qB[blk % 2]
        pT, kd_, vt_, at_, qd_ = psT[i % 2], kdec[i % 3], vtok[i % 3], atm[i % 3], qdec[i % 3]
        cx.op("act", lambda a: a.activation(out=kd_[:], in_=pT[:, 0:256], func=AF.Copy, scale=cst[:, C_RDK + h:C_RDK + h + 1]),
              reads=[pT, G["cstb"]], writes=[kd_])
        cx.op("act", lambda a: a.activation(out=vt_[:], in_=pT[:, 256:768], func=AF.Copy), reads=[pT], writes=[vt_])
        cx.op("dve", lambda e: e.tensor_tensor(out=at_[:], in0=psA[:, 0:128], in1=cst[:, C_RMASK + h * 128:C_RMASK + (h + 1) * 128], op=ALU.mult),
              reads=[psA, G["cstb"]], writes=[at_])
        for dc in range(2):
            cx.op("pool", lambda e, dc=dc: e.tensor_tensor(out=qd_[:, dc, :], in0=q[:, 2 * h + dc, :], in1=cst[:, C_RDQ + h * 128:C_RDQ + (h + 1) * 128], op=ALU.mult),
                  reads=[q, G["cstb"]], writes=[qd_])

    def stC(i):
        blk, h = divmod(i, 8)
        kd_, vt_, at_, qd_ = kdec[i % 3], vtok[i % 3], atm[i % 3], qdec[i % 3]
        psO = psO2[i % 2]
        for dc in range(2):
            cx.op("pe", lambda p, dc=dc: p.matmul(psO[:], lhsT=qd_[:, dc, :], rhs=Sb[:, h, dc, :], start=(dc == 0), stop=False),
                  reads=[qd_, Sb], writes=[psO])
        cx.op("pe", lambda p: p.matmul(psO[:], lhsT=at_[:], rhs=vt_[:], start=False, stop=True), reads=[at_, vt_], writes=[psO])
        for dc in range(2):
            cx.op("pe", lambda p, dc=dc: p.matmul(psS[dc][:], lhsT=kd_[:, dc * 128:(dc + 1) * 128], rhs=vt_[:], start=True, stop=True),
                  reads=[kd_, vt_], writes=[psS[dc]])

    def stD(i):
        blk, h = divmod(i, 8)
        psO = psO2[i % 2]
        on_ = on[i % 4]
        cd = GAMMA[h] ** 128
        for dc in range(2):
            cx.op("dve", lambda e, dc=dc: e.scalar_tensor_tensor(out=Sf[:, h, dc, :], in0=Sf[:, h, dc, :], scalar=cd, in1=psS[dc][:], op0=ALU.mult, op1=ALU.add),
                  reads=[psS[dc]], writes=[Sf])
        for dc in range(2):
            cx.op("act", lambda a, dc=dc: a.activation(out=Sb[:, h, dc, :], in_=Sf[:, h, dc, :], func=AF.Copy), reads=[Sf], writes=[Sb])
        s6, mv_, sd_, rs_, nb_ = st6[i % 4], mv[i % 4], sd[i % 4], rs[i % 4], nb[i % 4]
        cx.op("dve", lambda e: e.bn_stats(out=s6[:], in_=psO[:]), reads=[psO], writes=[s6])
        cx.op("dve", lambda e: e.bn_aggr(out=mv_[:], in_=s6[:]), reads=[s6], writes=[mv_])
        cx.op("act", lambda a: a.activation(out=sd_[:], in_=mv_[:, 1:2], func=AF.Sqrt, bias=G["eps"][:, 1:2]), reads=[mv_, G["cstb"]], writes=[sd_])
        cx.op("dve", lambda e: e.reciprocal(out=rs_[:], in_=sd_[:]), reads=[sd_], writes=[rs_])
        cx.op("dve", lambda e: e.scalar_tensor_tensor(out=nb_[:], in0=mv_[:, 0:1], scalar=-1.0, in1=rs_[:], op0=ALU.mult, op1=ALU.mult),
              reads=[mv_, rs_], writes=[nb_])
        cx.op("act", lambda a: a.activation(out=on_[:], in_=psO[:], func=AF.Identity, scale=rs_[:, 0:1], bias=nb_[:, 0:1]),
              reads=[psO, rs_, nb_], writes=[on_])

    def stE(i):
        on_ = on[i % 4]
        for ec in range(4):
            cx.op("pe", lambda p, ec=ec: p.transpose(out=psOT[:, ec * 128:(ec + 1) * 128], in_=on_[:, ec * 128:(ec + 1) * 128], identity=ident[:]),
                  reads=[on_, G["identb"]], writes=[psOT])

    def stF(i):
        blk, h = divmod(i, 8)
        sg, ost = sgB[blk % 2], ostB[blk % 2]
        cx.op("dve", lambda e: e.tensor_tensor(out=ost[:, 4 * h:4 * h + 4, :], in0=psOT[:, 0:512].rearrange("p (e t) -> p e t", e=4),
                                               in1=sg[:, 4 * h:4 * h + 4, :], op=ALU.mult), reads=[psOT, sg], writes=[ost])
        if h == 7:
            tsl = slice(blk * BT, (blk + 1) * BT)
            cx.dma("sp", G["orT"].rearrange("(c p) t -> p c t", p=128)[:, 0:16, tsl], ost[:, 0:16, :], ost, reads=[ost])
            cx.dma("sp", G["orT"].rearrange("(c p) t -> p c t", p=128)[:, 16:32, tsl], ost[:, 16:32, :], ost, reads=[ost])

    skew(8 * NBLK, [(stA, 0), (stB, 0), (stC, 1), (stD, 1), (stE, 3), (stF, 3)])
    return
    if first:
        for i in range(4):
            cx.dma("sp", G["exS_in"][i].rearrange("(h c p) e -> p h c e", p=128, c=2), Sf[:, 2 * i:2 * i + 2], Sf, reads=[Sf])


def retention_local_state(cx, G):
    BT = 256
    cst = G["cst"]
    ident = G["ident"]
    kb = [cx.sb([128, 4, BT], BF16, "lk") for _ in range(3)]
    vb = [cx.sb([128, 8, BT], BF16, "lv") for _ in range(3)]
    kdec = [cx.sb([128, 256], BF16, "lkd") for _ in range(3)]
    vtok = [cx.sb([128, 512], BF16, "lvt") for _ in range(3)]
    psT = [cx.ps([128, 1024], BF16, "lpT") for _ in range(3)]
    acc = [[cx.ps([128, 512], F32, "lacc") for _ in range(2)] for _ in range(2)]
    stg = cx.sb([128, 2, 2, 512], F32, "lstg")
    for hp in range(4):
        NI = (T // BT) * (BT // 128) * 2

        def dec(i):
            blk, rem = divmod(i, (BT // 128) * 2)
            ch, hh = divmod(rem, 2)
            return blk, ch, hh

        def stA(i):
            blk, ch, hh = dec(i)
            k, v = kb[blk % 3], vb[blk % 3]
            if ch == 0 and hh == 0:
                for lb in ([0, 1, 2] if blk == 0 else [blk + 2]):
                    if lb < T // BT:
                        tsl = slice(lb * BT, (lb + 1) * BT)
                        k2, v2 = kb[lb % 3], vb[lb % 3]
                        cx.dma("sp", k2[:], G["krT"][hp * 512:(hp + 1) * 512, :].rearrange("(c p) t -> p c t", p=128)[:, :, tsl], k2, writes=[k2])
                        cx.dma("sp", v2[:], G["vrT"][hp * 1024:(hp + 1) * 1024, :].rearrange("(c p) t -> p c t", p=128)[:, :, tsl], v2, writes=[v2])
            cs = slice(ch * 128, (ch + 1) * 128)
            pT = psT[i % 3]
            for dc in range(2):
                cx.op("pe", lambda p, dc=dc: p.transpose(out=pT[:, dc * 128:(dc + 1) * 128], in_=k[:, 2 * hh + dc, cs], identity=ident[:]),
                      reads=[k, G["identb"]], writes=[pT])
            for ec in range(4):
                cx.op("pe", lambda p, ec=ec: p.transpose(out=pT[:, 256 + ec * 128:256 + (ec + 1) * 128], in_=v[:, 4 * hh + ec, cs], identity=ident[:]),
                      reads=[v, G["identb"]], writes=[pT])

        def stB(i):
            blk, ch, hh = dec(i)
            n = blk * (BT // 128) + ch
            h = 2 * hp + hh
            pT, kd_, vt_ = psT[i % 3], kdec[i % 3], vtok[i % 3]
            col = C_DKABS + n * 8 + h
            cx.op("dve", lambda e: e.tensor_scalar(out=kd_[:], in0=pT[:, 0:256], scalar1=cst[:, col:col + 1], scalar2=None, op0=ALU.mult),
                  reads=[pT, G["cstb"]], writes=[kd_])
            cx.op("act", lambda a: a.activation(out=vt_[:], in_=pT[:, 256:768], func=AF.Copy), reads=[pT], writes=[vt_])

        def stC(i):
            blk, ch, hh = dec(i)
            n = blk * (BT // 128) + ch
            kd_, vt_ = kdec[i % 3], vtok[i % 3]
            for dc in range(2):
                cx.op("pe", lambda p, dc=dc: p.matmul(acc[hh][dc][:], lhsT=kd_[:, dc * 128:(dc + 1) * 128], rhs=vt_[:], start=(n == 0), stop=(n == 31)),
                      reads=[kd_, vt_], writes=[acc[hh][dc]])

        skew(NI, [(stA, 0), (stB, 0), (stC, 1)])
        for hh in range(2):
            for dc in range(2):
                if dc == 0:
                    cx.op("act", lambda a: a.activation(out=stg[:, hh, dc, :], in_=acc[hh][dc][:], func=AF.Copy), reads=[acc[hh][dc]], writes=[stg])
                else:
                    cx.op("dve", lambda e: e.tensor_copy(out=stg[:, hh, dc, :], in_=acc[hh][dc][:]), reads=[acc[hh][dc]], writes=[stg])
        cx.dma("sp", G["exS_in"][hp].rearrange("(h c p) e -> p h c e", p=128, c=2), stg[:], stg, reads=[stg])


def state_init(cx, G, S, zero):
    Sf, Sb = S
    cst = G["cst"]
    if zero:
        cx.op("pool", lambda e: e.memset(Sf[:], 0.0), writes=[Sf])
        cx.op("pool", lambda e: e.memset(Sb[:], 0.0), writes=[Sb])
        return
    gt = [cx.sb([128, 8, 2, 512], F32, "gst") for _ in range(1)]
    g0 = gt[0]
    for r in range(3):
        for i in range(4):
            src = G["exS_out"][i][r * 512:(r + 1) * 512, :].rearrange("(h c p) e -> p h c e", p=128, c=2)
            cx.dma("sp", g0[:, 2 * i:2 * i + 2], src, g0, writes=[g0])
        for h in range(8):
            col = cst[:, C_COEF + r * 8 + h:C_COEF + r * 8 + h + 1]
            if r == 0:
                cx.op("dve", lambda e, h=h, col=col: e.tensor_scalar(out=Sf[:, h], in0=g0[:, h], scalar1=col, scalar2=None, op0=ALU.mult),
                      reads=[g0, G["cstb"]], writes=[Sf])
            else:
                cx.op("dve", lambda e, h=h, col=col: e.scalar_tensor_tensor(out=Sf[:, h], in0=g0[:, h], scalar=col, in1=Sf[:, h], op0=ALU.mult, op1=ALU.add),
                      reads=[g0, G["cstb"]], writes=[Sf])
    cx.op("pool", lambda e: e.tensor_copy(out=Sb[:], in_=Sf[:]), reads=[Sf], writes=[Sb])


ATT_GROUPS = ((128, 1), (512, 4), (2048, 16))
XOVR = None
USE_POW = False


def attention_phase(cx, G):
    cst = G["cst"]
    ident = G["ident"]
    HMAX = 2048
    qh = cx.sb([128, T], BF16, "aq")
    kh = cx.sb([128, HMAX + T], BF16, "ak")
    vh = cx.sb([128, HMAX + T], BF16, "av")
    qd = cx.sb([128, T], BF16, "aqd")
    kd = cx.sb([128, HMAX + T], BF16, "akd")
    vd = cx.sb([128, HMAX + T], BF16, "avd")
    hk = [cx.sb([128, HMAX], BF16, "hk") for _ in range(3)]
    hv = [cx.sb([128, HMAX], BF16, "hv") for _ in range(3)]
    NSL = 6
    s_sb = [cx.sb([128, 256], F32, "s") for _ in range(NSL)]
    p_sb = [cx.sb([128, 256], BF16, "p") for _ in range(NSL)]
    pT_sb = [cx.sb([128, 256], BF16, "pT") for _ in range(NSL)]
    NVT = 10
    vt = [cx.sb([128, 128], BF16, "vt") for _ in range(NVT)]
    ost = [cx.sb([128, 128], F32, "ao") for _ in range(NSL)]
    mlst = [cx.sb([128, 2], F32, "ml") for _ in range(NSL)]
    negm = [cx.sb([128, 1], F32, "negm") for _ in range(NSL)]
    bank = [cx.ps([128, 512], F32, "abk") for _ in range(NSL)]
    psV = [cx.ps([128, 1024], BF16, "apV") for _ in range(2)]
    scale = 128.0 ** -0.5
    it = 0
    vti = 0
    for g, (window, d) in enumerate(ATT_GROUPS):
        HALO = 128 * d
        L = T // d
        LK = L + 128
        nbl = L // 128
        for hh in range(4):
            H = 4 * g + hh
            rows = slice(H * 128, (H + 1) * 128)
            cx.dma("sp", qh[:], G["qaT"][rows, :], qh, writes=[qh])
            cx.dma("sp", kh[:, HALO:HALO + T], G["kaT"][rows, :], kh, writes=[kh])
            cx.dma("sp", vh[:, HALO:HALO + T], G["vaT"][rows, :], vh, writes=[vh])
            for r in range(3):
                if g == 2:
                    ksrc = G["exT2o"][hh // 2][r * 256 + (hh % 2) * 128:r * 256 + (hh % 2) * 128 + 128, :]
                    vsrc = G["exT2o"][2 + hh // 2][r * 256 + (hh % 2) * 128:r * 256 + (hh % 2) * 128 + 128, :]
                elif g == 1:
                    ksrc = G["exT1o"][r * 1024 + hh * 128:r * 1024 + hh * 128 + 128, :]
                    vsrc = G["exT1o"][r * 1024 + 512 + hh * 128:r * 1024 + 512 + hh * 128 + 128, :]
                else:
                    ksrc = G["exT0o"][r * 1024 + hh * 128:r * 1024 + hh * 128 + 128, :]
                    vsrc = G["exT0o"][r * 1024 + 512 + hh * 128:r * 1024 + 512 + hh * 128 + 128, :]
                cx.dma("sp", hk[r][:, 0:HALO], ksrc, hk[r], writes=[hk[r]])
                cx.dma("sp", hv[r][:, 0:HALO], vsrc, hv[r], writes=[hv[r]])
            for (dst, srcs) in ((kh, hk), (vh, hv)):
                cx.op("dve", lambda e, dst=dst, srcs=srcs: e.tensor_scalar(out=dst[:, 0:HALO], in0=srcs[0][:, 0:HALO], scalar1=cst[:, C_SEL:C_SEL + 1], scalar2=None, op0=ALU.mult),
                      reads=[srcs[0], G["cstb"]], writes=[dst])
                for r in (1, 2):
                    cx.op("dve", lambda e, dst=dst, srcs=srcs, r=r: e.scalar_tensor_tensor(out=dst[:, 0:HALO], in0=srcs[r][:, 0:HALO], scalar=cst[:, C_SEL + r:C_SEL + r + 1],
                                                                                          in1=dst[:, 0:HALO], op0=ALU.mult, op1=ALU.add),
                          reads=[srcs[r], G["cstb"]], writes=[dst])
            if d > 1:
                cx.op("dve", lambda e: e.tensor_copy(out=qd[:, 0:T].rearrange("p (r l) -> p r l", r=d), in_=qh[:, 0:T].rearrange("p (l r) -> p r l", r=d)),
                      reads=[qh], writes=[qd])
                cx.op("act", lambda e: e.activation(out=kd[:, 0:HALO + T].rearrange("p (r l) -> p r l", r=d), in_=kh[:, 0:HALO + T].rearrange("p (l r) -> p r l", r=d), func=AF.Copy),
                      reads=[kh], writes=[kd])
                cx.op("dve", lambda e: e.tensor_copy(out=vd[:, 0:HALO + T].rearrange("p (r l) -> p r l", r=d), in_=vh[:, 0:HALO + T].rearrange("p (l r) -> p r l", r=d)),
                      reads=[vh], writes=[vd])
                Q, K, V = qd, kd, vd
            else:
                Q, K, V = qh, kh, vh
            items = [(r, b) for r in range(d) for b in range(nbl)]
            vtile = {}

            def vtrans(r, bi, Vt=None):
                nonlocal vti
                V_ = Vt
                tl = vt[vti % NVT]
                pv = psV[vti % 2]
                vti += 1
                kb = r * LK
                cx.op("pe", lambda p: p.transpose(out=pv[:, 0:128], in_=V_[:, kb + bi * 128:kb + (bi + 1) * 128], identity=ident[:]),
                      reads=[V_, G["identb"]], writes=[pv])
                cx.op("act", lambda a: a.activation(out=tl[:], in_=pv[:, 0:128], func=AF.Copy), reads=[pv], writes=[tl])
                vtile[(r, bi)] = tl

            def stA(i, Q=Q, K=K, V=V):
                r, b = items[i]
                if b == 0:
                    vtrans(r, 0, V)
                vtrans(r, b + 1, V)
                bk_ = bank[i % NSL]
                qb, kb = r * L, r * LK
                cx.op("pe", lambda p: p.matmul(bk_[:, 0:256], lhsT=Q[:, qb + b * 128:qb + (b + 1) * 128], rhs=K[:, kb + b * 128:kb + b * 128 + 256], start=True, stop=True),
                      reads=[Q, K], writes=[bk_])

            def stB(i):
                r, b = items[i]
                sl_ = i % NSL
                bk_, s_, p_, ml_, nm_ = bank[sl_], s_sb[sl_], p_sb[sl_], mlst[sl_], negm[sl_]
                mcol = C_AMASK + (256 if b == 0 else 0)
                cx.op("dve", lambda e: e.scalar_tensor_tensor(out=s_[:], in0=bk_[:, 0:256], scalar=scale, in1=cst[:, mcol:mcol + 256], op0=ALU.mult, op1=ALU.add),
                      reads=[bk_, G["cstb"]], writes=[s_])
                cx.op("dve", lambda e: e.reduce_max(out=ml_[:, 0:1], in_=s_[:], axis=AX.X), reads=[s_], writes=[ml_])
                cx.op("dve", lambda e: e.tensor_scalar(out=nm_[:], in0=ml_[:, 0:1], scalar1=-1.0, scalar2=None, op0=ALU.mult), reads=[ml_], writes=[nm_])
                cx.op("act", lambda a: a.activation(out=p_[:], in_=s_[:], func=AF.Exp, bias=nm_[:, 0:1], accum_out=ml_[:, 1:2]),
                      reads=[s_, nm_], writes=[p_, ml_])

            def stC(i):
                sl_ = i % NSL
                bk_, p_, pT_ = bank[sl_], p_sb[sl_], pT_sb[sl_]
                pPTv = bk_[:, 256:384].bitcast(BF16)
                for half in range(2):
                    cx.op("pe", lambda p, half=half: p.transpose(out=pPTv[:, half * 128:(half + 1) * 128], in_=p_[:, half * 128:(half + 1) * 128], identity=ident[:]),
                          reads=[p_, G["identb"]], writes=[bk_])
                cx.op("dve", lambda e: e.tensor_copy(out=pT_[:], in_=pPTv[:, 0:256]), reads=[bk_], writes=[pT_])

            def stD(i, g=g, hh=hh, d=d):
                r, b = items[i]
                sl_ = i % NSL
                bk_, pT_, o_, ml_ = bank[sl_], pT_sb[sl_], ost[sl_], mlst[sl_]
                vprev, vcur = vtile[(r, b)], vtile[(r, b + 1)]
                cx.op("pe", lambda p: p.matmul(bk_[:, 384:512], lhsT=pT_[:, 0:128], rhs=vprev[:], start=True, stop=False), reads=[pT_, vprev], writes=[bk_])
                cx.op("pe", lambda p: p.matmul(bk_[:, 384:512], lhsT=pT_[:, 128:256], rhs=vcur[:], start=False, stop=True), reads=[pT_, vcur], writes=[bk_])
                cx.op("act", lambda a: a.activation(out=o_[:], in_=bk_[:, 384:512], func=AF.Copy), reads=[bk_], writes=[o_])
                orows = G["oacc"][g].rearrange("(l r) f -> r l f", r=d)[r, b * 128:(b + 1) * 128, hh * 128:(hh + 1) * 128]
                cx.dma("sp", orows, o_[:], o_, reads=[o_])
                mrows = G["mlacc"].rearrange("(l r) f -> r l f", r=d)[r, b * 128:(b + 1) * 128, (g * 4 + hh) * 2:(g * 4 + hh) * 2 + 2]
                cx.dma("sp", mrows, ml_[:], ml_, reads=[ml_])

            skew(len(items), [(stA, 0), (stB, 0), (stC, 2), (stD, 4)])


def attn_merge_phase(cx, G):
    ident = G["ident"]
    NT_ = T // 128
    o3 = [cx.sb([128, 3, 512], F32, "mo") for _ in range(2)]
    ml = [cx.sb([128, 3, 4, 2], F32, "mml") for _ in range(2)]
    M = cx.sb([128, 4], F32, "mM")
    df = cx.sb([128, 3, 4], F32, "mdf")
    w = cx.sb([128, 3, 4], F32, "mw")
    wl = cx.sb([128, 3, 4], F32, "mwl")
    den = cx.sb([128, 4], F32, "mden")
    rden = cx.sb([128, 4], F32, "mrden")
    wn = cx.sb([128, 3, 4], F32, "mwn")
    acc = cx.sb([128, 512], F32, "macc")
    ob = [cx.sb([128, 512], BF16, "mob") for _ in range(2)]
    stg = [cx.sb([128, 4, 512], BF16, "mstg") for _ in range(2)]
    psT = [cx.ps([128, 1024], BF16, "mpT") for _ in range(2)]
    for tt in range(NT_):
        i2 = tt % 2
        o_, ml_, ob_, pT = o3[i2], ml[i2], ob[i2], psT[i2]
        sg = stg[(tt // 4) % 2]
        rows = slice(tt * 128, (tt + 1) * 128)
        for g in range(3):
            cx.dma("sp", o_[:, g, :], G["oacc"][g][rows, :], o_, writes=[o_])
        cx.dma("sp", ml_[:], G["mlacc"][rows, :].rearrange("t (g h k) -> t g h k", g=3, h=4), ml_, writes=[ml_])
        cx.op("dve", lambda e: e.tensor_tensor(out=M[:], in0=ml_[:, 0, :, 0], in1=ml_[:, 1, :, 0], op=ALU.max), reads=[ml_], writes=[M])
        cx.op("dve", lambda e: e.tensor_tensor(out=M[:], in0=M[:], in1=ml_[:, 2, :, 0], op=ALU.max), reads=[ml_], writes=[M])
        for g in range(3):
            cx.op("dve", lambda e, g=g: e.tensor_tensor(out=df[:, g, :], in0=ml_[:, g, :, 0], in1=M[:], op=ALU.subtract), reads=[ml_, M], writes=[df])
        cx.op("act", lambda a: a.activation(out=w[:], in_=df[:], func=AF.Exp), reads=[df], writes=[w])
        cx.op("dve", lambda e: e.tensor_tensor(out=wl[:], in0=w[:], in1=ml_[:, :, :, 1], op=ALU.mult), reads=[w, ml_], writes=[wl])
        cx.op("dve", lambda e: e.tensor_tensor(out=den[:], in0=wl[:, 0, :], in1=wl[:, 1, :], op=ALU.add), reads=[wl], writes=[den])
        cx.op("dve", lambda e: e.tensor_tensor(out=den[:], in0=den[:], in1=wl[:, 2, :], op=ALU.add), reads=[wl], writes=[den])
        cx.op("dve", lambda e: e.reciprocal(out=rden[:], in_=den[:]), reads=[den], writes=[rden])
        for g in range(3):
            cx.op("dve", lambda e, g=g: e.tensor_tensor(out=wn[:, g, :], in0=w[:, g, :], in1=rden[:], op=ALU.mult), reads=[w, rden], writes=[wn])
        for hh in range(4):
            hs = slice(hh * 128, (hh + 1) * 128)
            cx.op("dve", lambda e, hh=hh, hs=hs: e.tensor_scalar(out=acc[:, hs], in0=o_[:, 0, hs], scalar1=wn[:, 0, hh:hh + 1], scalar2=None, op0=ALU.mult),
                  reads=[o_, wn], writes=[acc])
            cx.op("dve", lambda e, hh=hh, hs=hs: e.scalar_tensor_tensor(out=acc[:, hs], in0=o_[:, 1, hs], scalar=wn[:, 1, hh:hh + 1], in1=acc[:, hs], op0=ALU.mult, op1=ALU.add),
                  reads=[o_, wn], writes=[acc])
            cx.op("dve", lambda e, hh=hh, hs=hs: e.scalar_tensor_tensor(out=ob_[:, hs], in0=o_[:, 2, hs], scalar=wn[:, 2, hh:hh + 1], in1=acc[:, hs], op0=ALU.mult, op1=ALU.add),
                  reads=[o_, wn, acc], writes=[ob_])
        for hh in range(4):
            cx.op("pe", lambda p, hh=hh: p.transpose(out=pT[:, hh * 128:(hh + 1) * 128], in_=ob_[:, hh * 128:(hh + 1) * 128], identity=ident[:]),
                  reads=[ob_, G["identb"]], writes=[pT])
        q4 = tt % 4
        cx.op("act", lambda a: a.activation(out=sg[:, :, q4 * 128:(q4 + 1) * 128], in_=pT[:, 0:512].rearrange("p (h t) -> p h t", h=4), func=AF.Copy),
              reads=[pT], writes=[sg])
        if q4 == 3:
            t0 = (tt - 3) * 128
            cx.dma("sp", G["oaT"].rearrange("(c p) t -> p c t", p=128)[:, :, t0:t0 + 512], sg[:], sg, reads=[sg])


def build():
    nc = bass.Bass("TRN2", target_bir_lowering=False)
    cx = Cx(nc)
    G = {}

    def din(name, shape, dt=F32):
        return nc.dram_tensor(name, list(shape), dt, kind="ExternalInput").ap()

    DBGT = {}

    def dscr(name, shape, dt):
        a = nc.dram_tensor(name, list(shape), dt, kind="Internal").ap()
        if name in DBG:
            DBGT[name] = (a, nc.dram_tensor("dbg_" + name, [shape[0], 256], dt, kind="ExternalOutput").ap())
        return a

    def finish():
        db = Buf("dbgcopy")
        for name, (a, o) in DBGT.items():
            cx.dma("sp", o, a[:, 0:256], db)
        cx.barrier()
        return nc

    xT = din("xT", [D, T])
    WSHAPE = {"ffn1_w_gate": [D, DFF], "ffn1_w_up": [D, DFF], "ffn1_w_down": [DFF, D], "w_in": [D, INC],
              "w_proj_attn": [512, D], "w_proj_ret": [4096, D], "w_out": [D, D],
              "ffn2_w_gate": [D, DFF], "ffn2_w_up": [D, DFF], "ffn2_w_down": [DFF, D]}
    USED_W.clear()

    class _WL:
        def __init__(self, nm):
            self.nm = nm

        def __getitem__(self, l):
            key = "%s_%d" % (self.nm, l)
            if key not in USED_W:
                USED_W[key] = (self.nm, l, din(key, WSHAPE[self.nm]))
            return USED_W[key][2]

    W = {nm: _WL(nm) for nm in WSHAPE}
    cst_d = din("cst", [128, CW])
    G["ropeA"] = din("ropeA", [2, 128, T])
    G["ropeR"] = din("ropeR", [2, 128, T])
    yT = nc.dram_tensor("yT", [D, T], F32, kind="ExternalOutput").ap()

    XA = dscr("XA", [D, T], F32)
    XB = dscr("XB", [D, T], F32)
    uT = dscr("uT", [DFF, T], BF16)
    for nm, rows in (("qaT", 1536), ("kaT", 1536), ("vaT", 1536), ("qrT", 2048), ("krT", 2048), ("vrT", 4096), ("sgT", 4096),
                     ("gaT", 2048), ("gbT", 2048), ("oaT", 512), ("orT", 4096), ("mT", 2048)):
        G[nm] = dscr(nm, [rows, T], BF16)
    G["oacc"] = [dscr("oacc%d" % g, [T, 512], F32) for g in range(3)]
    G["mlacc"] = dscr("mlacc", [T, 24], F32)
    EX = []
    G["exS_in"], G["exS_out"] = [], []
    for i in range(4):
        a = nc.dram_tensor("exS_in%d" % i, [512, 512], F32)
        b = nc.dram_tensor("exS_out%d" % i, [4 * 512, 512], F32)
        EX.append((a, b))
        G["exS_in"].append(a.ap())
        G["exS_out"].append(b.ap())
    G["exT2"], G["exT2o"] = [], []
    for j in range(4):
        a = nc.dram_tensor("exT2_%d" % j, [256, 2048], BF16)
        b = nc.dram_tensor("exT2o_%d" % j, [4 * 256, 2048], BF16)
        EX.append((a, b))
        G["exT2"].append(a.ap())
        G["exT2o"].append(b.ap())
    a = nc.dram_tensor("exT1", [1024, 512], BF16)
    b = nc.dram_tensor("exT1o", [4 * 1024, 512], BF16)
    EX.append((a, b))
    G["exT1"], G["exT1o"] = a.ap(), b.ap()
    a = nc.dram_tensor("exT0", [1024, 128], BF16)
    b = nc.dram_tensor("exT0o", [4 * 1024, 128], BF16)
    EX.append((a, b))
    G["exT0"], G["exT0o"] = a.ap(), b.ap()
    cc_sems = [nc.alloc_semaphore("cc%d" % i) for i in range(len(EX) * DEPTH)]
    ccn = [0]

    cst = nc.alloc_sbuf_tensor("cst_sb", [128, CW], F32)
    identb = nc.alloc_sbuf_tensor("identb", [128, 128], BF16)
    ones = nc.alloc_sbuf_tensor("ones", [128, 128], BF16)
    eps = nc.alloc_sbuf_tensor("eps", [128, 2], F32)
    G["cst"], G["ident"], G["ones"], G["eps"] = cst, identb, ones, eps
    G["cstb"], G["identb"], G["onesb"] = Buf("cst"), Buf("ident"), Buf("ones")
    cx.dma("sp", cst[:], cst_d, G["cstb"], writes=[G["cstb"]])
    cx.op("dve", lambda e: e.tensor_copy(out=identb[:], in_=cst[:, C_ID:C_ID + 128]), reads=[G["cstb"]], writes=[G["identb"]])
    cx.op("pool", lambda e: e.memset(ones[:], 1.0), writes=[G["onesb"]])
    cx.op("pool", lambda e: e.memset(eps[:, 0:1], RMS_EPS), writes=[G["cstb"]])
    cx.op("pool", lambda e: e.memset(eps[:, 1:2], GN_EPS), writes=[G["cstb"]])
    cx.barrier()

    def ffn(l, which, xi, xo_, gidx):
        wg, wu, wd = W["ffn%d_w_gate" % which][l], W["ffn%d_w_up" % which][l], W["ffn%d_w_down" % which][l]
        groups = []
        for s in range(DFF // 256):
            groups.append(dict(outs=[(wg[:, s * 256:(s + 1) * 256], 0), (wu[:, s * 256:(s + 1) * 256], 0)],
                               units=[[(0, 0), (1, 0)], [(0, 128), (1, 128)]], kind="swiglu", dst=uT, row0=s * 256))
        with cx.phase("ffn_up"):
            linear_phase(cx, G, 2048, 256, groups, norm=(xi, gidx), need=("stgA", "tmp"))
        if STOP == "ffn_up":
            return True
        with cx.phase("ffn_down"):
            linear_phase(cx, G, 1024, 0, simple_groups(wd, D, "res", xo_, xin=xi, scale=0.5), xsrc=[uT], need=("res",))
        return False

    def exchange():
        nonlocal_ = ccn
        for (i_, o_) in EX:
            ins = nc.gpsimd.collective_compute("AllGather", ALU.bypass, replica_groups=[[0, 1, 2, 3], [4, 5, 6, 7]],
                                               ins=[i_.ap().opt()], outs=[o_.ap().opt()])
            sem = cc_sems[nonlocal_[0]]
            nonlocal_[0] += 1
            ins.then_inc(sem)
            nc.gpsimd.wait_ge(sem, 1)
            for e in ("pe", "act", "dve", "sp"):
                cx.eng[e].wait_ge(sem, 1)

    xcur, xnxt = xT, XA

    def done():
        return nc

    for l in range(DEPTH):
        if not ("ffn1" in SKIP and l == 0):
            if ffn(l, 1, xcur, xnxt, l * 3 + 0):
                return finish()
            xcur, xnxt = xnxt, (XB if xnxt is XA else XA)
        if STOP == "ffn1":
            return finish()
        wi = W["w_in"][l]
        groups = []
        for i in range(6):
            groups.append(dict(outs=[(wi[:, O_QA + i * 256:O_QA + (i + 1) * 256], 0)], units=[[(0, 0), (0, 128)]], kind="ropeA", dst=G["qaT"], row0=i * 256))
        for i in range(6):
            groups.append(dict(outs=[(wi[:, O_KA + i * 256:O_KA + (i + 1) * 256], 0)], units=[[(0, 0), (0, 128)]], kind="ropeA", dst=G["kaT"], row0=i * 256))
        for i in range(8):
            groups.append(dict(outs=[(wi[:, O_QR + i * 256:O_QR + (i + 1) * 256], 0)], units=[[(0, 0), (0, 128)]], kind="ropeR", dst=G["qrT"], row0=i * 256))
        for i in range(8):
            groups.append(dict(outs=[(wi[:, O_KR + i * 256:O_KR + (i + 1) * 256], 0)], units=[[(0, 0), (0, 128)]], kind="ropeR", dst=G["krT"], row0=i * 256))
        groups += simple_groups(wi, 1536, "copy", G["vaT"], col0=O_VA)
        groups += simple_groups(wi, 4096, "copy", G["vrT"], col0=O_VR)
        groups += simple_groups(wi, 4096, "silu", G["sgT"], col0=O_GR)
        groups += simple_groups(wi, 2048, "sigmoid", G["gaT"], col0=O_GA)
        groups += simple_groups(wi, 2048, "sigmoid", G["gbT"], col0=O_GB)
        if "w_in" not in SKIP:
            with cx.phase("w_in"):
                linear_phase(cx, G, 2048, 256, groups, norm=(xcur, l * 3 + 1), need=("stgA", "stgB", "tmp", "rope"))
        if STOP == "w_in":
            return finish()
        with cx.phase("ret1"):
            retention_local_state(cx, G)
            tb_ = Buf("tails")
            for j in range(2):
                cx.dma("sp", G["exT2"][j], G["kaT"][1024 + j * 256:1024 + (j + 1) * 256, T - 2048:T], tb_)
                cx.dma("sp", G["exT2"][2 + j], G["vaT"][1024 + j * 256:1024 + (j + 1) * 256, T - 2048:T], tb_)
            cx.dma("sp", G["exT1"][0:512, :], G["kaT"][512:1024, T - 512:T], tb_)
            cx.dma("sp", G["exT1"][512:1024, :], G["vaT"][512:1024, T - 512:T], tb_)
            cx.dma("sp", G["exT0"][0:512, :], G["kaT"][0:512, T - 128:T], tb_)
            cx.dma("sp", G["exT0"][512:1024, :], G["vaT"][0:512, T - 128:T], tb_)
        if STOP == "ret1":
            return finish()
        if "exch" not in SKIP:
            exchange()
        if STOP == "exch":
            return finish()
        with cx.phase("ret2"):
            Sf = cx.sb([128, 8, 2, 512], F32, "Sf")
            Sb = cx.sb([128, 8, 2, 512], BF16, "Sb")
            with ExitStack() as st2:
                old = cx.st
                cx.st = st2
                state_init(cx, G, (Sf, Sb), False)
                cx.barrier()
                cx.st = old
            retention_pass(cx, G, (Sf, Sb), False)
        if STOP == "ret":
            return finish()
        with cx.phase("attn"):
            attention_phase(cx, G)
        with cx.phase("merge"):
            attn_merge_phase(cx, G)
        if STOP == "attn":
            return finish()
        pa, pr = W["w_proj_attn"][l], W["w_proj_ret"][l]
        groups = []
        for s in range(D // 256):
            groups.append(dict(outs=[(pa[:, s * 256:(s + 1) * 256], 0), (pr[:, s * 256:(s + 1) * 256], 4)],
                               units=[[(0, 0), (1, 0)], [(0, 128), (1, 128)]], kind="gate", dst=G["mT"], row0=s * 256, ga=G["gaT"], gb=G["gbT"]))
        with cx.phase("proj"):
            linear_phase(cx, G, 1024, 0, groups, xsrc=[G["oaT"], G["orT"]], need=("stgA", "tmp", "gate"))
        if STOP == "proj":
            return finish()
        with cx.phase("w_out"):
            linear_phase(cx, G, 2048, 0, simple_groups(W["w_out"][l], D, "res", xnxt, xin=xcur, scale=1.0), xsrc=[G["mT"]], need=("res",))
        xcur, xnxt = xnxt, (XB if xnxt is XA else XA)
        if STOP == "w_out":
            return finish()
        if ffn(l, 2, xcur, xnxt, l * 3 + 2):
            return finish()
        xcur, xnxt = xnxt, (XB if xnxt is XA else XA)
        if STOP == "layer0":
            return finish()
    with cx.phase("final"):
        final_norm_phase(cx, G, xcur, 6, yT)
    print("kernel build: instructions ~", cx.ninst, "counts", cx.cnt)
    return nc


def _consts(qd):
    c = np.zeros((128, CW), np.float64)
    c[:, C_ID:C_ID + 128] = np.eye(128)
    i = np.arange(128)
    for h in range(8):
        gm = GAMMA[h]
        rel = i[None, :] - i[:, None]
        m = np.where(rel >= 0, gm ** np.maximum(rel, 0), 0.0) * (256.0 ** -0.5)
        c[:, C_RMASK + h * 128:C_RMASK + (h + 1) * 128] = m
        c[:, C_RDQ + h * 128:C_RDQ + (h + 1) * 128] = (gm ** (i + 1.0))[None, :]
        c[:, C_RDK + h] = gm ** (127.0 - i) * (256.0 ** -0.5)
        for n in range(32):
            dk = gm ** (T - 1.0 - (n * 128 + i)) * (256.0 ** -0.5)
            c[:, C_DKABS + n * 8 + h] = np.where(dk < 1e-30, 0.0, dk)
        for r in range(4):
            c[:, C_COEF + r * 8 + h] = (gm ** (4096.0 * (qd - r - 1))) if r < qd else 0.0
    for r in range(4):
        c[:, C_SEL + r] = 1.0 if r == qd - 1 else 0.0
    qi = i[:, None]
    kj = np.arange(256)[None, :]
    band = np.where((kj >= qi) & (kj <= qi + 128), 0.0, NEG)
    c[:, C_AMASK:C_AMASK + 256] = band
    first = band.copy()
    if qd == 0:
        first[:, 0:128] = NEG
    c[:, C_AMASK + 256:C_AMASK + 512] = first
    return c


def _rope(qd, dim, reps):
    inv = (1.0 / (10000.0 ** (np.arange(0, dim, 2, dtype=np.float32) / np.float32(dim)))).astype(np.float32)
    pos = (np.arange(T, dtype=np.float32) + np.float32(qd * T)).astype(np.float32)
    ang = (pos[None, :] * inv[:, None]).astype(np.float32)
    cs = np.stack([np.cos(ang), np.sin(ang)]).astype(np.float32)
    return np.ascontiguousarray(np.tile(cs, (1, reps, 1)))


def _perm_w_in(w_in):
    w = np.array(w_in, copy=True)
    for base in (0, 1536):
        blk = w_in[:, :, base:base + 1536].reshape(DEPTH, D, 6, 2, 2, 64)
        w[:, :, base:base + 1536] = blk.transpose(0, 1, 2, 4, 3, 5).reshape(DEPTH, D, 1536)
    return w


def kernel(x, ffn1_norm, ffn1_w_gate, ffn1_w_up, ffn1_w_down, mix_norm, w_in, w_proj_attn, w_proj_ret, w_out,
           ffn2_norm, ffn2_w_gate, ffn2_w_up, ffn2_w_down, final_norm):
    x = np.asarray(x, np.float32)
    gains = [ffn1_norm[0], mix_norm[0], ffn2_norm[0], ffn1_norm[1], mix_norm[1], ffn2_norm[1], final_norm]
    gcols = np.concatenate([np.asarray(g, np.float32).reshape(16, 128).T for g in gains], axis=1)
    wperm = _perm_w_in(np.asarray(w_in, np.float32))
    allw = {"ffn1_w_gate": ffn1_w_gate, "ffn1_w_up": ffn1_w_up, "ffn1_w_down": ffn1_w_down, "w_in": wperm,
            "w_proj_attn": w_proj_attn, "w_proj_ret": w_proj_ret, "w_out": w_out,
            "ffn2_w_gate": ffn2_w_gate, "ffn2_w_up": ffn2_w_up, "ffn2_w_down": ffn2_w_down}
    nc = build()
    shared = {key: np.ascontiguousarray(np.asarray(allw[nm][l], np.float32)) for key, (nm, l, _) in USED_W.items()}
    in_maps = []
    for c in range(NCORES):
        b, qd = c // 4, c % 4
        m = dict(shared)
        m["xT"] = np.ascontiguousarray(x[b, qd * T:(qd + 1) * T, :].T)
        if XOVR is not None and c == 0:
            m["xT"][:, :XOVR.shape[0]] = XOVR.T
        cc = _consts(qd)
        cc[:, C_GAIN:C_GAIN + 112] = gcols
        m["cst"] = cc.astype(np.float32)
        m["ropeA"] = _rope(qd, 128, 2)
        m["ropeR"] = _rope(qd, 256, 1)
        in_maps.append(m)
    res = run_bass_kernel_spmd(nc, in_maps, core_ids=list(range(NCORES)))
    kernel.last = res
    out = np.empty((2, 4 * T, D), np.float32)
    for c in range(NCORES):
        b, qd = c // 4, c % 4
        out[b, qd * T:(qd + 1) * T, :] = res.results[c]["yT"].T
    return out
```

```python
import os
from contextlib import ExitStack, contextmanager
import numpy as np
import concourse.bass as bass
import concourse.mybir as mybir
from concourse.bass_utils import run_bass_kernel_spmd

F32 = mybir.dt.float32
BF16 = mybir.dt.bfloat16
ALU = mybir.AluOpType
AF = mybir.ActivationFunctionType
AX = mybir.AxisListType

NCORES = 8
T = 4096
D = 2048
DFF = 5632
DEPTH = 2
INC = 20992
NEG = -1.0e30
RMS_EPS = 1e-6
GN_EPS = 1e-5
GAMMA = [1.0 - 2.0 ** (-5.0 - h) for h in range(8)]
O_QA, O_KA, O_VA, O_QR, O_KR, O_VR, O_GR, O_GA, O_GB = 0, 1536, 3072, 4608, 6656, 8704, 12800, 16896, 18944
C_ID = 0
C_RMASK = 128
C_RDQ = C_RMASK + 1024
C_RDK = C_RDQ + 1024
C_COEF = C_RDK + 8
C_SEL = C_COEF + 32
C_AMASK = C_SEL + 4
C_GAIN = C_AMASK + 512
C_DKABS = C_GAIN + 112
CW = C_DKABS + 256

STOP = os.environ.get("KSTOP", "")
SKIP = os.environ.get("KSKIP", "").split(",")
USED_W = {}
DBG = [s for s in os.environ.get("KDBG", "").split(",") if s]


class Buf:
    __slots__ = ("name", "w", "r", "ds")

    def __init__(self, name):
        self.name = name
        self.w = {}
        self.r = {}
        self.ds = None


def _merge(d, src):
    for k, (sem, v) in src.items():
        if k not in d or d[k][1] < v:
            d[k] = (sem, v)


class Tl:
    def __init__(self, h, name, psum=False):
        self.h = h
        self.b = Buf(name)
        self.psum = psum

    def __getitem__(self, k):
        return self.h[k]


class Cx:
    def __init__(self, nc):
        self.nc = nc
        self.eng = {"pe": nc.tensor, "act": nc.scalar, "dve": nc.vector, "pool": nc.gpsimd, "sp": nc.sync}
        self.psem = {}
        self.cnt = {}
        for e in ["pe", "act", "dve", "pool"]:
            self.psem[e] = nc.alloc_semaphore("s_" + e)
            self.cnt[e] = 0
        self.known = {e: {} for e in self.eng}
        self.dpool = [[nc.alloc_semaphore("d%d" % i), 0, "d%d" % i] for i in range(44)]
        self.dfree = list(range(8, 44))
        self.dfree_sw = list(range(8))
        self.dused = []
        self.uid = 0
        self.st = None
        self.ninst = 0

    def sb(self, shape, dtype, name="t"):
        self.uid += 1
        nm = "%s_%d" % (name, self.uid)
        h = self.st.enter_context(self.nc.sbuf_tensor(nm, list(shape), dtype))
        return Tl(h, nm)

    def ps(self, shape, dtype, name="p"):
        self.uid += 1
        nm = "%s_%d" % (name, self.uid)
        h = self.st.enter_context(self.nc.psum_tensor(nm, list(shape), dtype))
        return Tl(h, nm, psum=True)

    def _dsem(self, buf, sw=False):
        if buf.ds is None:
            fl = self.dfree_sw if sw else self.dfree
            i = fl.pop(0)
            buf.ds = self.dpool[i]
            self.dused.append((i, buf, sw))
        return buf.ds

    def _wait(self, e, toks):
        for name, (sem, val) in toks.items():
            if e == "pe" and name == "s_pe":
                continue
            if self.known[e].get(name, 0) < val:
                self.eng[e].wait_ge(sem, val)
                self.known[e][name] = val
                self.ninst += 1

    def _deps(self, reads, writes):
        deps = {}
        for b in reads:
            _merge(deps, b.w)
        for b in writes:
            _merge(deps, b.w)
            _merge(deps, b.r)
        return deps

    def _mark(self, tokname, tok, reads, writes):
        t = {tokname: tok}
        for b in writes:
            _merge(b.w, t)
            b.r = {}
        for b in reads:
            _merge(b.r, t)

    def op(self, e, fn, reads=(), writes=()):
        writes = list(writes) + [x for x in reads if isinstance(x, Tl) and x.psum]
        reads = [x.b if isinstance(x, Tl) else x for x in reads if not (isinstance(x, Tl) and x.psum)]
        writes = [x.b if isinstance(x, Tl) else x for x in writes]
        self._wait(e, self._deps(reads, writes))
        ins = fn(self.eng[e])
        self.cnt[e] += 1
        ins.then_inc(self.psem[e], 1)
        self.ninst += 1
        self._mark("s_" + e, (self.psem[e], self.cnt[e]), reads, writes)

    def dma(self, e, out, in_, sbuf, reads=(), writes=()):
        reads = [x.b if isinstance(x, Tl) else x for x in reads]
        writes = [x.b if isinstance(x, Tl) else x for x in writes]
        sb = sbuf.b if isinstance(sbuf, Tl) else sbuf
        self._wait(e, self._deps(reads, writes))
        ds = self._dsem(sb, e == "pool")
        ins = self.eng[e].dma_start(out=out, in_=in_)
        ds[1] += 16
        ins.then_inc(ds[0], 16)
        self.ninst += 1
        self._mark(ds[2], (ds[0], ds[1]), reads, writes)

    def barrier(self):
        allt = {}
        for e in self.psem:
            if self.cnt[e] > 0:
                allt["s_" + e] = (self.psem[e], self.cnt[e])
        for s, c, n in self.dpool:
            if c > 0:
                allt[n] = (s, c)
        for e in self.eng:
            self._wait(e, allt)
        for i, b, sw in self.dused:
            b.ds = None
            (self.dfree_sw if sw else self.dfree).append(i)
        self.dused = []

    @contextmanager
    def phase(self, name):
        st = ExitStack()
        self.st = st
        try:
            yield st
        finally:
            self.barrier()
            st.close()


def linear_phase(cx, G, TB, NS, groups, xsrc=None, norm=None, need=()):
    nc = cx.nc
    KC = 16 if norm is not None else sum(a.shape[0] for a in xsrc) // 128
    NSUB = TB // 512
    X = cx.sb([128, KC, TB], BF16, "X")
    maxk = max(max(w.shape[0] // 128 for (w, _) in g["outs"]) for g in groups)
    nouts = max(len(g["outs"]) for g in groups)
    slabs = [[cx.sb([128, maxk, 256], BF16, "slab") for _ in range(nouts)] for _ in range(2)]
    NPB = 3
    banks = [[cx.ps([128, 512], F32, "bk") for _ in range(2)] for _ in range(NPB)]
    stgA = [cx.sb([128, TB], BF16, "stgA") for _ in range(2)] if "stgA" in need else None
    stgB = [cx.sb([128, TB], BF16, "stgB") for _ in range(2)] if "stgB" in need else None
    tmp = [cx.sb([128, 512], F32, "tmp") for _ in range(4)] if "tmp" in need else None
    xin = [cx.sb([128, TB], F32, "xin") for _ in range(2)] if "res" in need else None
    xo = [cx.sb([128, TB], F32, "xo") for _ in range(2)] if "res" in need else None
    gta = [cx.sb([128, TB], BF16, "gta") for _ in range(2)] if "gate" in need else None
    gtb = [cx.sb([128, TB], BF16, "gtb") for _ in range(2)] if "gate" in need else None
    rtab = cx.sb([128, 4, TB], F32, "rtab") if "rope" in need else None
    if norm is not None:
        xs32 = [cx.sb([128, 16, NS], F32, "xs32") for _ in range(2)]
        sq = [cx.sb([128, 16, NS], BF16, "sq") for _ in range(2)]
        psn = [cx.ps([128, 512], F32, "psn") for _ in range(2)]
        rs0 = [cx.sb([128, NS], F32, "rs0") for _ in range(2)]
        rstd = [cx.sb([128, NS], F32, "rstd") for _ in range(2)]
    rot = {"bank": 0, "stg": 0, "tmp": 0, "res": 0, "gate": 0}

    def load_slabs(gi):
        g = groups[gi]
        for oi, (w, kc0) in enumerate(g["outs"]):
            sl = slabs[gi % 2][oi]
            nk = w.shape[0] // 128
            cx.dma("pool", sl[:, 0:nk, :], w.rearrange("(c p) f -> p c f", p=128), sl, writes=[sl])

    for tb in range(T // TB):
        t0 = tb * TB
        if norm is not None:
            x_ap, gidx = norm

            def nA(s_):
                x3, sq_ = xs32[s_ % 2], sq[s_ % 2]
                ts0 = t0 + s_ * NS
                cx.dma("sp", x3[:], x_ap.rearrange("(c p) t -> p c t", p=128)[:, :, ts0:ts0 + NS], x3, writes=[x3])
                cx.op("act", lambda a: a.activation(out=sq_[:], in_=x3[:], func=AF.Square), reads=[x3], writes=[sq_])

            def nB(s_):
                sq_, pn, r0, rd = sq[s_ % 2], psn[s_ % 2], rs0[s_ % 2], rstd[s_ % 2]
                for c in range(16):
                    cx.op("pe", lambda p, c=c: p.matmul(pn[:, 0:NS], lhsT=G["ones"][:], rhs=sq_[:, c, :], start=(c == 0), stop=(c == 15)),
                          reads=[sq_, G["onesb"]], writes=[pn])
                cx.op("act", lambda a: a.activation(out=r0[:], in_=pn[:, 0:NS], func=AF.Sqrt, scale=1.0 / D, bias=G["eps"][:, 0:1]),
                      reads=[pn, G["cstb"]], writes=[r0])
                cx.op("dve", lambda v: v.reciprocal(out=rd[:], in_=r0[:]), reads=[r0], writes=[rd])

            def nC(s_):
                x3, rd = xs32[s_ % 2], rstd[s_ % 2]
                for c in range(16):
                    cx.op("dve", lambda v, c=c: v.scalar_tensor_tensor(
                        out=X[:, c, s_ * NS:(s_ + 1) * NS], in0=x3[:, c, :],
                        scalar=G["cst"][:, C_GAIN + gidx * 16 + c:C_GAIN + gidx * 16 + c + 1],
                        in1=rd[:], op0=ALU.mult, op1=ALU.mult), reads=[x3, rd, G["cstb"]], writes=[X])

            skew(TB // NS, [(nA, 0), (nB, 1), (nC, 1)])
        else:
            kc = 0
            for a in xsrc:
                n = a.shape[0] // 128
                av = a.rearrange("(c p) t -> p c t", p=128)
                for c0 in range(0, n, 8):
                    c1 = min(n, c0 + 8)
                    cx.dma("sp", X[:, kc + c0:kc + c1, :], av[:, c0:c1, t0:t0 + TB], X, writes=[X])
                kc += n
        if rtab is not None:
            ra, rr = G["ropeA"], G["ropeR"]
            cx.dma("sp", rtab[:, 0:2, :], ra.rearrange("k p t -> p k t")[:, :, t0:t0 + TB], rtab, writes=[rtab])
            cx.dma("sp", rtab[:, 2:4, :], rr.rearrange("k p t -> p k t")[:, :, t0:t0 + TB], rtab, writes=[rtab])
        load_slabs(0)
        for gi, g in enumerate(groups):
            if gi + 1 < len(groups):
                load_slabs(gi + 1)
            kind = g["kind"]
            sl = slabs[gi % 2]
            for ui, unit in enumerate(g["units"]):
                if kind in ("copy", "silu", "sigmoid", "swiglu", "ropeA", "ropeR", "gate"):
                    sA = stgA[rot["stg"] % 2]
                    sB = stgB[rot["stg"] % 2] if stgB is not None else None
                    rot["stg"] += 1
                if kind == "res":
                    xi = xin[rot["res"] % 2]
                    xoo = xo[rot["res"] % 2]
                    rot["res"] += 1
                    r0 = g["row0"] + ui * 128
                    cx.dma("sp", xi[:], g["xin"][r0:r0 + 128, t0:t0 + TB], xi, writes=[xi])
                if kind == "gate":
                    ga_t = gta[rot["gate"] % 2]
                    gb_t = gtb[rot["gate"] % 2]
                    rot["gate"] += 1
                    r0 = g["row0"] + ui * 128
                    cx.dma("sp", ga_t[:], g["ga"][r0:r0 + 128, t0:t0 + TB], ga_t, writes=[ga_t])
                    cx.dma("sp", gb_t[:], g["gb"][r0:r0 + 128, t0:t0 + TB], gb_t, writes=[gb_t])
                for ts in range(NSUB):
                    bk = banks[rot["bank"] % NPB]
                    rot["bank"] += 1
                    cs = slice(ts * 512, (ts + 1) * 512)
                    for j, (oi, col) in enumerate(unit):
                        w, kc0 = g["outs"][oi]
                        nk = w.shape[0] // 128
                        for kc in range(nk):
                            cx.op("pe", lambda p, j=j, oi=oi, col=col, kc=kc, kc0=kc0, nk=nk: p.matmul(
                                bk[j][:], lhsT=sl[oi][:, kc, col:col + 128], rhs=X[:, kc0 + kc, cs],
                                start=(kc == 0), stop=(kc == nk - 1)), reads=[sl[oi], X], writes=[bk[j]])
                    if kind in ("copy", "silu", "sigmoid"):
                        fn = {"copy": AF.Copy, "silu": AF.Silu, "sigmoid": AF.Sigmoid}[kind]
                        cx.op("act", lambda a, fn=fn: a.activation(out=sA[:, cs], in_=bk[0][:], func=fn), reads=[bk[0]], writes=[sA])
                    elif kind == "swiglu":
                        tm = tmp[rot["tmp"] % 4]
                        rot["tmp"] += 1
                        cx.op("act", lambda a: a.activation(out=tm[:], in_=bk[0][:], func=AF.Silu), reads=[bk[0]], writes=[tm])
                        cx.op("dve", lambda v: v.tensor_tensor(out=sA[:, cs], in0=bk[1][:], in1=tm[:], op=ALU.mult), reads=[bk[1], tm], writes=[sA])
                    elif kind in ("ropeA", "ropeR"):
                        k0 = 0 if kind == "ropeA" else 2
                        cc = rtab[:, k0, cs]
                        ss = rtab[:, k0 + 1, cs]
                        t1, t2, t3, t4 = [tmp[(rot["tmp"] + i) % 4] for i in range(4)]
                        rot["tmp"] += 4
                        A, B = bk[0], bk[1]
                        cx.op("dve", lambda v: v.tensor_tensor(out=t1[:], in0=A[:], in1=cc, op=ALU.mult), reads=[A, rtab], writes=[t1])
                        cx.op("dve", lambda v: v.tensor_tensor(out=t2[:], in0=B[:], in1=ss, op=ALU.mult), reads=[B, rtab], writes=[t2])
                        cx.op("pool", lambda v: v.tensor_tensor(out=sA[:, cs], in0=t1[:], in1=t2[:], op=ALU.subtract), reads=[t1, t2], writes=[sA])
                        cx.op("dve", lambda v: v.tensor_tensor(out=t3[:], in0=B[:], in1=cc, op=ALU.mult), reads=[B, rtab], writes=[t3])
                        cx.op("dve", lambda v: v.tensor_tensor(out=t4[:], in0=A[:], in1=ss, op=ALU.mult), reads=[A, rtab], writes=[t4])
                        cx.op("pool", lambda v: v.tensor_tensor(out=sB[:, cs], in0=t3[:], in1=t4[:], op=ALU.add), reads=[t3, t4], writes=[sB])
                    elif kind == "res":
                        sc = g["scale"]
                        cx.op("dve", lambda v: v.scalar_tensor_tensor(out=xoo[:, cs], in0=bk[0][:], scalar=sc, in1=xi[:, cs],
                                                                      op0=ALU.mult, op1=ALU.add), reads=[bk[0], xi], writes=[xoo])
                    elif kind == "gate":
                        t1, t2 = [tmp[(rot["tmp"] + i) % 4] for i in range(2)]
                        rot["tmp"] += 2
                        cx.op("dve", lambda v: v.tensor_tensor(out=t1[:], in0=bk[0][:], in1=ga_t[:, cs], op=ALU.mult), reads=[bk[0], ga_t], writes=[t1])
                        cx.op("dve", lambda v: v.tensor_tensor(out=t2[:], in0=bk[1][:], in1=gb_t[:, cs], op=ALU.mult), reads=[bk[1], gb_t], writes=[t2])
                        cx.op("pool", lambda v: v.tensor_tensor(out=sA[:, cs], in0=t1[:], in1=t2[:], op=ALU.add), reads=[t1, t2], writes=[sA])
                tsl = slice(t0, t0 + TB)
                if kind in ("copy", "silu", "sigmoid", "swiglu", "gate"):
                    r0 = g["row0"] + ui * 128
                    cx.dma("sp", g["dst"][r0:r0 + 128, tsl], sA[:], sA, reads=[sA])
                elif kind == "ropeR":
                    r0 = g["row0"]
                    cx.dma("sp", g["dst"][r0:r0 + 128, tsl], sA[:], sA, reads=[sA])
                    cx.dma("sp", g["dst"][r0 + 128:r0 + 256, tsl], sB[:], sB, reads=[sB])
                elif kind == "ropeA":
                    r0 = g["row0"]
                    cx.dma("sp", g["dst"][r0:r0 + 64, tsl], sA[0:64, :], sA, reads=[sA])
                    cx.dma("sp", g["dst"][r0 + 128:r0 + 192, tsl], sA[64:128, :], sA, reads=[sA])
                    cx.dma("sp", g["dst"][r0 + 64:r0 + 128, tsl], sB[0:64, :], sB, reads=[sB])
                    cx.dma("sp", g["dst"][r0 + 192:r0 + 256, tsl], sB[64:128, :], sB, reads=[sB])
                elif kind == "res":
                    r0 = g["row0"] + ui * 128
                    cx.dma("sp", g["dst"][r0:r0 + 128, tsl], xoo[:], xoo, reads=[xoo])


def skew(n, stages):
    md = max(d for _, d in stages)
    for t in range(n + md):
        for fn, d in stages:
            i = t - d
            if 0 <= i < n:
                fn(i)


def simple_groups(w, ncols, kind, dst, col0=0, row0=0, **kw):
    gs = []
    for s in range(ncols // 256):
        g = dict(outs=[(w[:, col0 + s * 256:col0 + (s + 1) * 256], 0)], units=[[(0, 0)], [(0, 128)]], kind=kind, dst=dst,
                 row0=row0 + s * 256)
        g.update(kw)
        gs.append(g)
    return gs


def final_norm_phase(cx, G, x_ap, gidx, y_ap):
    NS = 256
    xs32 = [cx.sb([128, 16, NS], F32, "xs32") for _ in range(2)]
    sq = cx.sb([128, 16, NS], BF16, "sq")
    psn = cx.ps([128, 512], F32, "psn")
    rs0 = cx.sb([128, NS], F32, "rs0")
    rstd = cx.sb([128, NS], F32, "rstd")
    yo = [cx.sb([128, 16, NS], F32, "yo") for _ in range(2)]
    for s in range(T // NS):
        ts0 = s * NS
        x3 = xs32[s % 2]
        y3 = yo[s % 2]
        cx.dma("sp", x3[:], x_ap.rearrange("(c p) t -> p c t", p=128)[:, :, ts0:ts0 + NS], x3, writes=[x3])
        cx.op("act", lambda a: a.activation(out=sq[:], in_=x3[:], func=AF.Square), reads=[x3], writes=[sq])
        for c in range(16):
            cx.op("pe", lambda p, c=c: p.matmul(psn[:, 0:NS], lhsT=G["ones"][:], rhs=sq[:, c, :], start=(c == 0), stop=(c == 15)),
                  reads=[sq, G["onesb"]], writes=[psn])
        cx.op("act", lambda a: a.activation(out=rs0[:], in_=psn[:, 0:NS], func=AF.Sqrt, scale=1.0 / D, bias=G["eps"][:, 0:1]),
              reads=[psn, G["cstb"]], writes=[rs0])
        cx.op("dve", lambda v: v.reciprocal(out=rstd[:], in_=rs0[:]), reads=[rs0], writes=[rstd])
        for c in range(16):
            cx.op("dve", lambda v, c=c: v.scalar_tensor_tensor(
                out=y3[:, c, :], in0=x3[:, c, :], scalar=G["cst"][:, C_GAIN + gidx * 16 + c:C_GAIN + gidx * 16 + c + 1],
                in1=rstd[:], op0=ALU.mult, op1=ALU.mult), reads=[x3, rstd, G["cstb"]], writes=[y3])
        cx.dma("sp", y_ap.rearrange("(c p) t -> p c t", p=128)[:, :, ts0:ts0 + NS], y3[:], y3, reads=[y3])


def retention_pass(cx, G, S, first):
    nc = cx.nc
    BT = 128
    cst = G["cst"]
    qB = [cx.sb([128, 16, BT], BF16, "rq") for _ in range(2)]
    kB = [cx.sb([128, 16, BT], BF16, "rk") for _ in range(2)]
    vB = [cx.sb([128, 32, BT], BF16, "rv") for _ in range(2)]
    sgB = [cx.sb([128, 32, BT], BF16, "rsg") for _ in range(2)]
    ostB = [cx.sb([128, 32, BT], BF16, "rost") for _ in range(2)]
    Sf, Sb = S
    psA = cx.ps([128, 512], F32, "psA")
    psO2 = [cx.ps([128, 512], F32, "psO") for _ in range(2)]
    psOT = cx.ps([128, 1024], BF16, "psOT")
    psT = [cx.ps([128, 1024], BF16, "psT") for _ in range(2)]
    psS = [cx.ps([128, 512], F32, "psS") for _ in range(2)]
    ident = G["ident"]
    assert not first
    atm = [cx.sb([128, 128], BF16, "atm") for _ in range(3)]
    qdec = [cx.sb([128, 2, 128], BF16, "qdec") for _ in range(3)]
    on = [cx.sb([128, 512], BF16, "on") for _ in range(4)]
    st6 = [cx.sb([128, 6], F32, "st6") for _ in range(4)]
    mv = [cx.sb([128, 2], F32, "mv") for _ in range(4)]
    sd = [cx.sb([128, 1], F32, "sd") for _ in range(4)]
    rs = [cx.sb([128, 1], F32, "rs") for _ in range(4)]
    nb = [cx.sb([128, 1], F32, "nb") for _ in range(4)]
    kdec = [cx.sb([128, 256], BF16, "kdec") for _ in range(3)]
    vtok = [cx.sb([128, 512], BF16, "vtok") for _ in range(3)]
    NBLK = T // BT

    def load_block(lb):
        tsl = slice(lb * BT, (lb + 1) * BT)
        q, k, v, sg = qB[lb % 2], kB[lb % 2], vB[lb % 2], sgB[lb % 2]
        cx.dma("sp", k[:], G["krT"].rearrange("(c p) t -> p c t", p=128)[:, :, tsl], k, writes=[k])
        cx.dma("sp", q[:], G["qrT"].rearrange("(c p) t -> p c t", p=128)[:, :, tsl], q, writes=[q])
        cx.dma("sp", v[:, 0:16, :], G["vrT"].rearrange("(c p) t -> p c t", p=128)[:, 0:16, tsl], v, writes=[v])
        cx.dma("sp", v[:, 16:32, :], G["vrT"].rearrange("(c p) t -> p c t", p=128)[:, 16:32, tsl], v, writes=[v])
        cx.dma("sp", sg[:, 0:16, :], G["sgT"].rearrange("(c p) t -> p c t", p=128)[:, 0:16, tsl], sg, writes=[sg])
        cx.dma("sp", sg[:, 16:32, :], G["sgT"].rearrange("(c p) t -> p c t", p=128)[:, 16:32, tsl], sg, writes=[sg])

    def stA(i):
        blk, h = divmod(i, 8)
        if i == 0:
            load_block(0)
        if h == 4 and blk + 1 < NBLK:
            load_block(blk + 1)
        q, k, v = qB[blk % 2], kB[blk % 2], vB[blk % 2]
        pT = psT[i % 2]
        for dc in range(2):
            cx.op("pe", lambda p, dc=dc: p.transpose(out=pT[:, dc * 128:(dc + 1) * 128], in_=k[:, 2 * h + dc, :], identity=ident[:]),
                  reads=[k, G["identb"]], writes=[pT])
        for ec in range(4):
            cx.op("pe", lambda p, ec=ec: p.transpose(out=pT[:, 256 + ec * 128:256 + (ec + 1) * 128], in_=v[:, 4 * h + ec, :], identity=ident[:]),
                  reads=[v, G["identb"]], writes=[pT])
        for dc in range(2):
            cx.op("pe", lambda p, dc=dc: p.matmul(psA[:, 0:128], lhsT=k[:, 2 * h + dc, :], rhs=q[:, 2 * h + dc, :], start=(dc == 0), stop=(dc == 1)),
                  reads=[k, q], writes=[psA])

    def stB(i):
        blk, h = divmod(i, 8)
        q = qB[blk % 2]
        pT, kd_, vt_, at_, qd_ = psT[i % 2], kdec[i % 3], vtok[i % 3], atm[i % 3], qdec[i % 3]
        cx.op("act", lambda a: a.activation(out=kd_[:], in_=pT[:, 0:256], func=AF.Copy, scale=cst[:, C_RDK + h:C_RDK + h + 1]),
              reads=[pT, G["cstb"]], writes=[kd_])
        cx.op("act", lambda a: a.activation(out=vt_[:], in_=pT[:, 256:768], func=AF.Copy), reads=[pT], writes=[vt_])
        cx.op("dve", lambda e: e.tensor_tensor(out=at_[:], in0=psA[:, 0:128], in1=cst[:, C_RMASK + h * 128:C_RMASK + (h + 1) * 128], op=ALU.mult),
              reads=[psA, G["cstb"]], writes=[at_])
        for dc in range(2):
            cx.op("pool", lambda e, dc=dc: e.tensor_tensor(out=qd_[:, dc, :], in0=q[:, 2 * h + dc, :], in1=cst[:, C_RDQ + h * 128:C_RDQ + (h + 1) * 128], op=ALU.mult),
                  reads=[q, G["cstb"]], writes=[qd_])

    def stC(i):
        blk, h = divmod(i, 8)
        kd_, vt_, at_, qd_ = kdec[i % 3], vtok[i % 3], atm[i % 3], qdec[i % 3]
        psO = psO2[i % 2]
        for dc in range(2):
            cx.op("pe", lambda p, dc=dc: p.matmul(psO[:], lhsT=qd_[:, dc, :], rhs=Sb[:, h, dc, :], start=(dc == 0), stop=False),
                  reads=[qd_, Sb], writes=[psO])
        cx.op("pe", lambda p: p.matmul(psO[:], lhsT=at_[:], rhs=vt_[:], start=False, stop=True), reads=[at_, vt_], writes=[psO])
        for dc in range(2):
            cx.op("pe", lambda p, dc=dc: p.matmul(psS[dc][:], lhsT=kd_[:, dc * 128:(dc + 1) * 128], rhs=vt_[:], start=True, stop=True),
                  reads=[kd_, vt_], writes=[psS[dc]])

    def stD(i):
        blk, h = divmod(i, 8)
        psO = psO2[i % 2]
        on_ = on[i % 4]
        cd = GAMMA[h] ** 128
        for dc in range(2):
            cx.op("dve", lambda e, dc=dc: e.scalar_tensor_tensor(out=Sf[:, h, dc, :], in0=Sf[:, h, dc, :], scalar=cd, in1=psS[dc][:], op0=ALU.mult, op1=ALU.add),
                  reads=[psS[dc]], writes=[Sf])
        for dc in range(2):
            cx.op("act", lambda a, dc=dc: a.activation(out=Sb[:, h, dc, :], in_=Sf[:, h, dc, :], func=AF.Copy), reads=[Sf], writes=[Sb])
        s6, mv_, sd_, rs_, nb_ = st6[i % 4], mv[i % 4], sd[i % 4], rs[i % 4], nb[i % 4]
        cx.op("dve", lambda e: e.bn_stats(out=s6[:], in_=psO[:]), reads=[psO], writes=[s6])
        cx.op("dve", lambda e: e.bn_aggr(out=mv_[:], in_=s6[:]), reads=[s6], writes=[mv_])
        cx.op("act", lambda a: a.activation(out=sd_[:], in_=mv_[:, 1:2], func=AF.Sqrt, bias=G["eps"][:, 1:2]), reads=[mv_, G["cstb"]], writes=[sd_])
        cx.op("dve", lambda e: e.reciprocal(out=rs_[:], in_=sd_[:]), reads=[sd_], writes=[rs_])
        cx.op("dve", lambda e: e.scalar_tensor_tensor(out=nb_[:], in0=mv_[:, 0:1], scalar=-1.0, in1=rs_[:], op0=ALU.mult, op1=ALU.mult),
              reads=[mv_, rs_], writes=[nb_])
        cx.op("act", lambda a: a.activation(out=on_[:], in_=psO[:], func=AF.Identity, scale=rs_[:, 0:1], bias=nb_[:, 0:1]),
              reads=[psO, rs_, nb_], writes=[on_])

    def stE(i):
        on_ = on[i % 4]
        for ec in range(4):
            cx.op("pe", lambda p, ec=ec: p.transpose(out=psOT[:, ec * 128:(ec + 1) * 128], in_=on_[:, ec * 128:(ec + 1) * 128], identity=ident[:]),
                  reads=[on_, G["identb"]], writes=[psOT])

    def stF(i):
        blk, h = divmod(i, 8)
        sg, ost = sgB[blk % 2], ostB[blk % 2]
        cx.op("dve", lambda e: e.tensor_tensor(out=ost[:, 4 * h:4 * h + 4, :], in0=psOT[:, 0:512].rearrange("p (e t) -> p e t", e=4),
                                               in1=sg[:, 4 * h:4 * h + 4, :], op=ALU.mult), reads=[psOT, sg], writes=[ost])
        if h == 7:
            tsl = slice(blk * BT, (blk + 1) * BT)
            cx.dma("sp", G["orT"].rearrange("(c p) t -> p c t", p=128)[:, 0:16, tsl], ost[:, 0:16, :], ost, reads=[ost])
            cx.dma("sp", G["orT"].rearrange("(c p) t -> p c t", p=128)[:, 16:32, tsl], ost[:, 16:32, :], ost, reads=[ost])

    skew(8 * NBLK, [(stA, 0), (stB, 0), (stC, 1), (stD, 1), (stE, 3), (stF, 3)])
    return
    if first:
        for i in range(4):
            cx.dma("sp", G["exS_in"][i].rearrange("(h c p) e -> p h c e", p=128, c=2), Sf[:, 2 * i:2 * i + 2], Sf, reads=[Sf])


def retention_local_state(cx, G):
    BT = 256
    cst = G["cst"]
    ident = G["ident"]
    kb = [cx.sb([128, 4, BT], BF16, "lk") for _ in range(3)]
    vb = [cx.sb([128, 8, BT], BF16, "lv") for _ in range(3)]
    kdec = [cx.sb([128, 256], BF16, "lkd") for _ in range(3)]
    vtok = [cx.sb([128, 512], BF16, "lvt") for _ in range(3)]
    psT = [cx.ps([128, 1024], BF16, "lpT") for _ in range(3)]
    acc = [[cx.ps([128, 512], F32, "lacc") for _ in range(2)] for _ in range(2)]
    stg = cx.sb([128, 2, 2, 512], F32, "lstg")
    for hp in range(4):
        NI = (T // BT) * (BT // 128) * 2

        def dec(i):
            blk, rem = divmod(i, (BT // 128) * 2)
            ch, hh = divmod(rem, 2)
            return blk, ch, hh

        def stA(i):
            blk, ch, hh = dec(i)
            k, v = kb[blk % 3], vb[blk % 3]
            if ch == 0 and hh == 0:
                for lb in ([0, 1, 2] if blk == 0 else [blk + 2]):
                    if lb < T // BT:
                        tsl = slice(lb * BT, (lb + 1) * BT)
                        k2, v2 = kb[lb % 3], vb[lb % 3]
                        cx.dma("sp", k2[:], G["krT"][hp * 512:(hp + 1) * 512, :].rearrange("(c p) t -> p c t", p=128)[:, :, tsl], k2, writes=[k2])
                        cx.dma("sp", v2[:], G["vrT"][hp * 1024:(hp + 1) * 1024, :].rearrange("(c p) t -> p c t", p=128)[:, :, tsl], v2, writes=[v2])
            cs = slice(ch * 128, (ch + 1) * 128)
            pT = psT[i % 3]
            for dc in range(2):
                cx.op("pe", lambda p, dc=dc: p.transpose(out=pT[:, dc * 128:(dc + 1) * 128], in_=k[:, 2 * hh + dc, cs], identity=ident[:]),
                      reads=[k, G["identb"]], writes=[pT])
            for ec in range(4):
                cx.op("pe", lambda p, ec=ec: p.transpose(out=pT[:, 256 + ec * 128:256 + (ec + 1) * 128], in_=v[:, 4 * hh + ec, cs], identity=ident[:]),
                      reads=[v, G["identb"]], writes=[pT])

        def stB(i):
            blk, ch, hh = dec(i)
            n = blk * (BT // 128) + ch
            h = 2 * hp + hh
            pT, kd_, vt_ = psT[i % 3], kdec[i % 3], vtok[i % 3]
            col = C_DKABS + n * 8 + h
            cx.op("dve", lambda e: e.tensor_scalar(out=kd_[:], in0=pT[:, 0:256], scalar1=cst[:, col:col + 1], scalar2=None, op0=ALU.mult),
                  reads=[pT, G["cstb"]], writes=[kd_])
            cx.op("act", lambda a: a.activation(out=vt_[:], in_=pT[:, 256:768], func=AF.Copy), reads=[pT], writes=[vt_])

        def stC(i):
            blk, ch, hh = dec(i)
            n = blk * (BT // 128) + ch
            kd_, vt_ = kdec[i % 3], vtok[i % 3]
            for dc in range(2):
                cx.op("pe", lambda p, dc=dc: p.matmul(acc[hh][dc][:], lhsT=kd_[:, dc * 128:(dc + 1) * 128], rhs=vt_[:], start=(n == 0), stop=(n == 31)),
                      reads=[kd_, vt_], writes=[acc[hh][dc]])

        skew(NI, [(stA, 0), (stB, 0), (stC, 1)])
        for hh in range(2):
            for dc in range(2):
                if dc == 0:
                    cx.op("act", lambda a: a.activation(out=stg[:, hh, dc, :], in_=acc[hh][dc][:], func=AF.Copy), reads=[acc[hh][dc]], writes=[stg])
                else:
                    cx.op("dve", lambda e: e.tensor_copy(out=stg[:, hh, dc, :], in_=acc[hh][dc][:]), reads=[acc[hh][dc]], writes=[stg])
        cx.dma("sp", G["exS_in"][hp].rearrange("(h c p) e -> p h c e", p=128, c=2), stg[:], stg, reads=[stg])


def state_init(cx, G, S, zero):
    Sf, Sb = S
    cst = G["cst"]
    if zero:
        cx.op("pool", lambda e: e.memset(Sf[:], 0.0), writes=[Sf])
        cx.op("pool", lambda e: e.memset(Sb[:], 0.0), writes=[Sb])
        return
    gt = [cx.sb([128, 8, 2, 512], F32, "gst") for _ in range(1)]
    g0 = gt[0]
    for r in range(3):
        for i in range(4):
            src = G["exS_out"][i][r * 512:(r + 1) * 512, :].rearrange("(h c p) e -> p h c e", p=128, c=2)
            cx.dma("sp", g0[:, 2 * i:2 * i + 2], src, g0, writes=[g0])
        for h in range(8):
            col = cst[:, C_COEF + r * 8 + h:C_COEF + r * 8 + h + 1]
            if r == 0:
                cx.op("dve", lambda e, h=h, col=col: e.tensor_scalar(out=Sf[:, h], in0=g0[:, h], scalar1=col, scalar2=None, op0=ALU.mult),
                      reads=[g0, G["cstb"]], writes=[Sf])
            else:
                cx.op("dve", lambda e, h=h, col=col: e.scalar_tensor_tensor(out=Sf[:, h], in0=g0[:, h], scalar=col, in1=Sf[:, h], op0=ALU.mult, op1=ALU.add),
                      reads=[g0, G["cstb"]], writes=[Sf])
    cx.op("pool", lambda e: e.tensor_copy(out=Sb[:], in_=Sf[:]), reads=[Sf], writes=[Sb])


ATT_GROUPS = ((128, 1), (512, 4), (2048, 16))
XOVR = None
USE_POW = False


def attention_phase(cx, G):
    cst = G["cst"]
    ident = G["ident"]
    HMAX = 2048
    qh = cx.sb([128, T], BF16, "aq")
    kh = cx.sb([128, HMAX + T], BF16, "ak")
    vh = cx.sb([128, HMAX + T], BF16, "av")
    qd = cx.sb([128, T], BF16, "aqd")
    kd = cx.sb([128, HMAX + T], BF16, "akd")
    vd = cx.sb([128, HMAX + T], BF16, "avd")
    hk = [cx.sb([128, HMAX], BF16, "hk") for _ in range(3)]
    hv = [cx.sb([128, HMAX], BF16, "hv") for _ in range(3)]
    NSL = 6
    s_sb = [cx.sb([128, 256], F32, "s") for _ in range(NSL)]
    p_sb = [cx.sb([128, 256], BF16, "p") for _ in range(NSL)]
    pT_sb = [cx.sb([128, 256], BF16, "pT") for _ in range(NSL)]
    NVT = 10
    vt = [cx.sb([128, 128], BF16, "vt") for _ in range(NVT)]
    ost = [cx.sb([128, 128], F32, "ao") for _ in range(NSL)]
    mlst = [cx.sb([128, 2], F32, "ml") for _ in range(NSL)]
    negm = [cx.sb([128, 1], F32, "negm") for _ in range(NSL)]
    bank = [cx.ps([128, 512], F32, "abk") for _ in range(NSL)]
    psV = [cx.ps([128, 1024], BF16, "apV") for _ in range(2)]
    scale = 128.0 ** -0.5
    it = 0
    vti = 0
    for g, (window, d) in enumerate(ATT_GROUPS):
        HALO = 128 * d
        L = T // d
        LK = L + 128
        nbl = L // 128
        for hh in range(4):
            H = 4 * g + hh
            rows = slice(H * 128, (H + 1) * 128)
            cx.dma("sp", qh[:], G["qaT"][rows, :], qh, writes=[qh])
            cx.dma("sp", kh[:, HALO:HALO + T], G["kaT"][rows, :], kh, writes=[kh])
            cx.dma("sp", vh[:, HALO:HALO + T], G["vaT"][rows, :], vh, writes=[vh])
            for r in range(3):
                if g == 2:
                    ksrc = G["exT2o"][hh // 2][r * 256 + (hh % 2) * 128:r * 256 + (hh % 2) * 128 + 128, :]
                    vsrc = G["exT2o"][2 + hh // 2][r * 256 + (hh % 2) * 128:r * 256 + (hh % 2) * 128 + 128, :]
                elif g == 1:
                    ksrc = G["exT1o"][r * 1024 + hh * 128:r * 1024 + hh * 128 + 128, :]
                    vsrc = G["exT1o"][r * 1024 + 512 + hh * 128:r * 1024 + 512 + hh * 128 + 128, :]
                else:
                    ksrc = G["exT0o"][r * 1024 + hh * 128:r * 1024 + hh * 128 + 128, :]
                    vsrc = G["exT0o"][r * 1024 + 512 + hh * 128:r * 1024 + 512 + hh * 128 + 128, :]
                cx.dma("sp", hk[r][:, 0:HALO], ksrc, hk[r], writes=[hk[r]])
                cx.dma("sp", hv[r][:, 0:HALO], vsrc, hv[r], writes=[hv[r]])
            for (dst, srcs) in ((kh, hk), (vh, hv)):
                cx.op("dve", lambda e, dst=dst, srcs=srcs: e.tensor_scalar(out=dst[:, 0:HALO], in0=srcs[0][:, 0:HALO], scalar1=cst[:, C_SEL:C_SEL + 1], scalar2=None, op0=ALU.mult),
                      reads=[srcs[0], G["cstb"]], writes=[dst])
                for r in (1, 2):
                    cx.op("dve", lambda e, dst=dst, srcs=srcs, r=r: e.scalar_tensor_tensor(out=dst[:, 0:HALO], in0=srcs[r][:, 0:HALO], scalar=cst[:, C_SEL + r:C_SEL + r + 1],
                                                                                          in1=dst[:, 0:HALO], op0=ALU.mult, op1=ALU.add),
                          reads=[srcs[r], G["cstb"]], writes=[dst])
            if d > 1:
                cx.op("dve", lambda e: e.tensor_copy(out=qd[:, 0:T].rearrange("p (r l) -> p r l", r=d), in_=qh[:, 0:T].rearrange("p (l r) -> p r l", r=d)),
                      reads=[qh], writes=[qd])
                cx.op("act", lambda e: e.activation(out=kd[:, 0:HALO + T].rearrange("p (r l) -> p r l", r=d), in_=kh[:, 0:HALO + T].rearrange("p (l r) -> p r l", r=d), func=AF.Copy),
                      reads=[kh], writes=[kd])
                cx.op("dve", lambda e: e.tensor_copy(out=vd[:, 0:HALO + T].rearrange("p (r l) -> p r l", r=d), in_=vh[:, 0:HALO + T].rearrange("p (l r) -> p r l", r=d)),
                      reads=[vh], writes=[vd])
                Q, K, V = qd, kd, vd
            else:
                Q, K, V = qh, kh, vh
            items = [(r, b) for r in range(d) for b in range(nbl)]
            vtile = {}

            def vtrans(r, bi, Vt=None):
                nonlocal vti
                V_ = Vt
                tl = vt[vti % NVT]
                pv = psV[vti % 2]
                vti += 1
                kb = r * LK
                cx.op("pe", lambda p: p.transpose(out=pv[:, 0:128], in_=V_[:, kb + bi * 128:kb + (bi + 1) * 128], identity=ident[:]),
                      reads=[V_, G["identb"]], writes=[pv])
                cx.op("act", lambda a: a.activation(out=tl[:], in_=pv[:, 0:128], func=AF.Copy), reads=[pv], writes=[tl])
                vtile[(r, bi)] = tl

            def stA(i, Q=Q, K=K, V=V):
                r, b = items[i]
                if b == 0:
                    vtrans(r, 0, V)
                vtrans(r, b + 1, V)
                bk_ = bank[i % NSL]
                qb, kb = r * L, r * LK
                cx.op("pe", lambda p: p.matmul(bk_[:, 0:256], lhsT=Q[:, qb + b * 128:qb + (b + 1) * 128], rhs=K[:, kb + b * 128:kb + b * 128 + 256], start=True, stop=True),
                      reads=[Q, K], writes=[bk_])

            def stB(i):
                r, b = items[i]
                sl_ = i % NSL
                bk_, s_, p_, ml_, nm_ = bank[sl_], s_sb[sl_], p_sb[sl_], mlst[sl_], negm[sl_]
                mcol = C_AMASK + (256 if b == 0 else 0)
                cx.op("dve", lambda e: e.scalar_tensor_tensor(out=s_[:], in0=bk_[:, 0:256], scalar=scale, in1=cst[:, mcol:mcol + 256], op0=ALU.mult, op1=ALU.add),
                      reads=[bk_, G["cstb"]], writes=[s_])
                cx.op("dve", lambda e: e.reduce_max(out=ml_[:, 0:1], in_=s_[:], axis=AX.X), reads=[s_], writes=[ml_])
                cx.op("dve", lambda e: e.tensor_scalar(out=nm_[:], in0=ml_[:, 0:1], scalar1=-1.0, scalar2=None, op0=ALU.mult), reads=[ml_], writes=[nm_])
                cx.op("act", lambda a: a.activation(out=p_[:], in_=s_[:], func=AF.Exp, bias=nm_[:, 0:1], accum_out=ml_[:, 1:2]),
                      reads=[s_, nm_], writes=[p_, ml_])

            def stC(i):
                sl_ = i % NSL
                bk_, p_, pT_ = bank[sl_], p_sb[sl_], pT_sb[sl_]
                pPTv = bk_[:, 256:384].bitcast(BF16)
                for half in range(2):
                    cx.op("pe", lambda p, half=half: p.transpose(out=pPTv[:, half * 128:(half + 1) * 128], in_=p_[:, half * 128:(half + 1) * 128], identity=ident[:]),
                          reads=[p_, G["identb"]], writes=[bk_])
                cx.op("dve", lambda e: e.tensor_copy(out=pT_[:], in_=pPTv[:, 0:256]), reads=[bk_], writes=[pT_])

            def stD(i, g=g, hh=hh, d=d):
                r, b = items[i]
                sl_ = i % NSL
                bk_, pT_, o_, ml_ = bank[sl_], pT_sb[sl_], ost[sl_], mlst[sl_]
                vprev, vcur = vtile[(r, b)], vtile[(r, b + 1)]
                cx.op("pe", lambda p: p.matmul(bk_[:, 384:512], lhsT=pT_[:, 0:128], rhs=vprev[:], start=True, stop=False), reads=[pT_, vprev], writes=[bk_])
                cx.op("pe", lambda p: p.matmul(bk_[:, 384:512], lhsT=pT_[:, 128:256], rhs=vcur[:], start=False, stop=True), reads=[pT_, vcur], writes=[bk_])
                cx.op("act", lambda a: a.activation(out=o_[:], in_=bk_[:, 384:512], func=AF.Copy), reads=[bk_], writes=[o_])
                orows = G["oacc"][g].rearrange("(l r) f -> r l f", r=d)[r, b * 128:(b + 1) * 128, hh * 128:(hh + 1) * 128]
                cx.dma("sp", orows, o_[:], o_, reads=[o_])
                mrows = G["mlacc"].rearrange("(l r) f -> r l f", r=d)[r, b * 128:(b + 1) * 128, (g * 4 + hh) * 2:(g * 4 + hh) * 2 + 2]
                cx.dma("sp", mrows, ml_[:], ml_, reads=[ml_])

            skew(len(items), [(stA, 0), (stB, 0), (stC, 2), (stD, 4)])


def attn_merge_phase(cx, G):
    ident = G["ident"]
    NT_ = T // 128
    o3 = [cx.sb([128, 3, 512], F32, "mo") for _ in range(2)]
    ml = [cx.sb([128, 3, 4, 2], F32, "mml") for _ in range(2)]
    M = cx.sb([128, 4], F32, "mM")
    df = cx.sb([128, 3, 4], F32, "mdf")
    w = cx.sb([128, 3, 4], F32, "mw")
    wl = cx.sb([128, 3, 4], F32, "mwl")
    den = cx.sb([128, 4], F32, "mden")
    rden = cx.sb([128, 4], F32, "mrden")
    wn = cx.sb([128, 3, 4], F32, "mwn")
    acc = cx.sb([128, 512], F32, "macc")
    ob = [cx.sb([128, 512], BF16, "mob") for _ in range(2)]
    stg = [cx.sb([128, 4, 512], BF16, "mstg") for _ in range(2)]
    psT = [cx.ps([128, 1024], BF16, "mpT") for _ in range(2)]
    for tt in range(NT_):
        i2 = tt % 2
        o_, ml_, ob_, pT = o3[i2], ml[i2], ob[i2], psT[i2]
        sg = stg[(tt // 4) % 2]
        rows = slice(tt * 128, (tt + 1) * 128)
        for g in range(3):
            cx.dma("sp", o_[:, g, :], G["oacc"][g][rows, :], o_, writes=[o_])
        cx.dma("sp", ml_[:], G["mlacc"][rows, :].rearrange("t (g h k) -> t g h k", g=3, h=4), ml_, writes=[ml_])
        cx.op("dve", lambda e: e.tensor_tensor(out=M[:], in0=ml_[:, 0, :, 0], in1=ml_[:, 1, :, 0], op=ALU.max), reads=[ml_], writes=[M])
        cx.op("dve", lambda e: e.tensor_tensor(out=M[:], in0=M[:], in1=ml_[:, 2, :, 0], op=ALU.max), reads=[ml_], writes=[M])
        for g in range(3):
            cx.op("dve", lambda e, g=g: e.tensor_tensor(out=df[:, g, :], in0=ml_[:, g, :, 0], in1=M[:], op=ALU.subtract), reads=[ml_, M], writes=[df])
        cx.op("act", lambda a: a.activation(out=w[:], in_=df[:], func=AF.Exp), reads=[df], writes=[w])
        cx.op("dve", lambda e: e.tensor_tensor(out=wl[:], in0=w[:], in1=ml_[:, :, :, 1], op=ALU.mult), reads=[w, ml_], writes=[wl])
        cx.op("dve", lambda e: e.tensor_tensor(out=den[:], in0=wl[:, 0, :], in1=wl[:, 1, :], op=ALU.add), reads=[wl], writes=[den])
        cx.op("dve", lambda e: e.tensor_tensor(out=den[:], in0=den[:], in1=wl[:, 2, :], op=ALU.add), reads=[wl], writes=[den])
        cx.op("dve", lambda e: e.reciprocal(out=rden[:], in_=den[:]), reads=[den], writes=[rden])
        for g in range(3):
            cx.op("dve", lambda e, g=g: e.tensor_tensor(out=wn[:, g, :], in0=w[:, g, :], in1=rden[:], op=ALU.mult), reads=[w, rden], writes=[wn])
        for hh in range(4):
            hs = slice(hh * 128, (hh + 1) * 128)
            cx.op("dve", lambda e, hh=hh, hs=hs: e.tensor_scalar(out=acc[:, hs], in0=o_[:, 0, hs], scalar1=wn[:, 0, hh:hh + 1], scalar2=None, op0=ALU.mult),
                  reads=[o_, wn], writes=[acc])
            cx.op("dve", lambda e, hh=hh, hs=hs: e.scalar_tensor_tensor(out=acc[:, hs], in0=o_[:, 1, hs], scalar=wn[:, 1, hh:hh + 1], in1=acc[:, hs], op0=ALU.mult, op1=ALU.add),
                  reads=[o_, wn], writes=[acc])
            cx.op("dve", lambda e, hh=hh, hs=hs: e.scalar_tensor_tensor(out=ob_[:, hs], in0=o_[:, 2, hs], scalar=wn[:, 2, hh:hh + 1], in1=acc[:, hs], op0=ALU.mult, op1=ALU.add),
                  reads=[o_, wn, acc], writes=[ob_])
        for hh in range(4):
            cx.op("pe", lambda p, hh=hh: p.transpose(out=pT[:, hh * 128:(hh + 1) * 128], in_=ob_[:, hh * 128:(hh + 1) * 128], identity=ident[:]),
                  reads=[ob_, G["identb"]], writes=[pT])
        q4 = tt % 4
        cx.op("act", lambda a: a.activation(out=sg[:, :, q4 * 128:(q4 + 1) * 128], in_=pT[:, 0:512].rearrange("p (h t) -> p h t", h=4), func=AF.Copy),
              reads=[pT], writes=[sg])
        if q4 == 3:
            t0 = (tt - 3) * 128
            cx.dma("sp", G["oaT"].rearrange("(c p) t -> p c t", p=128)[:, :, t0:t0 + 512], sg[:], sg, reads=[sg])


def build():
    nc = bass.Bass("TRN2", target_bir_lowering=False)
    cx = Cx(nc)
    G = {}

    def din(name, shape, dt=F32):
        return nc.dram_tensor(name, list(shape), dt, kind="ExternalInput").ap()

    DBGT = {}

    def dscr(name, shape, dt):
        a = nc.dram_tensor(name, list(shape), dt, kind="Internal").ap()
        if name in DBG:
            DBGT[name] = (a, nc.dram_tensor("dbg_" + name, [shape[0], 256], dt, kind="ExternalOutput").ap())
        return a

    def finish():
        db = Buf("dbgcopy")
        for name, (a, o) in DBGT.items():
            cx.dma("sp", o, a[:, 0:256], db)
        cx.barrier()
        return nc

    xT = din("xT", [D, T])
    WSHAPE = {"ffn1_w_gate": [D, DFF], "ffn1_w_up": [D, DFF], "ffn1_w_down": [DFF, D], "w_in": [D, INC],
              "w_proj_attn": [512, D], "w_proj_ret": [4096, D], "w_out": [D, D],
              "ffn2_w_gate": [D, DFF], "ffn2_w_up": [D, DFF], "ffn2_w_down": [DFF, D]}
    USED_W.clear()

    class _WL:
        def __init__(self, nm):
            self.nm = nm

        def __getitem__(self, l):
            key = "%s_%d" % (self.nm, l)
            if key not in USED_W:
                USED_W[key] = (self.nm, l, din(key, WSHAPE[self.nm]))
            return USED_W[key][2]

    W = {nm: _WL(nm) for nm in WSHAPE}
    cst_d = din("cst", [128, CW])
    G["ropeA"] = din("ropeA", [2, 128, T])
    G["ropeR"] = din("ropeR", [2, 128, T])
    yT = nc.dram_tensor("yT", [D, T], F32, kind="ExternalOutput").ap()

    XA = dscr("XA", [D, T], F32)
    XB = dscr("XB", [D, T], F32)
    uT = dscr("uT", [DFF, T], BF16)
    for nm, rows in (("qaT", 1536), ("kaT", 1536), ("vaT", 1536), ("qrT", 2048), ("krT", 2048), ("vrT", 4096), ("sgT", 4096),
                     ("gaT", 2048), ("gbT", 2048), ("oaT", 512), ("orT", 4096), ("mT", 2048)):
        G[nm] = dscr(nm, [rows, T], BF16)
    G["oacc"] = [dscr("oacc%d" % g, [T, 512], F32) for g in range(3)]
    G["mlacc"] = dscr("mlacc", [T, 24], F32)
    EX = []
    G["exS_in"], G["exS_out"] = [], []
    for i in range(4):
        a = nc.dram_tensor("exS_in%d" % i, [512, 512], F32)
        b = nc.dram_tensor("exS_out%d" % i, [4 * 512, 512], F32)
        EX.append((a, b))
        G["exS_in"].append(a.ap())
        G["exS_out"].append(b.ap())
    G["exT2"], G["exT2o"] = [], []
    for j in range(4):
        a = nc.dram_tensor("exT2_%d" % j, [256, 2048], BF16)
        b = nc.dram_tensor("exT2o_%d" % j, [4 * 256, 2048], BF16)
        EX.append((a, b))
        G["exT2"].append(a.ap())
        G["exT2o"].append(b.ap())
    a = nc.dram_tensor("exT1", [1024, 512], BF16)
    b = nc.dram_tensor("exT1o", [4 * 1024, 512], BF16)
    EX.append((a, b))
    G["exT1"], G["exT1o"] = a.ap(), b.ap()
    a = nc.dram_tensor("exT0", [1024, 128], BF16)
    b = nc.dram_tensor("exT0o", [4 * 1024, 128], BF16)
    EX.append((a, b))
    G["exT0"], G["exT0o"] = a.ap(), b.ap()
    cc_sems = [nc.alloc_semaphore("cc%d" % i) for i in range(len(EX) * DEPTH)]
    ccn = [0]

    cst = nc.alloc_sbuf_tensor("cst_sb", [128, CW], F32)
    identb = nc.alloc_sbuf_tensor("identb", [128, 128], BF16)
    ones = nc.alloc_sbuf_tensor("ones", [128, 128], BF16)
    eps = nc.alloc_sbuf_tensor("eps", [128, 2], F32)
    G["cst"], G["ident"], G["ones"], G["eps"] = cst, identb, ones, eps
    G["cstb"], G["identb"], G["onesb"] = Buf("cst"), Buf("ident"), Buf("ones")
    cx.dma("sp", cst[:], cst_d, G["cstb"], writes=[G["cstb"]])
    cx.op("dve", lambda e: e.tensor_copy(out=identb[:], in_=cst[:, C_ID:C_ID + 128]), reads=[G["cstb"]], writes=[G["identb"]])
    cx.op("pool", lambda e: e.memset(ones[:], 1.0), writes=[G["onesb"]])
    cx.op("pool", lambda e: e.memset(eps[:, 0:1], RMS_EPS), writes=[G["cstb"]])
    cx.op("pool", lambda e: e.memset(eps[:, 1:2], GN_EPS), writes=[G["cstb"]])
    cx.barrier()

    def ffn(l, which, xi, xo_, gidx):
        wg, wu, wd = W["ffn%d_w_gate" % which][l], W["ffn%d_w_up" % which][l], W["ffn%d_w_down" % which][l]
        groups = []
        for s in range(DFF // 256):
            groups.append(dict(outs=[(wg[:, s * 256:(s + 1) * 256], 0), (wu[:, s * 256:(s + 1) * 256], 0)],
                               units=[[(0, 0), (1, 0)], [(0, 128), (1, 128)]], kind="swiglu", dst=uT, row0=s * 256))
        with cx.phase("ffn_up"):
            linear_phase(cx, G, 2048, 256, groups, norm=(xi, gidx), need=("stgA", "tmp"))
        if STOP == "ffn_up":
            return True
        with cx.phase("ffn_down"):
            linear_phase(cx, G, 1024, 0, simple_groups(wd, D, "res", xo_, xin=xi, scale=0.5), xsrc=[uT], need=("res",))
        return False

    def exchange(part):
        sems = []
        for (i_, o_) in part:
            ins = nc.gpsimd.collective_compute("AllGather", ALU.bypass, replica_groups=[[0, 1, 2, 3], [4, 5, 6, 7]],
                                               ins=[i_.ap().opt()], outs=[o_.ap().opt()])
            sem = cc_sems[ccn[0]]
            ccn[0] += 1
            ins.then_inc(sem)
            nc.gpsimd.wait_ge(sem, 1)
            sems.append(sem)
        return sems

    def wait_cc(sems):
        for sem in sems:
            for e in ("pe", "act", "dve", "sp"):
                cx.eng[e].wait_ge(sem, 1)

    xcur, xnxt = xT, XA

    def done():
        return nc

    for l in range(DEPTH):
        if not ("ffn1" in SKIP and l == 0):
            if ffn(l, 1, xcur, xnxt, l * 3 + 0):
                return finish()
            xcur, xnxt = xnxt, (XB if xnxt is XA else XA)
        if STOP == "ffn1":
            return finish()
        wi = W["w_in"][l]
        groups = []
        for i in range(6):
            groups.append(dict(outs=[(wi[:, O_QA + i * 256:O_QA + (i + 1) * 256], 0)], units=[[(0, 0), (0, 128)]], kind="ropeA", dst=G["qaT"], row0=i * 256))
        for i in range(6):
            groups.append(dict(outs=[(wi[:, O_KA + i * 256:O_KA + (i + 1) * 256], 0)], units=[[(0, 0), (0, 128)]], kind="ropeA", dst=G["kaT"], row0=i * 256))
        for i in range(8):
            groups.append(dict(outs=[(wi[:, O_QR + i * 256:O_QR + (i + 1) * 256], 0)], units=[[(0, 0), (0, 128)]], kind="ropeR", dst=G["qrT"], row0=i * 256))
        for i in range(8):
            groups.append(dict(outs=[(wi[:, O_KR + i * 256:O_KR + (i + 1) * 256], 0)], units=[[(0, 0), (0, 128)]], kind="ropeR", dst=G["krT"], row0=i * 256))
        groups += simple_groups(wi, 1536, "copy", G["vaT"], col0=O_VA)
        groups += simple_groups(wi, 4096, "copy", G["vrT"], col0=O_VR)
        groups += simple_groups(wi, 4096, "silu", G["sgT"], col0=O_GR)
        groups += simple_groups(wi, 2048, "sigmoid", G["gaT"], col0=O_GA)
        groups += simple_groups(wi, 2048, "sigmoid", G["gbT"], col0=O_GB)
        if "w_in" not in SKIP:
            with cx.phase("w_in"):
                linear_phase(cx, G, 2048, 256, groups, norm=(xcur, l * 3 + 1), need=("stgA", "stgB", "tmp", "rope"))
        if STOP == "w_in":
            return finish()
        with cx.phase("tails"):
            tb_ = Buf("tails")
            for j in range(2):
                cx.dma("sp", G["exT2"][j], G["kaT"][1024 + j * 256:1024 + (j + 1) * 256, T - 2048:T], tb_)
                cx.dma("sp", G["exT2"][2 + j], G["vaT"][1024 + j * 256:1024 + (j + 1) * 256, T - 2048:T], tb_)
            cx.dma("sp", G["exT1"][0:512, :], G["kaT"][512:1024, T - 512:T], tb_)
            cx.dma("sp", G["exT1"][512:1024, :], G["vaT"][512:1024, T - 512:T], tb_)
            cx.dma("sp", G["exT0"][0:512, :], G["kaT"][0:512, T - 128:T], tb_)
            cx.dma("sp", G["exT0"][512:1024, :], G["vaT"][0:512, T - 128:T], tb_)
        tsems = exchange(EX[4:10]) if "exch" not in SKIP else []
        with cx.phase("ret1"):
            retention_local_state(cx, G)
        if STOP == "ret1":
            return finish()
        ssems = exchange(EX[0:4]) if "exch" not in SKIP else []
        wait_cc(tsems)
        if STOP == "exch":
            wait_cc(ssems)
            return finish()
        with cx.phase("attn"):
            attention_phase(cx, G)
        with cx.phase("merge"):
            attn_merge_phase(cx, G)
        wait_cc(ssems)
        with cx.phase("ret2"):
            Sf = cx.sb([128, 8, 2, 512], F32, "Sf")
            Sb = cx.sb([128, 8, 2, 512], BF16, "Sb")
            with ExitStack() as st2:
                old = cx.st
                cx.st = st2
                state_init(cx, G, (Sf, Sb), False)
                cx.barrier()
                cx.st = old
            retention_pass(cx, G, (Sf, Sb), False)
        if STOP in ("ret", "attn"):
            return finish()
        pa, pr = W["w_proj_attn"][l], W["w_proj_ret"][l]
        groups = []
        for s in range(D // 256):
            groups.append(dict(outs=[(pa[:, s * 256:(s + 1) * 256], 0), (pr[:, s * 256:(s + 1) * 256], 4)],
                               units=[[(0, 0), (1, 0)], [(0, 128), (1, 128)]], kind="gate", dst=G["mT"], row0=s * 256, ga=G["gaT"], gb=G["gbT"]))
        with cx.phase("proj"):
            linear_phase(cx, G, 1024, 0, groups, xsrc=[G["oaT"], G["orT"]], need=("stgA", "tmp", "gate"))
        if STOP == "proj":
            return finish()
        with cx.phase("w_out"):
            linear_phase(cx, G, 2048, 0, simple_groups(W["w_out"][l], D, "res", xnxt, xin=xcur, scale=1.0), xsrc=[G["mT"]], need=("res",))
        xcur, xnxt = xnxt, (XB if xnxt is XA else XA)
        if STOP == "w_out":
            return finish()
        if ffn(l, 2, xcur, xnxt, l * 3 + 2):
            return finish()
        xcur, xnxt = xnxt, (XB if xnxt is XA else XA)
        if STOP == "layer0":
            return finish()
    with cx.phase("final"):
        final_norm_phase(cx, G, xcur, 6, yT)
    print("kernel build: instructions ~", cx.ninst, "counts", cx.cnt)
    return nc


def _consts(qd):
    c = np.zeros((128, CW), np.float64)
    c[:, C_ID:C_ID + 128] = np.eye(128)
    i = np.arange(128)
    for h in range(8):
        gm = GAMMA[h]
        rel = i[None, :] - i[:, None]
        m = np.where(rel >= 0, gm ** np.maximum(rel, 0), 0.0) * (256.0 ** -0.5)
        c[:, C_RMASK + h * 128:C_RMASK + (h + 1) * 128] = m
        c[:, C_RDQ + h * 128:C_RDQ + (h + 1) * 128] = (gm ** (i + 1.0))[None, :]
        c[:, C_RDK + h] = gm ** (127.0 - i) * (256.0 ** -0.5)
        for n in range(32):
            dk = gm ** (T - 1.0 - (n * 128 + i)) * (256.0 ** -0.5)
            c[:, C_DKABS + n * 8 + h] = np.where(dk < 1e-30, 0.0, dk)
        for r in range(4):
            c[:, C_COEF + r * 8 + h] = (gm ** (4096.0 * (qd - r - 1))) if r < qd else 0.0
    for r in range(4):
        c[:, C_SEL + r] = 1.0 if r == qd - 1 else 0.0
    qi = i[:, None]
    kj = np.arange(256)[None, :]
    band = np.where((kj >= qi) & (kj <= qi + 128), 0.0, NEG)
    c[:, C_AMASK:C_AMASK + 256] = band
    first = band.copy()
    if qd == 0:
        first[:, 0:128] = NEG
    c[:, C_AMASK + 256:C_AMASK + 512] = first
    return c


def _rope(qd, dim, reps):
    inv = (1.0 / (10000.0 ** (np.arange(0, dim, 2, dtype=np.float32) / np.float32(dim)))).astype(np.float32)
    pos = (np.arange(T, dtype=np.float32) + np.float32(qd * T)).astype(np.float32)
    ang = (pos[None, :] * inv[:, None]).astype(np.float32)
    cs = np.stack([np.cos(ang), np.sin(ang)]).astype(np.float32)
    return np.ascontiguousarray(np.tile(cs, (1, reps, 1)))


def _perm_w_in(w_in):
    w = np.array(w_in, copy=True)
    for base in (0, 1536):
        blk = w_in[:, :, base:base + 1536].reshape(DEPTH, D, 6, 2, 2, 64)
        w[:, :, base:base + 1536] = blk.transpose(0, 1, 2, 4, 3, 5).reshape(DEPTH, D, 1536)
    return w


def kernel(x, ffn1_norm, ffn1_w_gate, ffn1_w_up, ffn1_w_down, mix_norm, w_in, w_proj_attn, w_proj_ret, w_out,
           ffn2_norm, ffn2_w_gate, ffn2_w_up, ffn2_w_down, final_norm):
    x = np.asarray(x, np.float32)
    gains = [ffn1_norm[0], mix_norm[0], ffn2_norm[0], ffn1_norm[1], mix_norm[1], ffn2_norm[1], final_norm]
    gcols = np.concatenate([np.asarray(g, np.float32).reshape(16, 128).T for g in gains], axis=1)
    wperm = _perm_w_in(np.asarray(w_in, np.float32))
    allw = {"ffn1_w_gate": ffn1_w_gate, "ffn1_w_up": ffn1_w_up, "ffn1_w_down": ffn1_w_down, "w_in": wperm,
            "w_proj_attn": w_proj_attn, "w_proj_ret": w_proj_ret, "w_out": w_out,
            "ffn2_w_gate": ffn2_w_gate, "ffn2_w_up": ffn2_w_up, "ffn2_w_down": ffn2_w_down}
    nc = build()
    shared = {key: np.ascontiguousarray(np.asarray(allw[nm][l], np.float32)) for key, (nm, l, _) in USED_W.items()}
    in_maps = []
    for c in range(NCORES):
        b, qd = c // 4, c % 4
        m = dict(shared)
        m["xT"] = np.ascontiguousarray(x[b, qd * T:(qd + 1) * T, :].T)
        if XOVR is not None and c == 0:
            m["xT"][:, :XOVR.shape[0]] = XOVR.T
        cc = _consts(qd)
        cc[:, C_GAIN:C_GAIN + 112] = gcols
        m["cst"] = cc.astype(np.float32)
        m["ropeA"] = _rope(qd, 128, 2)
        m["ropeR"] = _rope(qd, 256, 1)
        in_maps.append(m)
    res = run_bass_kernel_spmd(nc, in_maps, core_ids=list(range(NCORES)))
    kernel.last = res
    out = np.empty((2, 4 * T, D), np.float32)
    for c in range(NCORES):
        b, qd = c // 4, c % 4
        out[b, qd * T:(qd + 1) * T, :] = res.results[c]["yT"].T
    return out
```

```python
import os
from contextlib import ExitStack, contextmanager
import numpy as np
import concourse.bass as bass
import concourse.mybir as mybir
from concourse.bass_utils import run_bass_kernel_spmd

F32 = mybir.dt.float32
BF16 = mybir.dt.bfloat16
ALU = mybir.AluOpType
AF = mybir.ActivationFunctionType
AX = mybir.AxisListType

NCORES = 8
T = 4096
D = 2048
DFF = 5632
DEPTH = 2
INC = 20992
NEG = -1.0e30
RMS_EPS = 1e-6
GN_EPS = 1e-5
GAMMA = [1.0 - 2.0 ** (-5.0 - h) for h in range(8)]
O_QA, O_KA, O_VA, O_QR, O_KR, O_VR, O_GR, O_GA, O_GB = 0, 1536, 3072, 4608, 6656, 8704, 12800, 16896, 18944
C_ID = 0
C_RMASK = 128
C_RDQ = C_RMASK + 1024
C_RDK = C_RDQ + 1024
C_COEF = C_RDK + 8
C_SEL = C_COEF + 32
C_AMASK = C_SEL + 4
C_GAIN = C_AMASK + 512
C_DKABS = C_GAIN + 112
CW = C_DKABS + 256

STOP = os.environ.get("KSTOP", "")
SKIP = os.environ.get("KSKIP", "").split(",")
USED_W = {}
DBG = [s for s in os.environ.get("KDBG", "").split(",") if s]


class Buf:
    __slots__ = ("name", "w", "r", "ds")

    def __init__(self, name):
        self.name = name
        self.w = {}
        self.r = {}
        self.ds = None


def _merge(d, src):
    for k, (sem, v) in src.items():
        if k not in d or d[k][1] < v:
            d[k] = (sem, v)


class Tl:
    def __init__(self, h, name, psum=False):
        self.h = h
        self.b = Buf(name)
        self.psum = psum

    def __getitem__(self, k):
        return self.h[k]


class Cx:
    def __init__(self, nc):
        self.nc = nc
        self.eng = {"pe": nc.tensor, "act": nc.scalar, "dve": nc.vector, "pool": nc.gpsimd, "sp": nc.sync}
        self.psem = {}
        self.cnt = {}
        for e in ["pe", "act", "dve", "pool"]:
            self.psem[e] = nc.alloc_semaphore("s_" + e)
            self.cnt[e] = 0
        self.known = {e: {} for e in self.eng}
        self.dpool = [[nc.alloc_semaphore("d%d" % i), 0, "d%d" % i] for i in range(44)]
        self.dfree = list(range(8, 44))
        self.dfree_sw = list(range(8))
        self.dused = []
        self.uid = 0
        self.st = None
        self.ninst = 0

    def sb(self, shape, dtype, name="t"):
        self.uid += 1
        nm = "%s_%d" % (name, self.uid)
        h = self.st.enter_context(self.nc.sbuf_tensor(nm, list(shape), dtype))
        return Tl(h, nm)

    def ps(self, shape, dtype, name="p"):
        self.uid += 1
        nm = "%s_%d" % (name, self.uid)
        h = self.st.enter_context(self.nc.psum_tensor(nm, list(shape), dtype))
        return Tl(h, nm, psum=True)

    def _dsem(self, buf, sw=False):
        if buf.ds is None:
            fl = self.dfree_sw if sw else self.dfree
            i = fl.pop(0)
            buf.ds = self.dpool[i]
            self.dused.append((i, buf, sw))
        return buf.ds

    def _wait(self, e, toks):
        for name, (sem, val) in toks.items():
            if e == "pe" and name == "s_pe":
                continue
            if self.known[e].get(name, 0) < val:
                self.eng[e].wait_ge(sem, val)
                self.known[e][name] = val
                self.ninst += 1

    def _deps(self, reads, writes):
        deps = {}
        for b in reads:
            _merge(deps, b.w)
        for b in writes:
            _merge(deps, b.w)
            _merge(deps, b.r)
        return deps

    def _mark(self, tokname, tok, reads, writes):
        t = {tokname: tok}
        for b in writes:
            _merge(b.w, t)
            b.r = {}
        for b in reads:
            _merge(b.r, t)

    def op(self, e, fn, reads=(), writes=()):
        writes = list(writes) + [x for x in reads if isinstance(x, Tl) and x.psum]
        reads = [x.b if isinstance(x, Tl) else x for x in reads if not (isinstance(x, Tl) and x.psum)]
        writes = [x.b if isinstance(x, Tl) else x for x in writes]
        self._wait(e, self._deps(reads, writes))
        ins = fn(self.eng[e])
        self.cnt[e] += 1
        ins.then_inc(self.psem[e], 1)
        self.ninst += 1
        self._mark("s_" + e, (self.psem[e], self.cnt[e]), reads, writes)

    def dma(self, e, out, in_, sbuf, reads=(), writes=()):
        reads = [x.b if isinstance(x, Tl) else x for x in reads]
        writes = [x.b if isinstance(x, Tl) else x for x in writes]
        sb = sbuf.b if isinstance(sbuf, Tl) else sbuf
        self._wait(e, self._deps(reads, writes))
        ds = self._dsem(sb, e == "pool")
        ins = self.eng[e].dma_start(out=out, in_=in_)
        ds[1] += 16
        ins.then_inc(ds[0], 16)
        self.ninst += 1
        self._mark(ds[2], (ds[0], ds[1]), reads, writes)

    def barrier(self):
        allt = {}
        for e in self.psem:
            if self.cnt[e] > 0:
                allt["s_" + e] = (self.psem[e], self.cnt[e])
        for s, c, n in self.dpool:
            if c > 0:
                allt[n] = (s, c)
        for e in self.eng:
            self._wait(e, allt)
        for i, b, sw in self.dused:
            b.ds = None
            (self.dfree_sw if sw else self.dfree).append(i)
        self.dused = []

    @contextmanager
    def phase(self, name):
        st = ExitStack()
        self.st = st
        try:
            yield st
        finally:
            self.barrier()
            st.close()


def linear_phase(cx, G, TB, NS, groups, xsrc=None, norm=None, need=()):
    nc = cx.nc
    KC = 16 if norm is not None else sum(a.shape[0] for a in xsrc) // 128
    NSUB = TB // 512
    X = cx.sb([128, KC, TB], BF16, "X")
    XBs = [Buf("X%d" % i) for i in range(NSUB)]
    maxk = max(max(w.shape[0] // 128 for (w, _) in g["outs"]) for g in groups)
    nouts = max(len(g["outs"]) for g in groups)
    slabs = [[cx.sb([128, maxk, 256], BF16, "slab") for _ in range(nouts)] for _ in range(2)]
    NPB = 3
    banks = [[cx.ps([128, 512], F32, "bk") for _ in range(2)] for _ in range(NPB)]
    stgA = [cx.sb([128, TB], BF16, "stgA") for _ in range(2)] if "stgA" in need else None
    stgB = [cx.sb([128, TB], BF16, "stgB") for _ in range(2)] if "stgB" in need else None
    tmp = [cx.sb([128, 512], F32, "tmp") for _ in range(4)] if "tmp" in need else None
    xin = [cx.sb([128, TB], F32, "xin") for _ in range(2)] if "res" in need else None
    xo = [cx.sb([128, TB], F32, "xo") for _ in range(2)] if "res" in need else None
    gta = [cx.sb([128, TB], BF16, "gta") for _ in range(2)] if "gate" in need else None
    gtb = [cx.sb([128, TB], BF16, "gtb") for _ in range(2)] if "gate" in need else None
    rtab = cx.sb([128, 4, TB], F32, "rtab") if "rope" in need else None
    if norm is not None:
        xs32 = [cx.sb([128, 16, NS], F32, "xs32") for _ in range(2)]
        sq = [cx.sb([128, 16, NS], BF16, "sq") for _ in range(2)]
        psn = [cx.ps([128, 512], F32, "psn") for _ in range(2)]
        rs0 = [cx.sb([128, NS], F32, "rs0") for _ in range(2)]
        rstd = [cx.sb([128, NS], F32, "rstd") for _ in range(2)]
    rot = {"bank": 0, "stg": 0, "tmp": 0, "res": 0, "gate": 0}

    def load_slabs(gi):
        g = groups[gi]
        for oi, (w, kc0) in enumerate(g["outs"]):
            sl = slabs[gi % 2][oi]
            nk = w.shape[0] // 128
            cx.dma("pool", sl[:, 0:nk, :], w.rearrange("(c p) f -> p c f", p=128), sl, writes=[sl])

    for tb in range(T // TB):
        t0 = tb * TB
        if norm is not None:
            x_ap, gidx = norm

            def nA(s_):
                x3, sq_ = xs32[s_ % 2], sq[s_ % 2]
                ts0 = t0 + s_ * NS
                cx.dma("sp", x3[:], x_ap.rearrange("(c p) t -> p c t", p=128)[:, :, ts0:ts0 + NS], x3, writes=[x3])
                cx.op("act", lambda a: a.activation(out=sq_[:], in_=x3[:], func=AF.Square), reads=[x3], writes=[sq_])

            def nB(s_):
                sq_, pn, r0, rd = sq[s_ % 2], psn[s_ % 2], rs0[s_ % 2], rstd[s_ % 2]
                for c in range(16):
                    cx.op("pe", lambda p, c=c: p.matmul(pn[:, 0:NS], lhsT=G["ones"][:], rhs=sq_[:, c, :], start=(c == 0), stop=(c == 15)),
                          reads=[sq_, G["onesb"]], writes=[pn])
                cx.op("act", lambda a: a.activation(out=r0[:], in_=pn[:, 0:NS], func=AF.Sqrt, scale=1.0 / D, bias=G["eps"][:, 0:1]),
                      reads=[pn, G["cstb"]], writes=[r0])
                cx.op("dve", lambda v: v.reciprocal(out=rd[:], in_=r0[:]), reads=[r0], writes=[rd])

            def nC(s_):
                x3, rd = xs32[s_ % 2], rstd[s_ % 2]
                for c in range(16):
                    cx.op("dve", lambda v, c=c: v.scalar_tensor_tensor(
                        out=X[:, c, s_ * NS:(s_ + 1) * NS], in0=x3[:, c, :],
                        scalar=G["cst"][:, C_GAIN + gidx * 16 + c:C_GAIN + gidx * 16 + c + 1],
                        in1=rd[:], op0=ALU.mult, op1=ALU.mult), reads=[x3, rd, G["cstb"]], writes=[XBs[(s_ * NS) // 512]])

            skew(TB // NS, [(nA, 0), (nB, 1), (nC, 1)])
        else:
            for ts_ in range(NSUB):
                kc = 0
                for a in xsrc:
                    n = a.shape[0] // 128
                    av = a.rearrange("(c p) t -> p c t", p=128)
                    for c0 in range(0, n, 8):
                        c1 = min(n, c0 + 8)
                        cx.dma("sp", X[:, kc + c0:kc + c1, ts_ * 512:(ts_ + 1) * 512], av[:, c0:c1, t0 + ts_ * 512:t0 + (ts_ + 1) * 512],
                               XBs[ts_], writes=[XBs[ts_]])
                    kc += n
        if rtab is not None:
            ra, rr = G["ropeA"], G["ropeR"]
            cx.dma("sp", rtab[:, 0:2, :], ra.rearrange("k p t -> p k t")[:, :, t0:t0 + TB], rtab, writes=[rtab])
            cx.dma("sp", rtab[:, 2:4, :], rr.rearrange("k p t -> p k t")[:, :, t0:t0 + TB], rtab, writes=[rtab])
        load_slabs(0)
        for gi, g in enumerate(groups):
            if gi + 1 < len(groups):
                load_slabs(gi + 1)
            kind = g["kind"]
            sl = slabs[gi % 2]
            for ui, unit in enumerate(g["units"]):
                if kind in ("copy", "silu", "sigmoid", "swiglu", "ropeA", "ropeR", "gate"):
                    sA = stgA[rot["stg"] % 2]
                    sB = stgB[rot["stg"] % 2] if stgB is not None else None
                    rot["stg"] += 1
                if kind == "res":
                    xi = xin[rot["res"] % 2]
                    xoo = xo[rot["res"] % 2]
                    rot["res"] += 1
                    r0 = g["row0"] + ui * 128
                    cx.dma("sp", xi[:], g["xin"][r0:r0 + 128, t0:t0 + TB], xi, writes=[xi])
                if kind == "gate":
                    ga_t = gta[rot["gate"] % 2]
                    gb_t = gtb[rot["gate"] % 2]
                    rot["gate"] += 1
                    r0 = g["row0"] + ui * 128
                    cx.dma("sp", ga_t[:], g["ga"][r0:r0 + 128, t0:t0 + TB], ga_t, writes=[ga_t])
                    cx.dma("sp", gb_t[:], g["gb"][r0:r0 + 128, t0:t0 + TB], gb_t, writes=[gb_t])
                for ts in range(NSUB):
                    bk = banks[rot["bank"] % NPB]
                    rot["bank"] += 1
                    cs = slice(ts * 512, (ts + 1) * 512)
                    for j, (oi, col) in enumerate(unit):
                        w, kc0 = g["outs"][oi]
                        nk = w.shape[0] // 128
                        for kc in range(nk):
                            cx.op("pe", lambda p, j=j, oi=oi, col=col, kc=kc, kc0=kc0, nk=nk: p.matmul(
                                bk[j][:], lhsT=sl[oi][:, kc, col:col + 128], rhs=X[:, kc0 + kc, cs],
                                start=(kc == 0), stop=(kc == nk - 1)), reads=[sl[oi], XBs[ts]], writes=[bk[j]])
                    if kind in ("copy", "silu", "sigmoid"):
                        fn = {"copy": AF.Copy, "silu": AF.Silu, "sigmoid": AF.Sigmoid}[kind]
                        cx.op("act", lambda a, fn=fn: a.activation(out=sA[:, cs], in_=bk[0][:], func=fn), reads=[bk[0]], writes=[sA])
                    elif kind == "swiglu":
                        tm = tmp[rot["tmp"] % 4]
                        rot["tmp"] += 1
                        cx.op("act", lambda a: a.activation(out=tm[:], in_=bk[0][:], func=AF.Silu), reads=[bk[0]], writes=[tm])
                        cx.op("dve", lambda v: v.tensor_tensor(out=sA[:, cs], in0=bk[1][:], in1=tm[:], op=ALU.mult), reads=[bk[1], tm], writes=[sA])
                    elif kind in ("ropeA", "ropeR"):
                        k0 = 0 if kind == "ropeA" else 2
                        cc = rtab[:, k0, cs]
                        ss = rtab[:, k0 + 1, cs]
                        t1, t2, t3, t4 = [tmp[(rot["tmp"] + i) % 4] for i in range(4)]
                        rot["tmp"] += 4
                        A, B = bk[0], bk[1]
                        cx.op("dve", lambda v: v.tensor_tensor(out=t1[:], in0=A[:], in1=cc, op=ALU.mult), reads=[A, rtab], writes=[t1])
                        cx.op("dve", lambda v: v.tensor_tensor(out=t2[:], in0=B[:], in1=ss, op=ALU.mult), reads=[B, rtab], writes=[t2])
                        cx.op("pool", lambda v: v.tensor_tensor(out=sA[:, cs], in0=t1[:], in1=t2[:], op=ALU.subtract), reads=[t1, t2], writes=[sA])
                        cx.op("dve", lambda v: v.tensor_tensor(out=t3[:], in0=B[:], in1=cc, op=ALU.mult), reads=[B, rtab], writes=[t3])
                        cx.op("dve", lambda v: v.tensor_tensor(out=t4[:], in0=A[:], in1=ss, op=ALU.mult), reads=[A, rtab], writes=[t4])
                        cx.op("pool", lambda v: v.tensor_tensor(out=sB[:, cs], in0=t3[:], in1=t4[:], op=ALU.add), reads=[t3, t4], writes=[sB])
                    elif kind == "res":
                        sc = g["scale"]
                        cx.op("dve", lambda v: v.scalar_tensor_tensor(out=xoo[:, cs], in0=bk[0][:], scalar=sc, in1=xi[:, cs],
                                                                      op0=ALU.mult, op1=ALU.add), reads=[bk[0], xi], writes=[xoo])
                    elif kind == "gate":
                        t1, t2 = [tmp[(rot["tmp"] + i) % 4] for i in range(2)]
                        rot["tmp"] += 2
                        cx.op("dve", lambda v: v.tensor_tensor(out=t1[:], in0=bk[0][:], in1=ga_t[:, cs], op=ALU.mult), reads=[bk[0], ga_t], writes=[t1])
                        cx.op("dve", lambda v: v.tensor_tensor(out=t2[:], in0=bk[1][:], in1=gb_t[:, cs], op=ALU.mult), reads=[bk[1], gb_t], writes=[t2])
                        cx.op("pool", lambda v: v.tensor_tensor(out=sA[:, cs], in0=t1[:], in1=t2[:], op=ALU.add), reads=[t1, t2], writes=[sA])
                tsl = slice(t0, t0 + TB)
                if kind in ("copy", "silu", "sigmoid", "swiglu", "gate"):
                    r0 = g["row0"] + ui * 128
                    cx.dma("sp", g["dst"][r0:r0 + 128, tsl], sA[:], sA, reads=[sA])
                elif kind == "ropeR":
                    r0 = g["row0"]
                    cx.dma("sp", g["dst"][r0:r0 + 128, tsl], sA[:], sA, reads=[sA])
                    cx.dma("sp", g["dst"][r0 + 128:r0 + 256, tsl], sB[:], sB, reads=[sB])
                elif kind == "ropeA":
                    r0 = g["row0"]
                    cx.dma("sp", g["dst"][r0:r0 + 64, tsl], sA[0:64, :], sA, reads=[sA])
                    cx.dma("sp", g["dst"][r0 + 128:r0 + 192, tsl], sA[64:128, :], sA, reads=[sA])
                    cx.dma("sp", g["dst"][r0 + 64:r0 + 128, tsl], sB[0:64, :], sB, reads=[sB])
                    cx.dma("sp", g["dst"][r0 + 192:r0 + 256, tsl], sB[64:128, :], sB, reads=[sB])
                elif kind == "res":
                    r0 = g["row0"] + ui * 128
                    cx.dma("sp", g["dst"][r0:r0 + 128, tsl], xoo[:], xoo, reads=[xoo])


def skew(n, stages):
    md = max(d for _, d in stages)
    for t in range(n + md):
        for fn, d in stages:
            i = t - d
            if 0 <= i < n:
                fn(i)


def simple_groups(w, ncols, kind, dst, col0=0, row0=0, **kw):
    gs = []
    for s in range(ncols // 256):
        g = dict(outs=[(w[:, col0 + s * 256:col0 + (s + 1) * 256], 0)], units=[[(0, 0)], [(0, 128)]], kind=kind, dst=dst,
                 row0=row0 + s * 256)
        g.update(kw)
        gs.append(g)
    return gs


def final_norm_phase(cx, G, x_ap, gidx, y_ap):
    NS = 256
    xs32 = [cx.sb([128, 16, NS], F32, "xs32") for _ in range(2)]
    sq = cx.sb([128, 16, NS], BF16, "sq")
    psn = cx.ps([128, 512], F32, "psn")
    rs0 = cx.sb([128, NS], F32, "rs0")
    rstd = cx.sb([128, NS], F32, "rstd")
    yo = [cx.sb([128, 16, NS], F32, "yo") for _ in range(2)]
    for s in range(T // NS):
        ts0 = s * NS
        x3 = xs32[s % 2]
        y3 = yo[s % 2]
        cx.dma("sp", x3[:], x_ap.rearrange("(c p) t -> p c t", p=128)[:, :, ts0:ts0 + NS], x3, writes=[x3])
        cx.op("act", lambda a: a.activation(out=sq[:], in_=x3[:], func=AF.Square), reads=[x3], writes=[sq])
        for c in range(16):
            cx.op("pe", lambda p, c=c: p.matmul(psn[:, 0:NS], lhsT=G["ones"][:], rhs=sq[:, c, :], start=(c == 0), stop=(c == 15)),
                  reads=[sq, G["onesb"]], writes=[psn])
        cx.op("act", lambda a: a.activation(out=rs0[:], in_=psn[:, 0:NS], func=AF.Sqrt, scale=1.0 / D, bias=G["eps"][:, 0:1]),
              reads=[psn, G["cstb"]], writes=[rs0])
        cx.op("dve", lambda v: v.reciprocal(out=rstd[:], in_=rs0[:]), reads=[rs0], writes=[rstd])
        for c in range(16):
            cx.op("dve", lambda v, c=c: v.scalar_tensor_tensor(
                out=y3[:, c, :], in0=x3[:, c, :], scalar=G["cst"][:, C_GAIN + gidx * 16 + c:C_GAIN + gidx * 16 + c + 1],
                in1=rstd[:], op0=ALU.mult, op1=ALU.mult), reads=[x3, rstd, G["cstb"]], writes=[y3])
        cx.dma("sp", y_ap.rearrange("(c p) t -> p c t", p=128)[:, :, ts0:ts0 + NS], y3[:], y3, reads=[y3])


def retention_pass(cx, G, S, first):
    nc = cx.nc
    BT = 128
    cst = G["cst"]
    qB = [cx.sb([128, 16, BT], BF16, "rq") for _ in range(2)]
    kB = [cx.sb([128, 16, BT], BF16, "rk") for _ in range(2)]
    vB = [cx.sb([128, 32, BT], BF16, "rv") for _ in range(2)]
    sgB = [cx.sb([128, 32, BT], BF16, "rsg") for _ in range(2)]
    ostB = [cx.sb([128, 32, BT], BF16, "rost") for _ in range(2)]
    Sf, Sb = S
    psA = cx.ps([128, 512], F32, "psA")
    psO2 = [cx.ps([128, 512], F32, "psO") for _ in range(2)]
    psOT = cx.ps([128, 1024], BF16, "psOT")
    psT = [cx.ps([128, 1024], BF16, "psT") for _ in range(2)]
    psS = [cx.ps([128, 512], F32, "psS") for _ in range(2)]
    ident = G["ident"]
    assert not first
    atm = [cx.sb([128, 128], BF16, "atm") for _ in range(3)]
    qdec = [cx.sb([128, 2, 128], BF16, "qdec") for _ in range(3)]
    on = [cx.sb([128, 512], BF16, "on") for _ in range(4)]
    st6 = [cx.sb([128, 6], F32, "st6") for _ in range(4)]
    mv = [cx.sb([128, 2], F32, "mv") for _ in range(4)]
    sd = [cx.sb([128, 1], F32, "sd") for _ in range(4)]
    rs = [cx.sb([128, 1], F32, "rs") for _ in range(4)]
    nb = [cx.sb([128, 1], F32, "nb") for _ in range(4)]
    kdec = [cx.sb([128, 256], BF16, "kdec") for _ in range(3)]
    vtok = [cx.sb([128, 512], BF16, "vtok") for _ in range(3)]
    NBLK = T // BT

    def load_block(lb):
        tsl = slice(lb * BT, (lb + 1) * BT)
        q, k, v, sg = qB[lb % 2], kB[lb % 2], vB[lb % 2], sgB[lb % 2]
        cx.dma("sp", k[:], G["krT"].rearrange("(c p) t -> p c t", p=128)[:, :, tsl], k, writes=[k])
        cx.dma("sp", q[:], G["qrT"].rearrange("(c p) t -> p c t", p=128)[:, :, tsl], q, writes=[q])
        cx.dma("sp", v[:, 0:16, :], G["vrT"].rearrange("(c p) t -> p c t", p=128)[:, 0:16, tsl], v, writes=[v])
        cx.dma("sp", v[:, 16:32, :], G["vrT"].rearrange("(c p) t -> p c t", p=128)[:, 16:32, tsl], v, writes=[v])
        cx.dma("sp", sg[:, 0:16, :], G["sgT"].rearrange("(c p) t -> p c t", p=128)[:, 0:16, tsl], sg, writes=[sg])
        cx.dma("sp", sg[:, 16:32, :], G["sgT"].rearrange("(c p) t -> p c t", p=128)[:, 16:32, tsl], sg, writes=[sg])

    def stA(i):
        blk, h = divmod(i, 8)
        if i == 0:
            load_block(0)
        if h == 4 and blk + 1 < NBLK:
            load_block(blk + 1)
        q, k, v = qB[blk % 2], kB[blk % 2], vB[blk % 2]
        pT = psT[i % 2]
        for dc in range(2):
            cx.op("pe", lambda p, dc=dc: p.transpose(out=pT[:, dc * 128:(dc + 1) * 128], in_=k[:, 2 * h + dc, :], identity=ident[:]),
                  reads=[k, G["identb"]], writes=[pT])
        for ec in range(4):
            cx.op("pe", lambda p, ec=ec: p.transpose(out=pT[:, 256 + ec * 128:256 + (ec + 1) * 128], in_=v[:, 4 * h + ec, :], identity=ident[:]),
                  reads=[v, G["identb"]], writes=[pT])
        for dc in range(2):
            cx.op("pe", lambda p, dc=dc: p.matmul(psA[:, 0:128], lhsT=k[:, 2 * h + dc, :], rhs=q[:, 2 * h + dc, :], start=(dc == 0), stop=(dc == 1)),
                  reads=[k, q], writes=[psA])

    def stB(i):
        blk, h = divmod(i, 8)
        q = qB[blk % 2]
        pT, kd_, vt_, at_, qd_ = psT[i % 2], kdec[i % 3], vtok[i % 3], atm[i % 3], qdec[i % 3]
        cx.op("act", lambda a: a.activation(out=kd_[:], in_=pT[:, 0:256], func=AF.Copy, scale=cst[:, C_RDK + h:C_RDK + h + 1]),
              reads=[pT, G["cstb"]], writes=[kd_])
        cx.op("act", lambda a: a.activation(out=vt_[:], in_=pT[:, 256:768], func=AF.Copy), reads=[pT], writes=[vt_])
        cx.op("dve", lambda e: e.tensor_tensor(out=at_[:], in0=psA[:, 0:128], in1=cst[:, C_RMASK + h * 128:C_RMASK + (h + 1) * 128], op=ALU.mult),
              reads=[psA, G["cstb"]], writes=[at_])
        for dc in range(2):
            cx.op("pool", lambda e, dc=dc: e.tensor_tensor(out=qd_[:, dc, :], in0=q[:, 2 * h + dc, :], in1=cst[:, C_RDQ + h * 128:C_RDQ + (h + 1) * 128], op=ALU.mult),
                  reads=[q, G["cstb"]], writes=[qd_])

    def stC(i):
        blk, h = divmod(i, 8)
        kd_, vt_, at_, qd_ = kdec[i % 3], vtok[i % 3], atm[i % 3], qdec[i % 3]
        psO = psO2[i % 2]
        for dc in range(2):
            cx.op("pe", lambda p, dc=dc: p.matmul(psO[:], lhsT=qd_[:, dc, :], rhs=Sb[:, h, dc, :], start=(dc == 0), stop=False),
                  reads=[qd_, Sb], writes=[psO])
        cx.op("pe", lambda p: p.matmul(psO[:], lhsT=at_[:], rhs=vt_[:], start=False, stop=True), reads=[at_, vt_], writes=[psO])
        for dc in range(2):
            cx.op("pe", lambda p, dc=dc: p.matmul(psS[dc][:], lhsT=kd_[:, dc * 128:(dc + 1) * 128], rhs=vt_[:], start=True, stop=True),
                  reads=[kd_, vt_], writes=[psS[dc]])

    def stD(i):
        blk, h = divmod(i, 8)
        psO = psO2[i % 2]
        on_ = on[i % 4]
        cd = GAMMA[h] ** 128
        for dc in range(2):
            cx.op("dve", lambda e, dc=dc: e.scalar_tensor_tensor(out=Sf[:, h, dc, :], in0=Sf[:, h, dc, :], scalar=cd, in1=psS[dc][:], op0=ALU.mult, op1=ALU.add),
                  reads=[psS[dc]], writes=[Sf])
        for dc in range(2):
            cx.op("act", lambda a, dc=dc: a.activation(out=Sb[:, h, dc, :], in_=Sf[:, h, dc, :], func=AF.Copy), reads=[Sf], writes=[Sb])
        s6, mv_, sd_, rs_, nb_ = st6[i % 4], mv[i % 4], sd[i % 4], rs[i % 4], nb[i % 4]
        cx.op("dve", lambda e: e.bn_stats(out=s6[:], in_=psO[:]), reads=[psO], writes=[s6])
        cx.op("dve", lambda e: e.bn_aggr(out=mv_[:], in_=s6[:]), reads=[s6], writes=[mv_])
        cx.op("act", lambda a: a.activation(out=sd_[:], in_=mv_[:, 1:2], func=AF.Sqrt, bias=G["eps"][:, 1:2]), reads=[mv_, G["cstb"]], writes=[sd_])
        cx.op("dve", lambda e: e.reciprocal(out=rs_[:], in_=sd_[:]), reads=[sd_], writes=[rs_])
        cx.op("dve", lambda e: e.scalar_tensor_tensor(out=nb_[:], in0=mv_[:, 0:1], scalar=-1.0, in1=rs_[:], op0=ALU.mult, op1=ALU.mult),
              reads=[mv_, rs_], writes=[nb_])
        cx.op("act", lambda a: a.activation(out=on_[:], in_=psO[:], func=AF.Identity, scale=rs_[:, 0:1], bias=nb_[:, 0:1]),
              reads=[psO, rs_, nb_], writes=[on_])

    def stE(i):
        on_ = on[i % 4]
        for ec in range(4):
            cx.op("pe", lambda p, ec=ec: p.transpose(out=psOT[:, ec * 128:(ec + 1) * 128], in_=on_[:, ec * 128:(ec + 1) * 128], identity=ident[:]),
                  reads=[on_, G["identb"]], writes=[psOT])

    def stF(i):
        blk, h = divmod(i, 8)
        sg, ost = sgB[blk % 2], ostB[blk % 2]
        cx.op("dve", lambda e: e.tensor_tensor(out=ost[:, 4 * h:4 * h + 4, :], in0=psOT[:, 0:512].rearrange("p (e t) -> p e t", e=4),
                                               in1=sg[:, 4 * h:4 * h + 4, :], op=ALU.mult), reads=[psOT, sg], writes=[ost])
        if h == 7:
            tsl = slice(blk * BT, (blk + 1) * BT)
            cx.dma("sp", G["orT"].rearrange("(c p) t -> p c t", p=128)[:, 0:16, tsl], ost[:, 0:16, :], ost, reads=[ost])
            cx.dma("sp", G["orT"].rearrange("(c p) t -> p c t", p=128)[:, 16:32, tsl], ost[:, 16:32, :], ost, reads=[ost])

    skew(8 * NBLK, [(stA, 0), (stB, 0), (stC, 1), (stD, 1), (stE, 3), (stF, 3)])
    return
    if first:
        for i in range(4):
            cx.dma("sp", G["exS_in"][i].rearrange("(h c p) e -> p h c e", p=128, c=2), Sf[:, 2 * i:2 * i + 2], Sf, reads=[Sf])


def retention_local_state(cx, G):
    BT = 256
    cst = G["cst"]
    ident = G["ident"]
    kb = [cx.sb([128, 4, BT], BF16, "lk") for _ in range(3)]
    vb = [cx.sb([128, 8, BT], BF16, "lv") for _ in range(3)]
    kdec = [cx.sb([128, 256], BF16, "lkd") for _ in range(3)]
    vtok = [cx.sb([128, 512], BF16, "lvt") for _ in range(3)]
    psT = [cx.ps([128, 1024], BF16, "lpT") for _ in range(3)]
    acc = [[cx.ps([128, 512], F32, "lacc") for _ in range(2)] for _ in range(2)]
    stg = cx.sb([128, 2, 2, 512], F32, "lstg")
    for hp in range(4):
        NI = (T // BT) * (BT // 128) * 2

        def dec(i):
            blk, rem = divmod(i, (BT // 128) * 2)
            ch, hh = divmod(rem, 2)
            return blk, ch, hh

        def stA(i):
            blk, ch, hh = dec(i)
            k, v = kb[blk % 3], vb[blk % 3]
            if ch == 0 and hh == 0:
                for lb in ([0, 1, 2] if blk == 0 else [blk + 2]):
                    if lb < T // BT:
                        tsl = slice(lb * BT, (lb + 1) * BT)
                        k2, v2 = kb[lb % 3], vb[lb % 3]
                        cx.dma("sp", k2[:], G["krT"][hp * 512:(hp + 1) * 512, :].rearrange("(c p) t -> p c t", p=128)[:, :, tsl], k2, writes=[k2])
                        cx.dma("sp", v2[:], G["vrT"][hp * 1024:(hp + 1) * 1024, :].rearrange("(c p) t -> p c t", p=128)[:, :, tsl], v2, writes=[v2])
            cs = slice(ch * 128, (ch + 1) * 128)
            pT = psT[i % 3]
            for dc in range(2):
                cx.op("pe", lambda p, dc=dc: p.transpose(out=pT[:, dc * 128:(dc + 1) * 128], in_=k[:, 2 * hh + dc, cs], identity=ident[:]),
                      reads=[k, G["identb"]], writes=[pT])
            for ec in range(4):
                cx.op("pe", lambda p, ec=ec: p.transpose(out=pT[:, 256 + ec * 128:256 + (ec + 1) * 128], in_=v[:, 4 * hh + ec, cs], identity=ident[:]),
                      reads=[v, G["identb"]], writes=[pT])

        def stB(i):
            blk, ch, hh = dec(i)
            n = blk * (BT // 128) + ch
            h = 2 * hp + hh
            pT, kd_, vt_ = psT[i % 3], kdec[i % 3], vtok[i % 3]
            col = C_DKABS + n * 8 + h
            cx.op("dve", lambda e: e.tensor_scalar(out=kd_[:], in0=pT[:, 0:256], scalar1=cst[:, col:col + 1], scalar2=None, op0=ALU.mult),
                  reads=[pT, G["cstb"]], writes=[kd_])
            cx.op("act", lambda a: a.activation(out=vt_[:], in_=pT[:, 256:768], func=AF.Copy), reads=[pT], writes=[vt_])

        def stC(i):
            blk, ch, hh = dec(i)
            n = blk * (BT // 128) + ch
            kd_, vt_ = kdec[i % 3], vtok[i % 3]
            for dc in range(2):
                cx.op("pe", lambda p, dc=dc: p.matmul(acc[hh][dc][:], lhsT=kd_[:, dc * 128:(dc + 1) * 128], rhs=vt_[:], start=(n == 0), stop=(n == 31)),
                      reads=[kd_, vt_], writes=[acc[hh][dc]])

        skew(NI, [(stA, 0), (stB, 0), (stC, 1)])
        for hh in range(2):
            for dc in range(2):
                if dc == 0:
                    cx.op("act", lambda a: a.activation(out=stg[:, hh, dc, :], in_=acc[hh][dc][:], func=AF.Copy), reads=[acc[hh][dc]], writes=[stg])
                else:
                    cx.op("dve", lambda e: e.tensor_copy(out=stg[:, hh, dc, :], in_=acc[hh][dc][:]), reads=[acc[hh][dc]], writes=[stg])
        cx.dma("sp", G["exS_in"][hp].rearrange("(h c p) e -> p h c e", p=128, c=2), stg[:], stg, reads=[stg])


def state_init(cx, G, S, zero):
    Sf, Sb = S
    cst = G["cst"]
    if zero:
        cx.op("pool", lambda e: e.memset(Sf[:], 0.0), writes=[Sf])
        cx.op("pool", lambda e: e.memset(Sb[:], 0.0), writes=[Sb])
        return
    gt = [cx.sb([128, 8, 2, 512], F32, "gst") for _ in range(1)]
    g0 = gt[0]
    for r in range(3):
        for i in range(4):
            src = G["exS_out"][i][r * 512:(r + 1) * 512, :].rearrange("(h c p) e -> p h c e", p=128, c=2)
            cx.dma("sp", g0[:, 2 * i:2 * i + 2], src, g0, writes=[g0])
        for h in range(8):
            col = cst[:, C_COEF + r * 8 + h:C_COEF + r * 8 + h + 1]
            if r == 0:
                cx.op("dve", lambda e, h=h, col=col: e.tensor_scalar(out=Sf[:, h], in0=g0[:, h], scalar1=col, scalar2=None, op0=ALU.mult),
                      reads=[g0, G["cstb"]], writes=[Sf])
            else:
                cx.op("dve", lambda e, h=h, col=col: e.scalar_tensor_tensor(out=Sf[:, h], in0=g0[:, h], scalar=col, in1=Sf[:, h], op0=ALU.mult, op1=ALU.add),
                      reads=[g0, G["cstb"]], writes=[Sf])
    cx.op("pool", lambda e: e.tensor_copy(out=Sb[:], in_=Sf[:]), reads=[Sf], writes=[Sb])


ATT_GROUPS = ((128, 1), (512, 4), (2048, 16))
XOVR = None
USE_POW = False


def attention_phase(cx, G):
    cst = G["cst"]
    ident = G["ident"]
    HMAX = 2048
    qh = cx.sb([128, T], BF16, "aq")
    kh = cx.sb([128, HMAX + T], BF16, "ak")
    vh = cx.sb([128, HMAX + T], BF16, "av")
    qd = cx.sb([128, T], BF16, "aqd")
    kd = cx.sb([128, HMAX + T], BF16, "akd")
    vd = cx.sb([128, HMAX + T], BF16, "avd")
    hk = [cx.sb([128, HMAX], BF16, "hk") for _ in range(3)]
    hv = [cx.sb([128, HMAX], BF16, "hv") for _ in range(3)]
    NSL = 6
    s_sb = [cx.sb([128, 256], F32, "s") for _ in range(NSL)]
    p_sb = [cx.sb([128, 256], BF16, "p") for _ in range(NSL)]
    pT_sb = [cx.sb([128, 256], BF16, "pT") for _ in range(NSL)]
    NVT = 10
    vt = [cx.sb([128, 128], BF16, "vt") for _ in range(NVT)]
    ost = [cx.sb([128, 128], F32, "ao") for _ in range(NSL)]
    mlst = [cx.sb([128, 2], F32, "ml") for _ in range(NSL)]
    negm = [cx.sb([128, 1], F32, "negm") for _ in range(NSL)]
    bank = [cx.ps([128, 512], F32, "abk") for _ in range(NSL)]
    psV = [cx.ps([128, 1024], BF16, "apV") for _ in range(2)]
    scale = 128.0 ** -0.5
    it = 0
    vti = 0
    for g, (window, d) in enumerate(ATT_GROUPS):
        HALO = 128 * d
        L = T // d
        LK = L + 128
        nbl = L // 128
        for hh in range(4):
            H = 4 * g + hh
            rows = slice(H * 128, (H + 1) * 128)
            cx.dma("sp", qh[:], G["qaT"][rows, :], qh, writes=[qh])
            cx.dma("sp", kh[:, HALO:HALO + T], G["kaT"][rows, :], kh, writes=[kh])
            cx.dma("sp", vh[:, HALO:HALO + T], G["vaT"][rows, :], vh, writes=[vh])
            for r in range(3):
                if g == 2:
                    ksrc = G["exT2o"][hh // 2][r * 256 + (hh % 2) * 128:r * 256 + (hh % 2) * 128 + 128, :]
                    vsrc = G["exT2o"][2 + hh // 2][r * 256 + (hh % 2) * 128:r * 256 + (hh % 2) * 128 + 128, :]
                elif g == 1:
                    ksrc = G["exT1o"][r * 1024 + hh * 128:r * 1024 + hh * 128 + 128, :]
                    vsrc = G["exT1o"][r * 1024 + 512 + hh * 128:r * 1024 + 512 + hh * 128 + 128, :]
                else:
                    ksrc = G["exT0o"][r * 1024 + hh * 128:r * 1024 + hh * 128 + 128, :]
                    vsrc = G["exT0o"][r * 1024 + 512 + hh * 128:r * 1024 + 512 + hh * 128 + 128, :]
                cx.dma("sp", hk[r][:, 0:HALO], ksrc, hk[r], writes=[hk[r]])
                cx.dma("sp", hv[r][:, 0:HALO], vsrc, hv[r], writes=[hv[r]])
            for (dst, srcs) in ((kh, hk), (vh, hv)):
                cx.op("dve", lambda e, dst=dst, srcs=srcs: e.tensor_scalar(out=dst[:, 0:HALO], in0=srcs[0][:, 0:HALO], scalar1=cst[:, C_SEL:C_SEL + 1], scalar2=None, op0=ALU.mult),
                      reads=[srcs[0], G["cstb"]], writes=[dst])
                for r in (1, 2):
                    cx.op("dve", lambda e, dst=dst, srcs=srcs, r=r: e.scalar_tensor_tensor(out=dst[:, 0:HALO], in0=srcs[r][:, 0:HALO], scalar=cst[:, C_SEL + r:C_SEL + r + 1],
                                                                                          in1=dst[:, 0:HALO], op0=ALU.mult, op1=ALU.add),
                          reads=[srcs[r], G["cstb"]], writes=[dst])
            if d > 1:
                cx.op("dve", lambda e: e.tensor_copy(out=qd[:, 0:T].rearrange("p (r l) -> p r l", r=d), in_=qh[:, 0:T].rearrange("p (l r) -> p r l", r=d)),
                      reads=[qh], writes=[qd])
                cx.op("act", lambda e: e.activation(out=kd[:, 0:HALO + T].rearrange("p (r l) -> p r l", r=d), in_=kh[:, 0:HALO + T].rearrange("p (l r) -> p r l", r=d), func=AF.Copy),
                      reads=[kh], writes=[kd])
                cx.op("dve", lambda e: e.tensor_copy(out=vd[:, 0:HALO + T].rearrange("p (r l) -> p r l", r=d), in_=vh[:, 0:HALO + T].rearrange("p (l r) -> p r l", r=d)),
                      reads=[vh], writes=[vd])
                Q, K, V = qd, kd, vd
            else:
                Q, K, V = qh, kh, vh
            items = [(r, b) for r in range(d) for b in range(nbl)]
            vtile = {}

            def vtrans(r, bi, Vt=None):
                nonlocal vti
                V_ = Vt
                tl = vt[vti % NVT]
                pv = psV[vti % 2]
                vti += 1
                kb = r * LK
                cx.op("pe", lambda p: p.transpose(out=pv[:, 0:128], in_=V_[:, kb + bi * 128:kb + (bi + 1) * 128], identity=ident[:]),
                      reads=[V_, G["identb"]], writes=[pv])
                cx.op("act", lambda a: a.activation(out=tl[:], in_=pv[:, 0:128], func=AF.Copy), reads=[pv], writes=[tl])
                vtile[(r, bi)] = tl

            def stA(i, Q=Q, K=K, V=V):
                r, b = items[i]
                if b == 0:
                    vtrans(r, 0, V)
                vtrans(r, b + 1, V)
                bk_ = bank[i % NSL]
                qb, kb = r * L, r * LK
                cx.op("pe", lambda p: p.matmul(bk_[:, 0:256], lhsT=Q[:, qb + b * 128:qb + (b + 1) * 128], rhs=K[:, kb + b * 128:kb + b * 128 + 256], start=True, stop=True),
                      reads=[Q, K], writes=[bk_])

            def stB(i):
                r, b = items[i]
                sl_ = i % NSL
                bk_, s_, p_, ml_, nm_ = bank[sl_], s_sb[sl_], p_sb[sl_], mlst[sl_], negm[sl_]
                mcol = C_AMASK + (256 if b == 0 else 0)
                cx.op("dve", lambda e: e.scalar_tensor_tensor(out=s_[:], in0=bk_[:, 0:256], scalar=scale, in1=cst[:, mcol:mcol + 256], op0=ALU.mult, op1=ALU.add),
                      reads=[bk_, G["cstb"]], writes=[s_])
                cx.op("dve", lambda e: e.reduce_max(out=ml_[:, 0:1], in_=s_[:], axis=AX.X), reads=[s_], writes=[ml_])
                cx.op("dve", lambda e: e.tensor_scalar(out=nm_[:], in0=ml_[:, 0:1], scalar1=-1.0, scalar2=None, op0=ALU.mult), reads=[ml_], writes=[nm_])
                cx.op("act", lambda a: a.activation(out=p_[:], in_=s_[:], func=AF.Exp, bias=nm_[:, 0:1], accum_out=ml_[:, 1:2]),
                      reads=[s_, nm_], writes=[p_, ml_])

            def stC(i):
                sl_ = i % NSL
                bk_, p_, pT_ = bank[sl_], p_sb[sl_], pT_sb[sl_]
                pPTv = bk_[:, 256:384].bitcast(BF16)
                for half in range(2):
                    cx.op("pe", lambda p, half=half: p.transpose(out=pPTv[:, half * 128:(half + 1) * 128], in_=p_[:, half * 128:(half + 1) * 128], identity=ident[:]),
                          reads=[p_, G["identb"]], writes=[bk_])
                cx.op("dve", lambda e: e.tensor_copy(out=pT_[:], in_=pPTv[:, 0:256]), reads=[bk_], writes=[pT_])

            def stD(i, g=g, hh=hh, d=d):
                r, b = items[i]
                sl_ = i % NSL
                bk_, pT_, o_, ml_ = bank[sl_], pT_sb[sl_], ost[sl_], mlst[sl_]
                vprev, vcur = vtile[(r, b)], vtile[(r, b + 1)]
                cx.op("pe", lambda p: p.matmul(bk_[:, 384:512], lhsT=pT_[:, 0:128], rhs=vprev[:], start=True, stop=False), reads=[pT_, vprev], writes=[bk_])
                cx.op("pe", lambda p: p.matmul(bk_[:, 384:512], lhsT=pT_[:, 128:256], rhs=vcur[:], start=False, stop=True), reads=[pT_, vcur], writes=[bk_])
                cx.op("act", lambda a: a.activation(out=o_[:], in_=bk_[:, 384:512], func=AF.Copy), reads=[bk_], writes=[o_])
                orows = G["oacc"][g].rearrange("(l r) f -> r l f", r=d)[r, b * 128:(b + 1) * 128, hh * 128:(hh + 1) * 128]
                cx.dma("sp", orows, o_[:], o_, reads=[o_])
                mrows = G["mlacc"].rearrange("(l r) f -> r l f", r=d)[r, b * 128:(b + 1) * 128, (g * 4 + hh) * 2:(g * 4 + hh) * 2 + 2]
                cx.dma("sp", mrows, ml_[:], ml_, reads=[ml_])

            skew(len(items), [(stA, 0), (stB, 0), (stC, 2), (stD, 4)])


def attn_merge_phase(cx, G):
    ident = G["ident"]
    NT_ = T // 128
    o3 = [cx.sb([128, 3, 512], F32, "mo") for _ in range(2)]
    ml = [cx.sb([128, 3, 4, 2], F32, "mml") for _ in range(2)]
    M = cx.sb([128, 4], F32, "mM")
    df = cx.sb([128, 3, 4], F32, "mdf")
    w = cx.sb([128, 3, 4], F32, "mw")
    wl = cx.sb([128, 3, 4], F32, "mwl")
    den = cx.sb([128, 4], F32, "mden")
    rden = cx.sb([128, 4], F32, "mrden")
    wn = cx.sb([128, 3, 4], F32, "mwn")
    acc = cx.sb([128, 512], F32, "macc")
    ob = [cx.sb([128, 512], BF16, "mob") for _ in range(2)]
    stg = [cx.sb([128, 4, 512], BF16, "mstg") for _ in range(2)]
    psT = [cx.ps([128, 1024], BF16, "mpT") for _ in range(2)]
    for tt in range(NT_):
        i2 = tt % 2
        o_, ml_, ob_, pT = o3[i2], ml[i2], ob[i2], psT[i2]
        sg = stg[(tt // 4) % 2]
        rows = slice(tt * 128, (tt + 1) * 128)
        for g in range(3):
            cx.dma("sp", o_[:, g, :], G["oacc"][g][rows, :], o_, writes=[o_])
        cx.dma("sp", ml_[:], G["mlacc"][rows, :].rearrange("t (g h k) -> t g h k", g=3, h=4), ml_, writes=[ml_])
        cx.op("dve", lambda e: e.tensor_tensor(out=M[:], in0=ml_[:, 0, :, 0], in1=ml_[:, 1, :, 0], op=ALU.max), reads=[ml_], writes=[M])
        cx.op("dve", lambda e: e.tensor_tensor(out=M[:], in0=M[:], in1=ml_[:, 2, :, 0], op=ALU.max), reads=[ml_], writes=[M])
        for g in range(3):
            cx.op("dve", lambda e, g=g: e.tensor_tensor(out=df[:, g, :], in0=ml_[:, g, :, 0], in1=M[:], op=ALU.subtract), reads=[ml_, M], writes=[df])
        cx.op("act", lambda a: a.activation(out=w[:], in_=df[:], func=AF.Exp), reads=[df], writes=[w])
        cx.op("dve", lambda e: e.tensor_tensor(out=wl[:], in0=w[:], in1=ml_[:, :, :, 1], op=ALU.mult), reads=[w, ml_], writes=[wl])
        cx.op("dve", lambda e: e.tensor_tensor(out=den[:], in0=wl[:, 0, :], in1=wl[:, 1, :], op=ALU.add), reads=[wl], writes=[den])
        cx.op("dve", lambda e: e.tensor_tensor(out=den[:], in0=den[:], in1=wl[:, 2, :], op=ALU.add), reads=[wl], writes=[den])
        cx.op("dve", lambda e: e.reciprocal(out=rden[:], in_=den[:]), reads=[den], writes=[rden])
        for g in range(3):
            cx.op("dve", lambda e, g=g: e.tensor_tensor(out=wn[:, g, :], in0=w[:, g, :], in1=rden[:], op=ALU.mult), reads=[w, rden], writes=[wn])
        for hh in range(4):
            hs = slice(hh * 128, (hh + 1) * 128)
            cx.op("dve", lambda e, hh=hh, hs=hs: e.tensor_scalar(out=acc[:, hs], in0=o_[:, 0, hs], scalar1=wn[:, 0, hh:hh + 1], scalar2=None, op0=ALU.mult),
                  reads=[o_, wn], writes=[acc])
            cx.op("dve", lambda e, hh=hh, hs=hs: e.scalar_tensor_tensor(out=acc[:, hs], in0=o_[:, 1, hs], scalar=wn[:, 1, hh:hh + 1], in1=acc[:, hs], op0=ALU.mult, op1=ALU.add),
                  reads=[o_, wn], writes=[acc])
            cx.op("dve", lambda e, hh=hh, hs=hs: e.scalar_tensor_tensor(out=ob_[:, hs], in0=o_[:, 2, hs], scalar=wn[:, 2, hh:hh + 1], in1=acc[:, hs], op0=ALU.mult, op1=ALU.add),
                  reads=[o_, wn, acc], writes=[ob_])
        for hh in range(4):
            cx.op("pe", lambda p, hh=hh: p.transpose(out=pT[:, hh * 128:(hh + 1) * 128], in_=ob_[:, hh * 128:(hh + 1) * 128], identity=ident[:]),
                  reads=[ob_, G["identb"]], writes=[pT])
        q4 = tt % 4
        cx.op("act", lambda a: a.activation(out=sg[:, :, q4 * 128:(q4 + 1) * 128], in_=pT[:, 0:512].rearrange("p (h t) -> p h t", h=4), func=AF.Copy),
              reads=[pT], writes=[sg])
        if q4 == 3:
            t0 = (tt - 3) * 128
            cx.dma("sp", G["oaT"].rearrange("(c p) t -> p c t", p=128)[:, :, t0:t0 + 512], sg[:], sg, reads=[sg])


def build():
    nc = bass.Bass("TRN2", target_bir_lowering=False)
    cx = Cx(nc)
    G = {}

    def din(name, shape, dt=F32):
        return nc.dram_tensor(name, list(shape), dt, kind="ExternalInput").ap()

    DBGT = {}

    def dscr(name, shape, dt):
        a = nc.dram_tensor(name, list(shape), dt, kind="Internal").ap()
        if name in DBG:
            DBGT[name] = (a, nc.dram_tensor("dbg_" + name, [shape[0], 256], dt, kind="ExternalOutput").ap())
        return a

    def finish():
        db = Buf("dbgcopy")
        for name, (a, o) in DBGT.items():
            cx.dma("sp", o, a[:, 0:256], db)
        cx.barrier()
        return nc

    xT = din("xT", [D, T])
    WSHAPE = {"ffn1_w_gate": [D, DFF], "ffn1_w_up": [D, DFF], "ffn1_w_down": [DFF, D], "w_in": [D, INC],
              "w_proj_attn": [512, D], "w_proj_ret": [4096, D], "w_out": [D, D],
              "ffn2_w_gate": [D, DFF], "ffn2_w_up": [D, DFF], "ffn2_w_down": [DFF, D]}
    USED_W.clear()

    class _WL:
        def __init__(self, nm):
            self.nm = nm

        def __getitem__(self, l):
            key = "%s_%d" % (self.nm, l)
            if key not in USED_W:
                USED_W[key] = (self.nm, l, din(key, WSHAPE[self.nm]))
            return USED_W[key][2]

    W = {nm: _WL(nm) for nm in WSHAPE}
    cst_d = din("cst", [128, CW])
    G["ropeA"] = din("ropeA", [2, 128, T])
    G["ropeR"] = din("ropeR", [2, 128, T])
    yT = nc.dram_tensor("yT", [D, T], F32, kind="ExternalOutput").ap()

    XA = dscr("XA", [D, T], F32)
    XB = dscr("XB", [D, T], F32)
    uT = dscr("uT", [DFF, T], BF16)
    for nm, rows in (("qaT", 1536), ("kaT", 1536), ("vaT", 1536), ("qrT", 2048), ("krT", 2048), ("vrT", 4096), ("sgT", 4096),
                     ("gaT", 2048), ("gbT", 2048), ("oaT", 512), ("orT", 4096), ("mT", 2048)):
        G[nm] = dscr(nm, [rows, T], BF16)
    G["oacc"] = [dscr("oacc%d" % g, [T, 512], F32) for g in range(3)]
    G["mlacc"] = dscr("mlacc", [T, 24], F32)
    EX = []
    G["exS_in"], G["exS_out"] = [], []
    for i in range(4):
        a = nc.dram_tensor("exS_in%d" % i, [512, 512], F32)
        b = nc.dram_tensor("exS_out%d" % i, [4 * 512, 512], F32)
        EX.append((a, b))
        G["exS_in"].append(a.ap())
        G["exS_out"].append(b.ap())
    G["exT2"], G["exT2o"] = [], []
    for j in range(4):
        a = nc.dram_tensor("exT2_%d" % j, [256, 2048], BF16)
        b = nc.dram_tensor("exT2o_%d" % j, [4 * 256, 2048], BF16)
        EX.append((a, b))
        G["exT2"].append(a.ap())
        G["exT2o"].append(b.ap())
    a = nc.dram_tensor("exT1", [1024, 512], BF16)
    b = nc.dram_tensor("exT1o", [4 * 1024, 512], BF16)
    EX.append((a, b))
    G["exT1"], G["exT1o"] = a.ap(), b.ap()
    a = nc.dram_tensor("exT0", [1024, 128], BF16)
    b = nc.dram_tensor("exT0o", [4 * 1024, 128], BF16)
    EX.append((a, b))
    G["exT0"], G["exT0o"] = a.ap(), b.ap()
    cc_sems = [nc.alloc_semaphore("cc%d" % i) for i in range(len(EX) * DEPTH)]
    ccn = [0]

    cst = nc.alloc_sbuf_tensor("cst_sb", [128, CW], F32)
    identb = nc.alloc_sbuf_tensor("identb", [128, 128], BF16)
    ones = nc.alloc_sbuf_tensor("ones", [128, 128], BF16)
    eps = nc.alloc_sbuf_tensor("eps", [128, 2], F32)
    G["cst"], G["ident"], G["ones"], G["eps"] = cst, identb, ones, eps
    G["cstb"], G["identb"], G["onesb"] = Buf("cst"), Buf("ident"), Buf("ones")
    cx.dma("sp", cst[:], cst_d, G["cstb"], writes=[G["cstb"]])
    cx.op("dve", lambda e: e.tensor_copy(out=identb[:], in_=cst[:, C_ID:C_ID + 128]), reads=[G["cstb"]], writes=[G["identb"]])
    cx.op("pool", lambda e: e.memset(ones[:], 1.0), writes=[G["onesb"]])
    cx.op("pool", lambda e: e.memset(eps[:, 0:1], RMS_EPS), writes=[G["cstb"]])
    cx.op("pool", lambda e: e.memset(eps[:, 1:2], GN_EPS), writes=[G["cstb"]])
    cx.barrier()

    def ffn(l, which, xi, xo_, gidx):
        wg, wu, wd = W["ffn%d_w_gate" % which][l], W["ffn%d_w_up" % which][l], W["ffn%d_w_down" % which][l]
        groups = []
        for s in range(DFF // 256):
            groups.append(dict(outs=[(wg[:, s * 256:(s + 1) * 256], 0), (wu[:, s * 256:(s + 1) * 256], 0)],
                               units=[[(0, 0), (1, 0)], [(0, 128), (1, 128)]], kind="swiglu", dst=uT, row0=s * 256))
        with cx.phase("ffn_up"):
            linear_phase(cx, G, 2048, 256, groups, norm=(xi, gidx), need=("stgA", "tmp"))
        if STOP == "ffn_up":
            return True
        with cx.phase("ffn_down"):
            linear_phase(cx, G, 1024, 0, simple_groups(wd, D, "res", xo_, xin=xi, scale=0.5), xsrc=[uT], need=("res",))
        return False

    def exchange(part):
        sems = []
        for (i_, o_) in part:
            ins = nc.gpsimd.collective_compute("AllGather", ALU.bypass, replica_groups=[[0, 1, 2, 3], [4, 5, 6, 7]],
                                               ins=[i_.ap().opt()], outs=[o_.ap().opt()])
            sem = cc_sems[ccn[0]]
            ccn[0] += 1
            ins.then_inc(sem)
            nc.gpsimd.wait_ge(sem, 1)
            sems.append(sem)
        return sems

    def wait_cc(sems):
        for sem in sems:
            for e in ("pe", "act", "dve", "sp"):
                cx.eng[e].wait_ge(sem, 1)

    xcur, xnxt = xT, XA

    def done():
        return nc

    for l in range(DEPTH):
        if not ("ffn1" in SKIP and l == 0):
            if ffn(l, 1, xcur, xnxt, l * 3 + 0):
                return finish()
            xcur, xnxt = xnxt, (XB if xnxt is XA else XA)
        if STOP == "ffn1":
            return finish()
        wi = W["w_in"][l]
        groups = []
        for i in range(6):
            groups.append(dict(outs=[(wi[:, O_QA + i * 256:O_QA + (i + 1) * 256], 0)], units=[[(0, 0), (0, 128)]], kind="ropeA", dst=G["qaT"], row0=i * 256))
        for i in range(6):
            groups.append(dict(outs=[(wi[:, O_KA + i * 256:O_KA + (i + 1) * 256], 0)], units=[[(0, 0), (0, 128)]], kind="ropeA", dst=G["kaT"], row0=i * 256))
        for i in range(8):
            groups.append(dict(outs=[(wi[:, O_QR + i * 256:O_QR + (i + 1) * 256], 0)], units=[[(0, 0), (0, 128)]], kind="ropeR", dst=G["qrT"], row0=i * 256))
        for i in range(8):
            groups.append(dict(outs=[(wi[:, O_KR + i * 256:O_KR + (i + 1) * 256], 0)], units=[[(0, 0), (0, 128)]], kind="ropeR", dst=G["krT"], row0=i * 256))
        groups += simple_groups(wi, 1536, "copy", G["vaT"], col0=O_VA)
        groups += simple_groups(wi, 4096, "copy", G["vrT"], col0=O_VR)
        groups += simple_groups(wi, 4096, "silu", G["sgT"], col0=O_GR)
        groups += simple_groups(wi, 2048, "sigmoid", G["gaT"], col0=O_GA)
        groups += simple_groups(wi, 2048, "sigmoid", G["gbT"], col0=O_GB)
        if "w_in" not in SKIP:
            with cx.phase("w_in"):
                linear_phase(cx, G, 2048, 256, groups, norm=(xcur, l * 3 + 1), need=("stgA", "stgB", "tmp", "rope"))
        if STOP == "w_in":
            return finish()
        with cx.phase("tails"):
            tb_ = Buf("tails")
            for j in range(2):
                cx.dma("sp", G["exT2"][j], G["kaT"][1024 + j * 256:1024 + (j + 1) * 256, T - 2048:T], tb_)
                cx.dma("sp", G["exT2"][2 + j], G["vaT"][1024 + j * 256:1024 + (j + 1) * 256, T - 2048:T], tb_)
            cx.dma("sp", G["exT1"][0:512, :], G["kaT"][512:1024, T - 512:T], tb_)
            cx.dma("sp", G["exT1"][512:1024, :], G["vaT"][512:1024, T - 512:T], tb_)
            cx.dma("sp", G["exT0"][0:512, :], G["kaT"][0:512, T - 128:T], tb_)
            cx.dma("sp", G["exT0"][512:1024, :], G["vaT"][0:512, T - 128:T], tb_)
        tsems = exchange(EX[4:10]) if "exch" not in SKIP else []
        with cx.phase("ret1"):
            retention_local_state(cx, G)
        if STOP == "ret1":
            return finish()
        ssems = exchange(EX[0:4]) if "exch" not in SKIP else []
        wait_cc(tsems)
        if STOP == "exch":
            wait_cc(ssems)
            return finish()
        with cx.phase("attn"):
            attention_phase(cx, G)
        with cx.phase("merge"):
            attn_merge_phase(cx, G)
        wait_cc(ssems)
        with cx.phase("ret2"):
            Sf = cx.sb([128, 8, 2, 512], F32, "Sf")
            Sb = cx.sb([128, 8, 2, 512], BF16, "Sb")
            with ExitStack() as st2:
                old = cx.st
                cx.st = st2
                state_init(cx, G, (Sf, Sb), False)
                cx.barrier()
                cx.st = old
            retention_pass(cx, G, (Sf, Sb), False)
        if STOP in ("ret", "attn"):
            return finish()
        pa, pr = W["w_proj_attn"][l], W["w_proj_ret"][l]
        groups = []
        for s in range(D // 256):
            groups.append(dict(outs=[(pa[:, s * 256:(s + 1) * 256], 0), (pr[:, s * 256:(s + 1) * 256], 4)],
                               units=[[(0, 0), (1, 0)], [(0, 128), (1, 128)]], kind="gate", dst=G["mT"], row0=s * 256, ga=G["gaT"], gb=G["gbT"]))
        with cx.phase("proj"):
            linear_phase(cx, G, 1024, 0, groups, xsrc=[G["oaT"], G["orT"]], need=("stgA", "tmp", "gate"))
        if STOP == "proj":
            return finish()
        with cx.phase("w_out"):
            linear_phase(cx, G, 2048, 0, simple_groups(W["w_out"][l], D, "res", xnxt, xin=xcur, scale=1.0), xsrc=[G["mT"]], need=("res",))
        xcur, xnxt = xnxt, (XB if xnxt is XA else XA)
        if STOP == "w_out":
            return finish()
        if ffn(l, 2, xcur, xnxt, l * 3 + 2):
            return finish()
        xcur, xnxt = xnxt, (XB if xnxt is XA else XA)
        if STOP == "layer0":
            return finish()
    with cx.phase("final"):
        final_norm_phase(cx, G, xcur, 6, yT)
    print("kernel build: instructions ~", cx.ninst, "counts", cx.cnt)
    return nc


def _consts(qd):
    c = np.zeros((128, CW), np.float64)
    c[:, C_ID:C_ID + 128] = np.eye(128)
    i = np.arange(128)
    for h in range(8):
        gm = GAMMA[h]
        rel = i[None, :] - i[:, None]
        m = np.where(rel >= 0, gm ** np.maximum(rel, 0), 0.0) * (256.0 ** -0.5)
        c[:, C_RMASK + h * 128:C_RMASK + (h + 1) * 128] = m
        c[:, C_RDQ + h * 128:C_RDQ + (h + 1) * 128] = (gm ** (i + 1.0))[None, :]
        c[:, C_RDK + h] = gm ** (127.0 - i) * (256.0 ** -0.5)
        for n in range(32):
            dk = gm ** (T - 1.0 - (n * 128 + i)) * (256.0 ** -0.5)
            c[:, C_DKABS + n * 8 + h] = np.where(dk < 1e-30, 0.0, dk)
        for r in range(4):
            c[:, C_COEF + r * 8 + h] = (gm ** (4096.0 * (qd - r - 1))) if r < qd else 0.0
    for r in range(4):
        c[:, C_SEL + r] = 1.0 if r == qd - 1 else 0.0
    qi = i[:, None]
    kj = np.arange(256)[None, :]
    band = np.where((kj >= qi) & (kj <= qi + 128), 0.0, NEG)
    c[:, C_AMASK:C_AMASK + 256] = band
    first = band.copy()
    if qd == 0:
        first[:, 0:128] = NEG
    c[:, C_AMASK + 256:C_AMASK + 512] = first
    return c


def _rope(qd, dim, reps):
    inv = (1.0 / (10000.0 ** (np.arange(0, dim, 2, dtype=np.float32) / np.float32(dim)))).astype(np.float32)
    pos = (np.arange(T, dtype=np.float32) + np.float32(qd * T)).astype(np.float32)
    ang = (pos[None, :] * inv[:, None]).astype(np.float32)
    cs = np.stack([np.cos(ang), np.sin(ang)]).astype(np.float32)
    return np.ascontiguousarray(np.tile(cs, (1, reps, 1)))


def _perm_w_in(w_in):
    w = np.array(w_in, copy=True)
    for base in (0, 1536):
        blk = w_in[:, :, base:base + 1536].reshape(DEPTH, D, 6, 2, 2, 64)
        w[:, :, base:base + 1536] = blk.transpose(0, 1, 2, 4, 3, 5).reshape(DEPTH, D, 1536)
    return w


def kernel(x, ffn1_norm, ffn1_w_gate, ffn1_w_up, ffn1_w_down, mix_norm, w_in, w_proj_attn, w_proj_ret, w_out,
           ffn2_norm, ffn2_w_gate, ffn2_w_up, ffn2_w_down, final_norm):
    x = np.asarray(x, np.float32)
    gains = [ffn1_norm[0], mix_norm[0], ffn2_norm[0], ffn1_norm[1], mix_norm[1], ffn2_norm[1], final_norm]
    gcols = np.concatenate([np.asarray(g, np.float32).reshape(16, 128).T for g in gains], axis=1)
    wperm = _perm_w_in(np.asarray(w_in, np.float32))
    allw = {"ffn1_w_gate": ffn1_w_gate, "ffn1_w_up": ffn1_w_up, "ffn1_w_down": ffn1_w_down, "w_in": wperm,
            "w_proj_attn": w_proj_attn, "w_proj_ret": w_proj_ret, "w_out": w_out,
            "ffn2_w_gate": ffn2_w_gate, "ffn2_w_up": ffn2_w_up, "ffn2_w_down": ffn2_w_down}
    nc = build()
    shared = {key: np.ascontiguousarray(np.asarray(allw[nm][l], np.float32)) for key, (nm, l, _) in USED_W.items()}
    in_maps = []
    for c in range(NCORES):
        b, qd = c // 4, c % 4
        m = dict(shared)
        m["xT"] = np.ascontiguousarray(x[b, qd * T:(qd + 1) * T, :].T)
        if XOVR is not None and c == 0:
            m["xT"][:, :XOVR.shape[0]] = XOVR.T
        cc = _consts(qd)
        cc[:, C_GAIN:C_GAIN + 112] = gcols
        m["cst"] = cc.astype(np.float32)
        m["ropeA"] = _rope(qd, 128, 2)
        m["ropeR"] = _rope(qd, 256, 1)
        in_maps.append(m)
    res = run_bass_kernel_spmd(nc, in_maps, core_ids=list(range(NCORES)))
    kernel.last = res
    out = np.empty((2, 4 * T, D), np.float32)
    for c in range(NCORES):
        b, qd = c // 4, c % 4
        out[b, qd * T:(qd + 1) * T, :] = res.results[c]["yT"].T
    return out
```

```python
import os
from contextlib import ExitStack, contextmanager
import numpy as np
import concourse.bass as bass
import concourse.mybir as mybir
from concourse.bass_utils import run_bass_kernel_spmd

F32 = mybir.dt.float32
BF16 = mybir.dt.bfloat16
ALU = mybir.AluOpType
AF = mybir.ActivationFunctionType
AX = mybir.AxisListType

NCORES = 8
T = 4096
D = 2048
DFF = 5632
DEPTH = 2
INC = 20992
NEG = -1.0e30
RMS_EPS = 1e-6
GN_EPS = 1e-5
GAMMA = [1.0 - 2.0 ** (-5.0 - h) for h in range(8)]
O_QA, O_KA, O_VA, O_QR, O_KR, O_VR, O_GR, O_GA, O_GB = 0, 1536, 3072, 4608, 6656, 8704, 12800, 16896, 18944
C_ID = 0
C_RMASK = 128
C_RDQ = C_RMASK + 1024
C_RDK = C_RDQ + 1024
C_COEF = C_RDK + 8
C_SEL = C_COEF + 32
C_AMASK = C_SEL + 4
C_GAIN = C_AMASK + 512
C_DKABS = C_GAIN + 112
CW = C_DKABS + 256

STOP = os.environ.get("KSTOP", "")
SKIP = os.environ.get("KSKIP", "").split(",")
USED_W = {}
DBG = [s for s in os.environ.get("KDBG", "").split(",") if s]


class Buf:
    __slots__ = ("name", "w", "r", "ds")

    def __init__(self, name):
        self.name = name
        self.w = {}
        self.r = {}
        self.ds = None


def _merge(d, src):
    for k, (sem, v) in src.items():
        if k not in d or d[k][1] < v:
            d[k] = (sem, v)


class Tl:
    def __init__(self, h, name, psum=False):
        self.h = h
        self.b = Buf(name)
        self.psum = psum

    def __getitem__(self, k):
        return self.h[k]


class Cx:
    def __init__(self, nc):
        self.nc = nc
        self.eng = {"pe": nc.tensor, "act": nc.scalar, "dve": nc.vector, "pool": nc.gpsimd, "sp": nc.sync}
        self.psem = {}
        self.cnt = {}
        for e in ["pe", "act", "dve", "pool"]:
            self.psem[e] = nc.alloc_semaphore("s_" + e)
            self.cnt[e] = 0
        self.known = {e: {} for e in self.eng}
        self.dpool = [[nc.alloc_semaphore("d%d" % i), 0, "d%d" % i] for i in range(44)]
        self.dfree = list(range(8, 44))
        self.dfree_sw = list(range(8))
        self.dused = []
        self.uid = 0
        self.st = None
        self.ninst = 0

    def sb(self, shape, dtype, name="t"):
        self.uid += 1
        nm = "%s_%d" % (name, self.uid)
        h = self.st.enter_context(self.nc.sbuf_tensor(nm, list(shape), dtype))
        return Tl(h, nm)

    def ps(self, shape, dtype, name="p"):
        self.uid += 1
        nm = "%s_%d" % (name, self.uid)
        h = self.st.enter_context(self.nc.psum_tensor(nm, list(shape), dtype))
        return Tl(h, nm, psum=True)

    def _dsem(self, buf, sw=False):
        if buf.ds is None:
            fl = self.dfree_sw if sw else self.dfree
            i = fl.pop(0)
            buf.ds = self.dpool[i]
            self.dused.append((i, buf, sw))
        return buf.ds

    def _wait(self, e, toks):
        for name, (sem, val) in toks.items():
            if e == "pe" and name == "s_pe":
                continue
            if self.known[e].get(name, 0) < val:
                self.eng[e].wait_ge(sem, val)
                self.known[e][name] = val
                self.ninst += 1

    def _deps(self, reads, writes):
        deps = {}
        for b in reads:
            _merge(deps, b.w)
        for b in writes:
            _merge(deps, b.w)
            _merge(deps, b.r)
        return deps

    def _mark(self, tokname, tok, reads, writes):
        t = {tokname: tok}
        for b in writes:
            _merge(b.w, t)
            b.r = {}
        for b in reads:
            _merge(b.r, t)

    def op(self, e, fn, reads=(), writes=()):
        writes = list(writes) + [x for x in reads if isinstance(x, Tl) and x.psum]
        reads = [x.b if isinstance(x, Tl) else x for x in reads if not (isinstance(x, Tl) and x.psum)]
        writes = [x.b if isinstance(x, Tl) else x for x in writes]
        self._wait(e, self._deps(reads, writes))
        ins = fn(self.eng[e])
        self.cnt[e] += 1
        ins.then_inc(self.psem[e], 1)
        self.ninst += 1
        self._mark("s_" + e, (self.psem[e], self.cnt[e]), reads, writes)

    def dma(self, e, out, in_, sbuf, reads=(), writes=()):
        reads = [x.b if isinstance(x, Tl) else x for x in reads]
        writes = [x.b if isinstance(x, Tl) else x for x in writes]
        sb = sbuf.b if isinstance(sbuf, Tl) else sbuf
        self._wait(e, self._deps(reads, writes))
        ds = self._dsem(sb, e == "pool")
        ins = self.eng[e].dma_start(out=out, in_=in_)
        ds[1] += 16
        ins.then_inc(ds[0], 16)
        self.ninst += 1
        self._mark(ds[2], (ds[0], ds[1]), reads, writes)

    def barrier(self):
        allt = {}
        for e in self.psem:
            if self.cnt[e] > 0:
                allt["s_" + e] = (self.psem[e], self.cnt[e])
        for s, c, n in self.dpool:
            if c > 0:
                allt[n] = (s, c)
        for e in self.eng:
            self._wait(e, allt)
        for i, b, sw in self.dused:
            b.ds = None
            (self.dfree_sw if sw else self.dfree).append(i)
        self.dused = []

    @contextmanager
    def phase(self, name):
        st = ExitStack()
        self.st = st
        try:
            yield st
        finally:
            self.barrier()
            st.close()


def linear_phase(cx, G, TB, NS, groups, xsrc=None, norm=None, need=()):
    nc = cx.nc
    KC = 16 if norm is not None else sum(a.shape[0] for a in xsrc) // 128
    NSUB = TB // 512
    X = cx.sb([128, KC, TB], BF16, "X")
    XBs = [Buf("X%d" % i) for i in range(NSUB)]
    maxk = max(max(w.shape[0] // 128 for (w, _) in g["outs"]) for g in groups)
    nouts = max(len(g["outs"]) for g in groups)
    slabs = [[cx.sb([128, maxk, 256], BF16, "slab") for _ in range(nouts)] for _ in range(2)]
    NPB = 3
    banks = [[cx.ps([128, 512], F32, "bk") for _ in range(2)] for _ in range(NPB)]
    stgA = [cx.sb([128, TB], BF16, "stgA") for _ in range(2)] if "stgA" in need else None
    stgB = [cx.sb([128, TB], BF16, "stgB") for _ in range(2)] if "stgB" in need else None
    tmp = [cx.sb([128, 512], F32, "tmp") for _ in range(4)] if "tmp" in need else None
    xin = [cx.sb([128, TB], F32, "xin") for _ in range(2)] if "res" in need else None
    xo = [cx.sb([128, TB], F32, "xo") for _ in range(2)] if "res" in need else None
    gta = [cx.sb([128, TB], BF16, "gta") for _ in range(2)] if "gate" in need else None
    gtb = [cx.sb([128, TB], BF16, "gtb") for _ in range(2)] if "gate" in need else None
    rtab = cx.sb([128, 4, TB], F32, "rtab") if "rope" in need else None
    if norm is not None:
        xs32 = [cx.sb([128, 16, NS], F32, "xs32") for _ in range(2)]
        sq = [cx.sb([128, 16, NS], BF16, "sq") for _ in range(2)]
        psn = [cx.ps([128, 512], F32, "psn") for _ in range(2)]
        rs0 = [cx.sb([128, NS], F32, "rs0") for _ in range(2)]
        rstd = [cx.sb([128, NS], F32, "rstd") for _ in range(2)]
    rot = {"bank": 0, "stg": 0, "tmp": 0, "res": 0, "gate": 0}

    def load_slabs(gi):
        g = groups[gi]
        for oi, (w, kc0) in enumerate(g["outs"]):
            sl = slabs[gi % 2][oi]
            nk = w.shape[0] // 128
            cx.dma("pool", sl[:, 0:nk, :], w.rearrange("(c p) f -> p c f", p=128), sl, writes=[sl])

    for tb in range(T // TB):
        t0 = tb * TB
        if norm is not None:
            x_ap, gidx = norm

            def nA(s_):
                x3, sq_ = xs32[s_ % 2], sq[s_ % 2]
                ts0 = t0 + s_ * NS
                cx.dma("sp", x3[:], x_ap.rearrange("(c p) t -> p c t", p=128)[:, :, ts0:ts0 + NS], x3, writes=[x3])
                cx.op("act", lambda a: a.activation(out=sq_[:], in_=x3[:], func=AF.Square), reads=[x3], writes=[sq_])

            def nB(s_):
                sq_, pn, r0, rd = sq[s_ % 2], psn[s_ % 2], rs0[s_ % 2], rstd[s_ % 2]
                for c in range(16):
                    cx.op("pe", lambda p, c=c: p.matmul(pn[:, 0:NS], lhsT=G["ones"][:], rhs=sq_[:, c, :], start=(c == 0), stop=(c == 15)),
                          reads=[sq_, G["onesb"]], writes=[pn])
                cx.op("act", lambda a: a.activation(out=r0[:], in_=pn[:, 0:NS], func=AF.Sqrt, scale=1.0 / D, bias=G["eps"][:, 0:1]),
                      reads=[pn, G["cstb"]], writes=[r0])
                cx.op("dve", lambda v: v.reciprocal(out=rd[:], in_=r0[:]), reads=[r0], writes=[rd])

            def nC(s_):
                x3, rd = xs32[s_ % 2], rstd[s_ % 2]
                for c in range(16):
                    cx.op("dve", lambda v, c=c: v.scalar_tensor_tensor(
                        out=X[:, c, s_ * NS:(s_ + 1) * NS], in0=x3[:, c, :],
                        scalar=G["cst"][:, C_GAIN + gidx * 16 + c:C_GAIN + gidx * 16 + c + 1],
                        in1=rd[:], op0=ALU.mult, op1=ALU.mult), reads=[x3, rd, G["cstb"]], writes=[XBs[(s_ * NS) // 512]])

            skew(TB // NS, [(nA, 0), (nB, 1), (nC, 1)])
        else:
            for ts_ in range(NSUB):
                kc = 0
                for a in xsrc:
                    n = a.shape[0] // 128
                    av = a.rearrange("(c p) t -> p c t", p=128)
                    for c0 in range(0, n, 8):
                        c1 = min(n, c0 + 8)
                        cx.dma("sp", X[:, kc + c0:kc + c1, ts_ * 512:(ts_ + 1) * 512], av[:, c0:c1, t0 + ts_ * 512:t0 + (ts_ + 1) * 512],
                               XBs[ts_], writes=[XBs[ts_]])
                    kc += n
        if rtab is not None:
            ra, rr = G["ropeA"], G["ropeR"]
            cx.dma("sp", rtab[:, 0:2, :], ra.rearrange("k p t -> p k t")[:, :, t0:t0 + TB], rtab, writes=[rtab])
            cx.dma("sp", rtab[:, 2:4, :], rr.rearrange("k p t -> p k t")[:, :, t0:t0 + TB], rtab, writes=[rtab])
        load_slabs(0)
        for gi, g in enumerate(groups):
            if gi + 1 < len(groups):
                load_slabs(gi + 1)
            kind = g["kind"]
            sl = slabs[gi % 2]
            for ui, unit in enumerate(g["units"]):
                if kind in ("copy", "silu", "sigmoid", "swiglu", "ropeA", "ropeR", "gate"):
                    sA = stgA[rot["stg"] % 2]
                    sB = stgB[rot["stg"] % 2] if stgB is not None else None
                    rot["stg"] += 1
                if kind == "res":
                    xi = xin[rot["res"] % 2]
                    xoo = xo[rot["res"] % 2]
                    rot["res"] += 1
                    r0 = g["row0"] + ui * 128
                    cx.dma("sp", xi[:], g["xin"][r0:r0 + 128, t0:t0 + TB], xi, writes=[xi])
                if kind == "gate":
                    ga_t = gta[rot["gate"] % 2]
                    gb_t = gtb[rot["gate"] % 2]
                    rot["gate"] += 1
                    r0 = g["row0"] + ui * 128
                    cx.dma("sp", ga_t[:], g["ga"][r0:r0 + 128, t0:t0 + TB], ga_t, writes=[ga_t])
                    cx.dma("sp", gb_t[:], g["gb"][r0:r0 + 128, t0:t0 + TB], gb_t, writes=[gb_t])
                for ts in range(NSUB):
                    bk = banks[rot["bank"] % NPB]
                    rot["bank"] += 1
                    cs = slice(ts * 512, (ts + 1) * 512)
                    for j, (oi, col) in enumerate(unit):
                        w, kc0 = g["outs"][oi]
                        nk = w.shape[0] // 128
                        for kc in range(nk):
                            cx.op("pe", lambda p, j=j, oi=oi, col=col, kc=kc, kc0=kc0, nk=nk: p.matmul(
                                bk[j][:], lhsT=sl[oi][:, kc, col:col + 128], rhs=X[:, kc0 + kc, cs],
                                start=(kc == 0), stop=(kc == nk - 1)), reads=[sl[oi], XBs[ts]], writes=[bk[j]])
                    if kind in ("copy", "silu", "sigmoid"):
                        fn = {"copy": AF.Copy, "silu": AF.Silu, "sigmoid": AF.Sigmoid}[kind]
                        cx.op("act", lambda a, fn=fn: a.activation(out=sA[:, cs], in_=bk[0][:], func=fn), reads=[bk[0]], writes=[sA])
                    elif kind == "swiglu":
                        tm = tmp[rot["tmp"] % 4]
                        rot["tmp"] += 1
                        cx.op("act", lambda a: a.activation(out=tm[:], in_=bk[0][:], func=AF.Silu), reads=[bk[0]], writes=[tm])
                        cx.op("dve", lambda v: v.tensor_tensor(out=sA[:, cs], in0=bk[1][:], in1=tm[:], op=ALU.mult), reads=[bk[1], tm], writes=[sA])
                    elif kind in ("ropeA", "ropeR"):
                        k0 = 0 if kind == "ropeA" else 2
                        cc = rtab[:, k0, cs]
                        ss = rtab[:, k0 + 1, cs]
                        t1, t2, t3, t4 = [tmp[(rot["tmp"] + i) % 4] for i in range(4)]
                        rot["tmp"] += 4
                        A, B = bk[0], bk[1]
                        cx.op("dve", lambda v: v.tensor_tensor(out=t1[:], in0=A[:], in1=cc, op=ALU.mult), reads=[A, rtab], writes=[t1])
                        cx.op("dve", lambda v: v.tensor_tensor(out=t2[:], in0=B[:], in1=ss, op=ALU.mult), reads=[B, rtab], writes=[t2])
                        cx.op("pool", lambda v: v.tensor_tensor(out=sA[:, cs], in0=t1[:], in1=t2[:], op=ALU.subtract), reads=[t1, t2], writes=[sA])
                        cx.op("dve", lambda v: v.tensor_tensor(out=t3[:], in0=B[:], in1=cc, op=ALU.mult), reads=[B, rtab], writes=[t3])
                        cx.op("dve", lambda v: v.tensor_tensor(out=t4[:], in0=A[:], in1=ss, op=ALU.mult), reads=[A, rtab], writes=[t4])
                        cx.op("pool", lambda v: v.tensor_tensor(out=sB[:, cs], in0=t3[:], in1=t4[:], op=ALU.add), reads=[t3, t4], writes=[sB])
                    elif kind == "res":
                        sc = g["scale"]
                        cx.op("dve", lambda v: v.scalar_tensor_tensor(out=xoo[:, cs], in0=bk[0][:], scalar=sc, in1=xi[:, cs],
                                                                      op0=ALU.mult, op1=ALU.add), reads=[bk[0], xi], writes=[xoo])
                    elif kind == "gate":
                        t1, t2 = [tmp[(rot["tmp"] + i) % 4] for i in range(2)]
                        rot["tmp"] += 2
                        cx.op("dve", lambda v: v.tensor_tensor(out=t1[:], in0=bk[0][:], in1=ga_t[:, cs], op=ALU.mult), reads=[bk[0], ga_t], writes=[t1])
                        cx.op("dve", lambda v: v.tensor_tensor(out=t2[:], in0=bk[1][:], in1=gb_t[:, cs], op=ALU.mult), reads=[bk[1], gb_t], writes=[t2])
                        cx.op("pool", lambda v: v.tensor_tensor(out=sA[:, cs], in0=t1[:], in1=t2[:], op=ALU.add), reads=[t1, t2], writes=[sA])
                tsl = slice(t0, t0 + TB)
                if kind in ("copy", "silu", "sigmoid", "swiglu", "gate"):
                    r0 = g["row0"] + ui * 128
                    cx.dma("sp", g["dst"][r0:r0 + 128, tsl], sA[:], sA, reads=[sA])
                elif kind == "ropeR":
                    r0 = g["row0"]
                    cx.dma("sp", g["dst"][r0:r0 + 128, tsl], sA[:], sA, reads=[sA])
                    cx.dma("sp", g["dst"][r0 + 128:r0 + 256, tsl], sB[:], sB, reads=[sB])
                elif kind == "ropeA":
                    r0 = g["row0"]
                    cx.dma("sp", g["dst"][r0:r0 + 64, tsl], sA[0:64, :], sA, reads=[sA])
                    cx.dma("sp", g["dst"][r0 + 128:r0 + 192, tsl], sA[64:128, :], sA, reads=[sA])
                    cx.dma("sp", g["dst"][r0 + 64:r0 + 128, tsl], sB[0:64, :], sB, reads=[sB])
                    cx.dma("sp", g["dst"][r0 + 192:r0 + 256, tsl], sB[64:128, :], sB, reads=[sB])
                elif kind == "res":
                    r0 = g["row0"] + ui * 128
                    cx.dma("sp", g["dst"][r0:r0 + 128, tsl], xoo[:], xoo, reads=[xoo])


def skew(n, stages):
    md = max(d for _, d in stages)
    for t in range(n + md):
        for fn, d in stages:
            i = t - d
            if 0 <= i < n:
                fn(i)


def simple_groups(w, ncols, kind, dst, col0=0, row0=0, **kw):
    gs = []
    for s in range(ncols // 256):
        g = dict(outs=[(w[:, col0 + s * 256:col0 + (s + 1) * 256], 0)], units=[[(0, 0)], [(0, 128)]], kind=kind, dst=dst,
                 row0=row0 + s * 256)
        g.update(kw)
        gs.append(g)
    return gs


def final_norm_phase(cx, G, x_ap, gidx, y_ap):
    NS = 256
    xs32 = [cx.sb([128, 16, NS], F32, "xs32") for _ in range(2)]
    sq = cx.sb([128, 16, NS], BF16, "sq")
    psn = cx.ps([128, 512], F32, "psn")
    rs0 = cx.sb([128, NS], F32, "rs0")
    rstd = cx.sb([128, NS], F32, "rstd")
    yo = [cx.sb([128, 16, NS], F32, "yo") for _ in range(2)]
    for s in range(T // NS):
        ts0 = s * NS
        x3 = xs32[s % 2]
        y3 = yo[s % 2]
        cx.dma("sp", x3[:], x_ap.rearrange("(c p) t -> p c t", p=128)[:, :, ts0:ts0 + NS], x3, writes=[x3])
        cx.op("act", lambda a: a.activation(out=sq[:], in_=x3[:], func=AF.Square), reads=[x3], writes=[sq])
        for c in range(16):
            cx.op("pe", lambda p, c=c: p.matmul(psn[:, 0:NS], lhsT=G["ones"][:], rhs=sq[:, c, :], start=(c == 0), stop=(c == 15)),
                  reads=[sq, G["onesb"]], writes=[psn])
        cx.op("act", lambda a: a.activation(out=rs0[:], in_=psn[:, 0:NS], func=AF.Sqrt, scale=1.0 / D, bias=G["eps"][:, 0:1]),
              reads=[psn, G["cstb"]], writes=[rs0])
        cx.op("dve", lambda v: v.reciprocal(out=rstd[:], in_=rs0[:]), reads=[rs0], writes=[rstd])
        for c in range(16):
            cx.op("dve", lambda v, c=c: v.scalar_tensor_tensor(
                out=y3[:, c, :], in0=x3[:, c, :], scalar=G["cst"][:, C_GAIN + gidx * 16 + c:C_GAIN + gidx * 16 + c + 1],
                in1=rstd[:], op0=ALU.mult, op1=ALU.mult), reads=[x3, rstd, G["cstb"]], writes=[y3])
        cx.dma("sp", y_ap.rearrange("(c p) t -> p c t", p=128)[:, :, ts0:ts0 + NS], y3[:], y3, reads=[y3])


def retention_pass(cx, G, S, first):
    nc = cx.nc
    BT = 128
    cst = G["cst"]
    qB = [cx.sb([128, 16, BT], BF16, "rq") for _ in range(2)]
    kB = [cx.sb([128, 16, BT], BF16, "rk") for _ in range(2)]
    vB = [cx.sb([128, 32, BT], BF16, "rv") for _ in range(2)]
    sgB = [cx.sb([128, 32, BT], BF16, "rsg") for _ in range(2)]
    ostB = [cx.sb([128, 32, BT], BF16, "rost") for _ in range(2)]
    Sf, Sb = S
    psA = cx.ps([128, 512], F32, "psA")
    psO2 = [cx.ps([128, 512], F32, "psO") for _ in range(2)]
    psOT = cx.ps([128, 1024], BF16, "psOT")
    psT = [cx.ps([128, 1024], BF16, "psT") for _ in range(2)]
    psS = [cx.ps([128, 512], F32, "psS") for _ in range(2)]
    ident = G["ident"]
    assert not first
    atm = [cx.sb([128, 128], BF16, "atm") for _ in range(3)]
    qdec = [cx.sb([128, 2, 128], BF16, "qdec") for _ in range(3)]
    on = [cx.sb([128, 512], BF16, "on") for _ in range(4)]
    st6 = [cx.sb([128, 6], F32, "st6") for _ in range(4)]
    mv = [cx.sb([128, 2], F32, "mv") for _ in range(4)]
    sd = [cx.sb([128, 1], F32, "sd") for _ in range(4)]
    rs = [cx.sb([128, 1], F32, "rs") for _ in range(4)]
    nb = [cx.sb([128, 1], F32, "nb") for _ in range(4)]
    kdec = [cx.sb([128, 256], BF16, "kdec") for _ in range(3)]
    vtok = [cx.sb([128, 512], BF16, "vtok") for _ in range(3)]
    NBLK = T // BT

    def load_block(lb):
        tsl = slice(lb * BT, (lb + 1) * BT)
        q, k, v, sg = qB[lb % 2], kB[lb % 2], vB[lb % 2], sgB[lb % 2]
        cx.dma("sp", k[:], G["krT"].rearrange("(c p) t -> p c t", p=128)[:, :, tsl], k, writes=[k])
        cx.dma("sp", q[:], G["qrT"].rearrange("(c p) t -> p c t", p=128)[:, :, tsl], q, writes=[q])
        cx.dma("sp", v[:, 0:16, :], G["vrT"].rearrange("(c p) t -> p c t", p=128)[:, 0:16, tsl], v, writes=[v])
        cx.dma("sp", v[:, 16:32, :], G["vrT"].rearrange("(c p) t -> p c t", p=128)[:, 16:32, tsl], v, writes=[v])
        cx.dma("sp", sg[:, 0:16, :], G["sgT"].rearrange("(c p) t -> p c t", p=128)[:, 0:16, tsl], sg, writes=[sg])
        cx.dma("sp", sg[:, 16:32, :], G["sgT"].rearrange("(c p) t -> p c t", p=128)[:, 16:32, tsl], sg, writes=[sg])

    def stA(i):
        blk, h = divmod(i, 8)
        if i == 0:
            load_block(0)
        if h == 4 and blk + 1 < NBLK:
            load_block(blk + 1)
        q, k, v = qB[blk % 2], kB[blk % 2], vB[blk % 2]
        pT = psT[i % 2]
        for dc in range(2):
            cx.op("pe", lambda p, dc=dc: p.transpose(out=pT[:, dc * 128:(dc + 1) * 128], in_=k[:, 2 * h + dc, :], identity=ident[:]),
                  reads=[k, G["identb"]], writes=[pT])
        for ec in range(4):
            cx.op("pe", lambda p, ec=ec: p.transpose(out=pT[:, 256 + ec * 128:256 + (ec + 1) * 128], in_=v[:, 4 * h + ec, :], identity=ident[:]),
                  reads=[v, G["identb"]], writes=[pT])
        for dc in range(2):
            cx.op("pe", lambda p, dc=dc: p.matmul(psA[:, 0:128], lhsT=k[:, 2 * h + dc, :], rhs=q[:, 2 * h + dc, :], start=(dc == 0), stop=(dc == 1)),
                  reads=[k, q], writes=[psA])

    def stB(i):
        blk, h = divmod(i, 8)
        q = qB[blk % 2]
        pT, kd_, vt_, at_, qd_ = psT[i % 2], kdec[i % 3], vtok[i % 3], atm[i % 3], qdec[i % 3]
        cx.op("act", lambda a: a.activation(out=kd_[:], in_=pT[:, 0:256], func=AF.Copy, scale=cst[:, C_RDK + h:C_RDK + h + 1]),
              reads=[pT, G["cstb"]], writes=[kd_])
        cx.op("act", lambda a: a.activation(out=vt_[:], in_=pT[:, 256:768], func=AF.Copy), reads=[pT], writes=[vt_])
        cx.op("dve", lambda e: e.tensor_tensor(out=at_[:], in0=psA[:, 0:128], in1=cst[:, C_RMASK + h * 128:C_RMASK + (h + 1) * 128], op=ALU.mult),
              reads=[psA, G["cstb"]], writes=[at_])
        for dc in range(2):
            cx.op("pool", lambda e, dc=dc: e.tensor_tensor(out=qd_[:, dc, :], in0=q[:, 2 * h + dc, :], in1=cst[:, C_RDQ + h * 128:C_RDQ + (h + 1) * 128], op=ALU.mult),
                  reads=[q, G["cstb"]], writes=[qd_])

    def stC(i):
        blk, h = divmod(i, 8)
        kd_, vt_, at_, qd_ = kdec[i % 3], vtok[i % 3], atm[i % 3], qdec[i % 3]
        psO = psO2[i % 2]
        for dc in range(2):
            cx.op("pe", lambda p, dc=dc: p.matmul(psO[:], lhsT=qd_[:, dc, :], rhs=Sb[:, h, dc, :], start=(dc == 0), stop=False),
                  reads=[qd_, Sb], writes=[psO])
        cx.op("pe", lambda p: p.matmul(psO[:], lhsT=at_[:], rhs=vt_[:], start=False, stop=True), reads=[at_, vt_], writes=[psO])
        for dc in range(2):
            cx.op("pe", lambda p, dc=dc: p.matmul(psS[dc][:], lhsT=kd_[:, dc * 128:(dc + 1) * 128], rhs=vt_[:], start=True, stop=True),
                  reads=[kd_, vt_], writes=[psS[dc]])

    def stD(i):
        blk, h = divmod(i, 8)
        psO = psO2[i % 2]
        on_ = on[i % 4]
        cd = GAMMA[h] ** 128
        for dc in range(2):
            cx.op("dve", lambda e, dc=dc: e.scalar_tensor_tensor(out=Sf[:, h, dc, :], in0=Sf[:, h, dc, :], scalar=cd, in1=psS[dc][:], op0=ALU.mult, op1=ALU.add),
                  reads=[psS[dc]], writes=[Sf])
        for dc in range(2):
            cx.op("act", lambda a, dc=dc: a.activation(out=Sb[:, h, dc, :], in_=Sf[:, h, dc, :], func=AF.Copy), reads=[Sf], writes=[Sb])
        s6, mv_, sd_, rs_, nb_ = st6[i % 4], mv[i % 4], sd[i % 4], rs[i % 4], nb[i % 4]
        cx.op("dve", lambda e: e.bn_stats(out=s6[:], in_=psO[:]), reads=[psO], writes=[s6])
        cx.op("dve", lambda e: e.bn_aggr(out=mv_[:], in_=s6[:]), reads=[s6], writes=[mv_])
        cx.op("act", lambda a: a.activation(out=sd_[:], in_=mv_[:, 1:2], func=AF.Sqrt, bias=G["eps"][:, 1:2]), reads=[mv_, G["cstb"]], writes=[sd_])
        cx.op("dve", lambda e: e.reciprocal(out=rs_[:], in_=sd_[:]), reads=[sd_], writes=[rs_])
        cx.op("dve", lambda e: e.scalar_tensor_tensor(out=nb_[:], in0=mv_[:, 0:1], scalar=-1.0, in1=rs_[:], op0=ALU.mult, op1=ALU.mult),
              reads=[mv_, rs_], writes=[nb_])
        cx.op("act", lambda a: a.activation(out=on_[:], in_=psO[:], func=AF.Identity, scale=rs_[:, 0:1], bias=nb_[:, 0:1]),
              reads=[psO, rs_, nb_], writes=[on_])

    def stE(i):
        on_ = on[i % 4]
        for ec in range(4):
            cx.op("pe", lambda p, ec=ec: p.transpose(out=psOT[:, ec * 128:(ec + 1) * 128], in_=on_[:, ec * 128:(ec + 1) * 128], identity=ident[:]),
                  reads=[on_, G["identb"]], writes=[psOT])

    def stF(i):
        blk, h = divmod(i, 8)
        sg, ost = sgB[blk % 2], ostB[blk % 2]
        cx.op("dve", lambda e: e.tensor_tensor(out=ost[:, 4 * h:4 * h + 4, :], in0=psOT[:, 0:512].rearrange("p (e t) -> p e t", e=4),
                                               in1=sg[:, 4 * h:4 * h + 4, :], op=ALU.mult), reads=[psOT, sg], writes=[ost])
        if h == 7:
            tsl = slice(blk * BT, (blk + 1) * BT)
            cx.dma("sp", G["orT"].rearrange("(c p) t -> p c t", p=128)[:, 0:16, tsl], ost[:, 0:16, :], ost, reads=[ost])
            cx.dma("sp", G["orT"].rearrange("(c p) t -> p c t", p=128)[:, 16:32, tsl], ost[:, 16:32, :], ost, reads=[ost])

    skew(8 * NBLK, [(stA, 0), (stB, 0), (stC, 1), (stD, 1), (stE, 3), (stF, 3)])
    return
    if first:
        for i in range(4):
            cx.dma("sp", G["exS_in"][i].rearrange("(h c p) e -> p h c e", p=128, c=2), Sf[:, 2 * i:2 * i + 2], Sf, reads=[Sf])


def retention_local_state(cx, G):
    BT = 256
    cst = G["cst"]
    ident = G["ident"]
    kb = [cx.sb([128, 4, BT], BF16, "lk") for _ in range(3)]
    vb = [cx.sb([128, 8, BT], BF16, "lv") for _ in range(3)]
    kdec = [cx.sb([128, 256], BF16, "lkd") for _ in range(3)]
    vtok = [cx.sb([128, 512], BF16, "lvt") for _ in range(3)]
    psT = [cx.ps([128, 1024], BF16, "lpT") for _ in range(3)]
    acc = [[cx.ps([128, 512], F32, "lacc") for _ in range(2)] for _ in range(2)]
    stg = cx.sb([128, 2, 2, 512], F32, "lstg")
    for hp in range(4):
        NI = (T // BT) * (BT // 128) * 2

        def dec(i):
            blk, rem = divmod(i, (BT // 128) * 2)
            ch, hh = divmod(rem, 2)
            return blk, ch, hh

        def stA(i):
            blk, ch, hh = dec(i)
            k, v = kb[blk % 3], vb[blk % 3]
            if ch == 0 and hh == 0:
                for lb in ([0, 1, 2] if blk == 0 else [blk + 2]):
                    if lb < T // BT:
                        tsl = slice(lb * BT, (lb + 1) * BT)
                        k2, v2 = kb[lb % 3], vb[lb % 3]
                        cx.dma("sp", k2[:], G["krT"][hp * 512:(hp + 1) * 512, :].rearrange("(c p) t -> p c t", p=128)[:, :, tsl], k2, writes=[k2])
                        cx.dma("sp", v2[:], G["vrT"][hp * 1024:(hp + 1) * 1024, :].rearrange("(c p) t -> p c t", p=128)[:, :, tsl], v2, writes=[v2])
            cs = slice(ch * 128, (ch + 1) * 128)
            pT = psT[i % 3]
            for dc in range(2):
                cx.op("pe", lambda p, dc=dc: p.transpose(out=pT[:, dc * 128:(dc + 1) * 128], in_=k[:, 2 * hh + dc, cs], identity=ident[:]),
                      reads=[k, G["identb"]], writes=[pT])
            for ec in range(4):
                cx.op("pe", lambda p, ec=ec: p.transpose(out=pT[:, 256 + ec * 128:256 + (ec + 1) * 128], in_=v[:, 4 * hh + ec, cs], identity=ident[:]),
                      reads=[v, G["identb"]], writes=[pT])

        def stB(i):
            blk, ch, hh = dec(i)
            n = blk * (BT // 128) + ch
            h = 2 * hp + hh
            pT, kd_, vt_ = psT[i % 3], kdec[i % 3], vtok[i % 3]
            col = C_DKABS + n * 8 + h
            cx.op("dve", lambda e: e.tensor_scalar(out=kd_[:], in0=pT[:, 0:256], scalar1=cst[:, col:col + 1], scalar2=None, op0=ALU.mult),
                  reads=[pT, G["cstb"]], writes=[kd_])
            cx.op("act", lambda a: a.activation(out=vt_[:], in_=pT[:, 256:768], func=AF.Copy), reads=[pT], writes=[vt_])

        def stC(i):
            blk, ch, hh = dec(i)
            n = blk * (BT // 128) + ch
            kd_, vt_ = kdec[i % 3], vtok[i % 3]
            for dc in range(2):
                cx.op("pe", lambda p, dc=dc: p.matmul(acc[hh][dc][:], lhsT=kd_[:, dc * 128:(dc + 1) * 128], rhs=vt_[:], start=(n == 0), stop=(n == 31)),
                      reads=[kd_, vt_], writes=[acc[hh][dc]])

        skew(NI, [(stA, 0), (stB, 0), (stC, 1)])
        for hh in range(2):
            for dc in range(2):
                if dc == 0:
                    cx.op("act", lambda a: a.activation(out=stg[:, hh, dc, :], in_=acc[hh][dc][:], func=AF.Copy), reads=[acc[hh][dc]], writes=[stg])
                else:
                    cx.op("dve", lambda e: e.tensor_copy(out=stg[:, hh, dc, :], in_=acc[hh][dc][:]), reads=[acc[hh][dc]], writes=[stg])
        cx.dma("sp", G["exS_in"][hp].rearrange("(h c p) e -> p h c e", p=128, c=2), stg[:], stg, reads=[stg])


def state_init(cx, G, S, zero):
    Sf, Sb = S
    cst = G["cst"]
    if zero:
        cx.op("pool", lambda e: e.memset(Sf[:], 0.0), writes=[Sf])
        cx.op("pool", lambda e: e.memset(Sb[:], 0.0), writes=[Sb])
        return
    gt = [cx.sb([128, 8, 2, 512], F32, "gst") for _ in range(1)]
    g0 = gt[0]
    for r in range(3):
        for i in range(4):
            src = G["exS_out"][i][r * 512:(r + 1) * 512, :].rearrange("(h c p) e -> p h c e", p=128, c=2)
            cx.dma("sp", g0[:, 2 * i:2 * i + 2], src, g0, writes=[g0])
        for h in range(8):
            col = cst[:, C_COEF + r * 8 + h:C_COEF + r * 8 + h + 1]
            if r == 0:
                cx.op("dve", lambda e, h=h, col=col: e.tensor_scalar(out=Sf[:, h], in0=g0[:, h], scalar1=col, scalar2=None, op0=ALU.mult),
                      reads=[g0, G["cstb"]], writes=[Sf])
            else:
                cx.op("dve", lambda e, h=h, col=col: e.scalar_tensor_tensor(out=Sf[:, h], in0=g0[:, h], scalar=col, in1=Sf[:, h], op0=ALU.mult, op1=ALU.add),
                      reads=[g0, G["cstb"]], writes=[Sf])
    cx.op("pool", lambda e: e.tensor_copy(out=Sb[:], in_=Sf[:]), reads=[Sf], writes=[Sb])


ATT_GROUPS = ((128, 1), (512, 4), (2048, 16))
XOVR = None
USE_POW = False


def attention_phase(cx, G):
    cst = G["cst"]
    ident = G["ident"]
    HMAX = 2048
    sets = []
    for _ in range(2):
        sets.append(dict(qh=cx.sb([128, T], BF16, "aq"), kh=cx.sb([128, HMAX + T], BF16, "ak"), vh=cx.sb([128, HMAX + T], BF16, "av"),
                         qd=cx.sb([128, T], BF16, "aqd"), kd=cx.sb([128, HMAX + T], BF16, "akd"), vd=cx.sb([128, HMAX + T], BF16, "avd")))
    hk = [cx.sb([128, HMAX], BF16, "hk") for _ in range(3)]
    hv = [cx.sb([128, HMAX], BF16, "hv") for _ in range(3)]
    NSL = 6
    s_sb = [cx.sb([128, 256], F32, "s") for _ in range(NSL)]
    p_sb = [cx.sb([128, 256], BF16, "p") for _ in range(NSL)]
    pT_sb = [cx.sb([128, 256], BF16, "pT") for _ in range(NSL)]
    NVT = 10
    vt = [cx.sb([128, 128], BF16, "vt") for _ in range(NVT)]
    ost = [cx.sb([128, 128], F32, "ao") for _ in range(NSL)]
    mlst = [cx.sb([128, 2], F32, "ml") for _ in range(NSL)]
    negm = [cx.sb([128, 1], F32, "negm") for _ in range(NSL)]
    bank = [cx.ps([128, 512], F32, "abk") for _ in range(NSL)]
    psV = [cx.ps([128, 1024], BF16, "apV") for _ in range(2)]
    scale = 128.0 ** -0.5
    heads = [(g, hh) for g in range(3) for hh in range(4)]
    state = {"vti": 0, "slot": 0}

    def geom(idx):
        g, hh = heads[idx]
        d = ATT_GROUPS[g][1]
        return g, hh, d, 128 * d, T // d, T // d + 128, (T // d) // 128

    def load_head(idx):
        g, hh, d, HALO, L, LK, nbl = geom(idx)
        S_ = sets[idx % 2]
        H = 4 * g + hh
        rows = slice(H * 128, (H + 1) * 128)
        cx.dma("sp", S_["qh"][:], G["qaT"][rows, :], S_["qh"], writes=[S_["qh"]])
        cx.dma("sp", S_["kh"][:, HALO:HALO + T], G["kaT"][rows, :], S_["kh"], writes=[S_["kh"]])
        cx.dma("sp", S_["vh"][:, HALO:HALO + T], G["vaT"][rows, :], S_["vh"], writes=[S_["vh"]])
        for r in range(3):
            if g == 2:
                ksrc = G["exT2o"][hh // 2][r * 256 + (hh % 2) * 128:r * 256 + (hh % 2) * 128 + 128, :]
                vsrc = G["exT2o"][2 + hh // 2][r * 256 + (hh % 2) * 128:r * 256 + (hh % 2) * 128 + 128, :]
            elif g == 1:
                ksrc = G["exT1o"][r * 1024 + hh * 128:r * 1024 + hh * 128 + 128, :]
                vsrc = G["exT1o"][r * 1024 + 512 + hh * 128:r * 1024 + 512 + hh * 128 + 128, :]
            else:
                ksrc = G["exT0o"][r * 1024 + hh * 128:r * 1024 + hh * 128 + 128, :]
                vsrc = G["exT0o"][r * 1024 + 512 + hh * 128:r * 1024 + 512 + hh * 128 + 128, :]
            cx.dma("sp", hk[r][:, 0:HALO], ksrc, hk[r], writes=[hk[r]])
            cx.dma("sp", hv[r][:, 0:HALO], vsrc, hv[r], writes=[hv[r]])

    def prep_head(idx):
        g, hh, d, HALO, L, LK, nbl = geom(idx)
        S_ = sets[idx % 2]
        qh, kh, vh, qd, kd, vd = S_["qh"], S_["kh"], S_["vh"], S_["qd"], S_["kd"], S_["vd"]
        for (dst, srcs) in ((kh, hk), (vh, hv)):
            cx.op("dve", lambda e, dst=dst, srcs=srcs: e.tensor_scalar(out=dst[:, 0:HALO], in0=srcs[0][:, 0:HALO], scalar1=cst[:, C_SEL:C_SEL + 1], scalar2=None, op0=ALU.mult),
                  reads=[srcs[0], G["cstb"]], writes=[dst])
            for r in (1, 2):
                cx.op("dve", lambda e, dst=dst, srcs=srcs, r=r: e.scalar_tensor_tensor(out=dst[:, 0:HALO], in0=srcs[r][:, 0:HALO], scalar=cst[:, C_SEL + r:C_SEL + r + 1],
                                                                                      in1=dst[:, 0:HALO], op0=ALU.mult, op1=ALU.add),
                      reads=[srcs[r], G["cstb"]], writes=[dst])
        if d > 1:
            cx.op("dve", lambda e: e.tensor_copy(out=qd[:, 0:T].rearrange("p (r l) -> p r l", r=d), in_=qh[:, 0:T].rearrange("p (l r) -> p r l", r=d)),
                  reads=[qh], writes=[qd])
            cx.op("act", lambda e: e.activation(out=kd[:, 0:HALO + T].rearrange("p (r l) -> p r l", r=d), in_=kh[:, 0:HALO + T].rearrange("p (l r) -> p r l", r=d), func=AF.Copy),
                  reads=[kh], writes=[kd])
            cx.op("dve", lambda e: e.tensor_copy(out=vd[:, 0:HALO + T].rearrange("p (r l) -> p r l", r=d), in_=vh[:, 0:HALO + T].rearrange("p (l r) -> p r l", r=d)),
                  reads=[vh], writes=[vd])

    def run_head(idx):
        g, hh, d, HALO, L, LK, nbl = geom(idx)
        S_ = sets[idx % 2]
        if d > 1:
            Q, K, V = S_["qd"], S_["kd"], S_["vd"]
        else:
            Q, K, V = S_["qh"], S_["kh"], S_["vh"]
        items = [(r, b) for r in range(d) for b in range(nbl)]
        vtile = {}
        base = state["slot"]
        state["slot"] += len(items)

        def vtrans(r, bi):
            vti = state["vti"]
            state["vti"] += 1
            tl = vt[vti % NVT]
            pv = psV[vti % 2]
            kb = r * LK
            cx.op("pe", lambda p: p.transpose(out=pv[:, 0:128], in_=V[:, kb + bi * 128:kb + (bi + 1) * 128], identity=ident[:]),
                  reads=[V, G["identb"]], writes=[pv])
            cx.op("act", lambda a: a.activation(out=tl[:], in_=pv[:, 0:128], func=AF.Copy), reads=[pv], writes=[tl])
            vtile[(r, bi)] = tl

        def stA(i):
            if idx + 1 < len(heads):
                if i == 2:
                    load_head(idx + 1)
                if i == 22:
                    prep_head(idx + 1)
            r, b = items[i]
            if b == 0:
                vtrans(r, 0)
            vtrans(r, b + 1)
            bk_ = bank[(base + i) % NSL]
            qb, kb = r * L, r * LK
            cx.op("pe", lambda p: p.matmul(bk_[:, 0:256], lhsT=Q[:, qb + b * 128:qb + (b + 1) * 128], rhs=K[:, kb + b * 128:kb + b * 128 + 256], start=True, stop=True),
                  reads=[Q, K], writes=[bk_])

        def stB(i):
            r, b = items[i]
            sl_ = (base + i) % NSL
            bk_, s_, p_, ml_, nm_ = bank[sl_], s_sb[sl_], p_sb[sl_], mlst[sl_], negm[sl_]
            mcol = C_AMASK + (256 if b == 0 else 0)
            cx.op("dve", lambda e: e.scalar_tensor_tensor(out=s_[:], in0=bk_[:, 0:256], scalar=scale, in1=cst[:, mcol:mcol + 256], op0=ALU.mult, op1=ALU.add),
                  reads=[bk_, G["cstb"]], writes=[s_])
            cx.op("dve", lambda e: e.reduce_max(out=ml_[:, 0:1], in_=s_[:], axis=AX.X), reads=[s_], writes=[ml_])
            cx.op("dve", lambda e: e.tensor_scalar(out=nm_[:], in0=ml_[:, 0:1], scalar1=-1.0, scalar2=None, op0=ALU.mult), reads=[ml_], writes=[nm_])
            cx.op("act", lambda a: a.activation(out=p_[:], in_=s_[:], func=AF.Exp, bias=nm_[:, 0:1], accum_out=ml_[:, 1:2]),
                  reads=[s_, nm_], writes=[p_, ml_])

        def stC(i):
            sl_ = (base + i) % NSL
            bk_, p_, pT_ = bank[sl_], p_sb[sl_], pT_sb[sl_]
            pPTv = bk_[:, 256:384].bitcast(BF16)
            for half in range(2):
                cx.op("pe", lambda p, half=half: p.transpose(out=pPTv[:, half * 128:(half + 1) * 128], in_=p_[:, half * 128:(half + 1) * 128], identity=ident[:]),
                      reads=[p_, G["identb"]], writes=[bk_])
            cx.op("dve", lambda e: e.tensor_copy(out=pT_[:], in_=pPTv[:, 0:256]), reads=[bk_], writes=[pT_])

        def stD(i):
            r, b = items[i]
            sl_ = (base + i) % NSL
            bk_, pT_, o_, ml_ = bank[sl_], pT_sb[sl_], ost[sl_], mlst[sl_]
            vprev, vcur = vtile[(r, b)], vtile[(r, b + 1)]
            cx.op("pe", lambda p: p.matmul(bk_[:, 384:512], lhsT=pT_[:, 0:128], rhs=vprev[:], start=True, stop=False), reads=[pT_, vprev], writes=[bk_])
            cx.op("pe", lambda p: p.matmul(bk_[:, 384:512], lhsT=pT_[:, 128:256], rhs=vcur[:], start=False, stop=True), reads=[pT_, vcur], writes=[bk_])
            cx.op("act", lambda a: a.activation(out=o_[:], in_=bk_[:, 384:512], func=AF.Copy), reads=[bk_], writes=[o_])
            orows = G["oacc"][g].rearrange("(l r) f -> r l f", r=d)[r, b * 128:(b + 1) * 128, hh * 128:(hh + 1) * 128]
            cx.dma("sp", orows, o_[:], o_, reads=[o_])
            mrows = G["mlacc"].rearrange("(l r) f -> r l f", r=d)[r, b * 128:(b + 1) * 128, (g * 4 + hh) * 2:(g * 4 + hh) * 2 + 2]
            cx.dma("sp", mrows, ml_[:], ml_, reads=[ml_])

        skew(len(items), [(stA, 0), (stB, 0), (stC, 2), (stD, 4)])

    load_head(0)
    prep_head(0)
    for idx in range(len(heads)):
        run_head(idx)


def attn_merge_phase(cx, G):
    ident = G["ident"]
    NT_ = T // 128
    o3 = [cx.sb([128, 3, 512], F32, "mo") for _ in range(2)]
    ml = [cx.sb([128, 3, 4, 2], F32, "mml") for _ in range(2)]
    M = cx.sb([128, 4], F32, "mM")
    df = cx.sb([128, 3, 4], F32, "mdf")
    w = cx.sb([128, 3, 4], F32, "mw")
    wl = cx.sb([128, 3, 4], F32, "mwl")
    den = cx.sb([128, 4], F32, "mden")
    rden = cx.sb([128, 4], F32, "mrden")
    wn = cx.sb([128, 3, 4], F32, "mwn")
    acc = cx.sb([128, 512], F32, "macc")
    ob = [cx.sb([128, 512], BF16, "mob") for _ in range(2)]
    stg = [cx.sb([128, 4, 512], BF16, "mstg") for _ in range(2)]
    psT = [cx.ps([128, 1024], BF16, "mpT") for _ in range(2)]
    for tt in range(NT_):
        i2 = tt % 2
        o_, ml_, ob_, pT = o3[i2], ml[i2], ob[i2], psT[i2]
        sg = stg[(tt // 4) % 2]
        rows = slice(tt * 128, (tt + 1) * 128)
        for g in range(3):
            cx.dma("sp", o_[:, g, :], G["oacc"][g][rows, :], o_, writes=[o_])
        cx.dma("sp", ml_[:], G["mlacc"][rows, :].rearrange("t (g h k) -> t g h k", g=3, h=4), ml_, writes=[ml_])
        cx.op("dve", lambda e: e.tensor_tensor(out=M[:], in0=ml_[:, 0, :, 0], in1=ml_[:, 1, :, 0], op=ALU.max), reads=[ml_], writes=[M])
        cx.op("dve", lambda e: e.tensor_tensor(out=M[:], in0=M[:], in1=ml_[:, 2, :, 0], op=ALU.max), reads=[ml_], writes=[M])
        for g in range(3):
            cx.op("dve", lambda e, g=g: e.tensor_tensor(out=df[:, g, :], in0=ml_[:, g, :, 0], in1=M[:], op=ALU.subtract), reads=[ml_, M], writes=[df])
        cx.op("act", lambda a: a.activation(out=w[:], in_=df[:], func=AF.Exp), reads=[df], writes=[w])
        cx.op("dve", lambda e: e.tensor_tensor(out=wl[:], in0=w[:], in1=ml_[:, :, :, 1], op=ALU.mult), reads=[w, ml_], writes=[wl])
        cx.op("dve", lambda e: e.tensor_tensor(out=den[:], in0=wl[:, 0, :], in1=wl[:, 1, :], op=ALU.add), reads=[wl], writes=[den])
        cx.op("dve", lambda e: e.tensor_tensor(out=den[:], in0=den[:], in1=wl[:, 2, :], op=ALU.add), reads=[wl], writes=[den])
        cx.op("dve", lambda e: e.reciprocal(out=rden[:], in_=den[:]), reads=[den], writes=[rden])
        for g in range(3):
            cx.op("dve", lambda e, g=g: e.tensor_tensor(out=wn[:, g, :], in0=w[:, g, :], in1=rden[:], op=ALU.mult), reads=[w, rden], writes=[wn])
        for hh in range(4):
            hs = slice(hh * 128, (hh + 1) * 128)
            cx.op("dve", lambda e, hh=hh, hs=hs: e.tensor_scalar(out=acc[:, hs], in0=o_[:, 0, hs], scalar1=wn[:, 0, hh:hh + 1], scalar2=None, op0=ALU.mult),
                  reads=[o_, wn], writes=[acc])
            cx.op("dve", lambda e, hh=hh, hs=hs: e.scalar_tensor_tensor(out=acc[:, hs], in0=o_[:, 1, hs], scalar=wn[:, 1, hh:hh + 1], in1=acc[:, hs], op0=ALU.mult, op1=ALU.add),
                  reads=[o_, wn], writes=[acc])
            cx.op("dve", lambda e, hh=hh, hs=hs: e.scalar_tensor_tensor(out=ob_[:, hs], in0=o_[:, 2, hs], scalar=wn[:, 2, hh:hh + 1], in1=acc[:, hs], op0=ALU.mult, op1=ALU.add),
                  reads=[o_, wn, acc], writes=[ob_])
        for hh in range(4):
            cx.op("pe", lambda p, hh=hh: p.transpose(out=pT[:, hh * 128:(hh + 1) * 128], in_=ob_[:, hh * 128:(hh + 1) * 128], identity=ident[:]),
                  reads=[ob_, G["identb"]], writes=[pT])
        q4 = tt % 4
        cx.op("act", lambda a: a.activation(out=sg[:, :, q4 * 128:(q4 + 1) * 128], in_=pT[:, 0:512].rearrange("p (h t) -> p h t", h=4), func=AF.Copy),
              reads=[pT], writes=[sg])
        if q4 == 3:
            t0 = (tt - 3) * 128
            cx.dma("sp", G["oaT"].rearrange("(c p) t -> p c t", p=128)[:, :, t0:t0 + 512], sg[:], sg, reads=[sg])


def build():
    nc = bass.Bass("TRN2", target_bir_lowering=False)
    cx = Cx(nc)
    G = {}

    def din(name, shape, dt=F32):
        return nc.dram_tensor(name, list(shape), dt, kind="ExternalInput").ap()

    DBGT = {}

    def dscr(name, shape, dt):
        a = nc.dram_tensor(name, list(shape), dt, kind="Internal").ap()
        if name in DBG:
            DBGT[name] = (a, nc.dram_tensor("dbg_" + name, [shape[0], 256], dt, kind="ExternalOutput").ap())
        return a

    def finish():
        db = Buf("dbgcopy")
        for name, (a, o) in DBGT.items():
            cx.dma("sp", o, a[:, 0:256], db)
        cx.barrier()
        return nc

    xT = din("xT", [D, T])
    WSHAPE = {"ffn1_w_gate": [D, DFF], "ffn1_w_up": [D, DFF], "ffn1_w_down": [DFF, D], "w_in": [D, INC],
              "w_proj_attn": [512, D], "w_proj_ret": [4096, D], "w_out": [D, D],
              "ffn2_w_gate": [D, DFF], "ffn2_w_up": [D, DFF], "ffn2_w_down": [DFF, D]}
    USED_W.clear()

    class _WL:
        def __init__(self, nm):
            self.nm = nm

        def __getitem__(self, l):
            key = "%s_%d" % (self.nm, l)
            if key not in USED_W:
                USED_W[key] = (self.nm, l, din(key, WSHAPE[self.nm]))
            return USED_W[key][2]

    W = {nm: _WL(nm) for nm in WSHAPE}
    cst_d = din("cst", [128, CW])
    G["ropeA"] = din("ropeA", [2, 128, T])
    G["ropeR"] = din("ropeR", [2, 128, T])
    yT = nc.dram_tensor("yT", [D, T], F32, kind="ExternalOutput").ap()

    XA = dscr("XA", [D, T], F32)
    XB = dscr("XB", [D, T], F32)
    uT = dscr("uT", [DFF, T], BF16)
    for nm, rows in (("qaT", 1536), ("kaT", 1536), ("vaT", 1536), ("qrT", 2048), ("krT", 2048), ("vrT", 4096), ("sgT", 4096),
                     ("gaT", 2048), ("gbT", 2048), ("oaT", 512), ("orT", 4096), ("mT", 2048)):
        G[nm] = dscr(nm, [rows, T], BF16)
    G["oacc"] = [dscr("oacc%d" % g, [T, 512], F32) for g in range(3)]
    G["mlacc"] = dscr("mlacc", [T, 24], F32)
    EX = []
    G["exS_in"], G["exS_out"] = [], []
    for i in range(4):
        a = nc.dram_tensor("exS_in%d" % i, [512, 512], F32)
        b = nc.dram_tensor("exS_out%d" % i, [4 * 512, 512], F32)
        EX.append((a, b))
        G["exS_in"].append(a.ap())
        G["exS_out"].append(b.ap())
    G["exT2"], G["exT2o"] = [], []
    for j in range(4):
        a = nc.dram_tensor("exT2_%d" % j, [256, 2048], BF16)
        b = nc.dram_tensor("exT2o_%d" % j, [4 * 256, 2048], BF16)
        EX.append((a, b))
        G["exT2"].append(a.ap())
        G["exT2o"].append(b.ap())
    a = nc.dram_tensor("exT1", [1024, 512], BF16)
    b = nc.dram_tensor("exT1o", [4 * 1024, 512], BF16)
    EX.append((a, b))
    G["exT1"], G["exT1o"] = a.ap(), b.ap()
    a = nc.dram_tensor("exT0", [1024, 128], BF16)
    b = nc.dram_tensor("exT0o", [4 * 1024, 128], BF16)
    EX.append((a, b))
    G["exT0"], G["exT0o"] = a.ap(), b.ap()
    cc_sems = [nc.alloc_semaphore("cc%d" % i) for i in range(len(EX) * DEPTH)]
    ccn = [0]

    cst = nc.alloc_sbuf_tensor("cst_sb", [128, CW], F32)
    identb = nc.alloc_sbuf_tensor("identb", [128, 128], BF16)
    ones = nc.alloc_sbuf_tensor("ones", [128, 128], BF16)
    eps = nc.alloc_sbuf_tensor("eps", [128, 2], F32)
    G["cst"], G["ident"], G["ones"], G["eps"] = cst, identb, ones, eps
    G["cstb"], G["identb"], G["onesb"] = Buf("cst"), Buf("ident"), Buf("ones")
    cx.dma("sp", cst[:], cst_d, G["cstb"], writes=[G["cstb"]])
    cx.op("dve", lambda e: e.tensor_copy(out=identb[:], in_=cst[:, C_ID:C_ID + 128]), reads=[G["cstb"]], writes=[G["identb"]])
    cx.op("pool", lambda e: e.memset(ones[:], 1.0), writes=[G["onesb"]])
    cx.op("pool", lambda e: e.memset(eps[:, 0:1], RMS_EPS), writes=[G["cstb"]])
    cx.op("pool", lambda e: e.memset(eps[:, 1:2], GN_EPS), writes=[G["cstb"]])
    cx.barrier()

    def ffn(l, which, xi, xo_, gidx):
        wg, wu, wd = W["ffn%d_w_gate" % which][l], W["ffn%d_w_up" % which][l], W["ffn%d_w_down" % which][l]
        groups = []
        for s in range(DFF // 256):
            groups.append(dict(outs=[(wg[:, s * 256:(s + 1) * 256], 0), (wu[:, s * 256:(s + 1) * 256], 0)],
                               units=[[(0, 0), (1, 0)], [(0, 128), (1, 128)]], kind="swiglu", dst=uT, row0=s * 256))
        with cx.phase("ffn_up"):
            linear_phase(cx, G, 2048, 256, groups, norm=(xi, gidx), need=("stgA", "tmp"))
        if STOP == "ffn_up":
            return True
        with cx.phase("ffn_down"):
            linear_phase(cx, G, 1024, 0, simple_groups(wd, D, "res", xo_, xin=xi, scale=0.5), xsrc=[uT], need=("res",))
        return False

    def exchange(part):
        sems = []
        for (i_, o_) in part:
            ins = nc.gpsimd.collective_compute("AllGather", ALU.bypass, replica_groups=[[0, 1, 2, 3], [4, 5, 6, 7]],
                                               ins=[i_.ap().opt()], outs=[o_.ap().opt()])
            sem = cc_sems[ccn[0]]
            ccn[0] += 1
            ins.then_inc(sem)
            nc.gpsimd.wait_ge(sem, 1)
            sems.append(sem)
        return sems

    def wait_cc(sems):
        for sem in sems:
            for e in ("pe", "act", "dve", "sp"):
                cx.eng[e].wait_ge(sem, 1)

    xcur, xnxt = xT, XA

    def done():
        return nc

    for l in range(DEPTH):
        if not ("ffn1" in SKIP and l == 0):
            if ffn(l, 1, xcur, xnxt, l * 3 + 0):
                return finish()
            xcur, xnxt = xnxt, (XB if xnxt is XA else XA)
        if STOP == "ffn1":
            return finish()
        wi = W["w_in"][l]
        groups = []
        for i in range(6):
            groups.append(dict(outs=[(wi[:, O_QA + i * 256:O_QA + (i + 1) * 256], 0)], units=[[(0, 0), (0, 128)]], kind="ropeA", dst=G["qaT"], row0=i * 256))
        for i in range(6):
            groups.append(dict(outs=[(wi[:, O_KA + i * 256:O_KA + (i + 1) * 256], 0)], units=[[(0, 0), (0, 128)]], kind="ropeA", dst=G["kaT"], row0=i * 256))
        for i in range(8):
            groups.append(dict(outs=[(wi[:, O_QR + i * 256:O_QR + (i + 1) * 256], 0)], units=[[(0, 0), (0, 128)]], kind="ropeR", dst=G["qrT"], row0=i * 256))
        for i in range(8):
            groups.append(dict(outs=[(wi[:, O_KR + i * 256:O_KR + (i + 1) * 256], 0)], units=[[(0, 0), (0, 128)]], kind="ropeR", dst=G["krT"], row0=i * 256))
        groups += simple_groups(wi, 1536, "copy", G["vaT"], col0=O_VA)
        groups += simple_groups(wi, 4096, "copy", G["vrT"], col0=O_VR)
        groups += simple_groups(wi, 4096, "silu", G["sgT"], col0=O_GR)
        groups += simple_groups(wi, 2048, "sigmoid", G["gaT"], col0=O_GA)
        groups += simple_groups(wi, 2048, "sigmoid", G["gbT"], col0=O_GB)
        if "w_in" not in SKIP:
            with cx.phase("w_in"):
                linear_phase(cx, G, 2048, 256, groups, norm=(xcur, l * 3 + 1), need=("stgA", "stgB", "tmp", "rope"))
        if STOP == "w_in":
            return finish()
        with cx.phase("tails"):
            tb_ = Buf("tails")
            for j in range(2):
                cx.dma("sp", G["exT2"][j], G["kaT"][1024 + j * 256:1024 + (j + 1) * 256, T - 2048:T], tb_)
                cx.dma("sp", G["exT2"][2 + j], G["vaT"][1024 + j * 256:1024 + (j + 1) * 256, T - 2048:T], tb_)
            cx.dma("sp", G["exT1"][0:512, :], G["kaT"][512:1024, T - 512:T], tb_)
            cx.dma("sp", G["exT1"][512:1024, :], G["vaT"][512:1024, T - 512:T], tb_)
            cx.dma("sp", G["exT0"][0:512, :], G["kaT"][0:512, T - 128:T], tb_)
            cx.dma("sp", G["exT0"][512:1024, :], G["vaT"][0:512, T - 128:T], tb_)
        tsems = exchange(EX[4:10]) if "exch" not in SKIP else []
        with cx.phase("ret1"):
            retention_local_state(cx, G)
        if STOP == "ret1":
            return finish()
        ssems = exchange(EX[0:4]) if "exch" not in SKIP else []
        wait_cc(tsems)
        if STOP == "exch":
            wait_cc(ssems)
            return finish()
        with cx.phase("attn"):
            attention_phase(cx, G)
        with cx.phase("merge"):
            attn_merge_phase(cx, G)
        wait_cc(ssems)
        with cx.phase("ret2"):
            Sf = cx.sb([128, 8, 2, 512], F32, "Sf")
            Sb = cx.sb([128, 8, 2, 512], BF16, "Sb")
            with ExitStack() as st2:
                old = cx.st
                cx.st = st2
                state_init(cx, G, (Sf, Sb), False)
                cx.barrier()
                cx.st = old
            retention_pass(cx, G, (Sf, Sb), False)
        if STOP in ("ret", "attn"):
            return finish()
        pa, pr = W["w_proj_attn"][l], W["w_proj_ret"][l]
        groups = []
        for s in range(D // 256):
            groups.append(dict(outs=[(pa[:, s * 256:(s + 1) * 256], 0), (pr[:, s * 256:(s + 1) * 256], 4)],
                               units=[[(0, 0), (1, 0)], [(0, 128), (1, 128)]], kind="gate", dst=G["mT"], row0=s * 256, ga=G["gaT"], gb=G["gbT"]))
        with cx.phase("proj"):
            linear_phase(cx, G, 1024, 0, groups, xsrc=[G["oaT"], G["orT"]], need=("stgA", "tmp", "gate"))
        if STOP == "proj":
            return finish()
        with cx.phase("w_out"):
            linear_phase(cx, G, 2048, 0, simple_groups(W["w_out"][l], D, "res", xnxt, xin=xcur, scale=1.0), xsrc=[G["mT"]], need=("res",))
        xcur, xnxt = xnxt, (XB if xnxt is XA else XA)
        if STOP == "w_out":
            return finish()
        if ffn(l, 2, xcur, xnxt, l * 3 + 2):
            return finish()
        xcur, xnxt = xnxt, (XB if xnxt is XA else XA)
        if STOP == "layer0":
            return finish()
    with cx.phase("final"):
        final_norm_phase(cx, G, xcur, 6, yT)
    print("kernel build: instructions ~", cx.ninst, "counts", cx.cnt)
    return nc


def _consts(qd):
    c = np.zeros((128, CW), np.float64)
    c[:, C_ID:C_ID + 128] = np.eye(128)
    i = np.arange(128)
    for h in range(8):
        gm = GAMMA[h]
        rel = i[None, :] - i[:, None]
        m = np.where(rel >= 0, gm ** np.maximum(rel, 0), 0.0) * (256.0 ** -0.5)
        c[:, C_RMASK + h * 128:C_RMASK + (h + 1) * 128] = m
        c[:, C_RDQ + h * 128:C_RDQ + (h + 1) * 128] = (gm ** (i + 1.0))[None, :]
        c[:, C_RDK + h] = gm ** (127.0 - i) * (256.0 ** -0.5)
        for n in range(32):
            dk = gm ** (T - 1.0 - (n * 128 + i)) * (256.0 ** -0.5)
            c[:, C_DKABS + n * 8 + h] = np.where(dk < 1e-30, 0.0, dk)
        for r in range(4):
            c[:, C_COEF + r * 8 + h] = (gm ** (4096.0 * (qd - r - 1))) if r < qd else 0.0
    for r in range(4):
        c[:, C_SEL + r] = 1.0 if r == qd - 1 else 0.0
    qi = i[:, None]
    kj = np.arange(256)[None, :]
    band = np.where((kj >= qi) & (kj <= qi + 128), 0.0, NEG)
    c[:, C_AMASK:C_AMASK + 256] = band
    first = band.copy()
    if qd == 0:
        first[:, 0:128] = NEG
    c[:, C_AMASK + 256:C_AMASK + 512] = first
    return c


def _rope(qd, dim, reps):
    inv = (1.0 / (10000.0 ** (np.arange(0, dim, 2, dtype=np.float32) / np.float32(dim)))).astype(np.float32)
    pos = (np.arange(T, dtype=np.float32) + np.float32(qd * T)).astype(np.float32)
    ang = (pos[None, :] * inv[:, None]).astype(np.float32)
    cs = np.stack([np.cos(ang), np.sin(ang)]).astype(np.float32)
    return np.ascontiguousarray(np.tile(cs, (1, reps, 1)))


def _perm_w_in(w_in):
    w = np.array(w_in, copy=True)
    for base in (0, 1536):
        blk = w_in[:, :, base:base + 1536].reshape(DEPTH, D, 6, 2, 2, 64)
        w[:, :, base:base + 1536] = blk.transpose(0, 1, 2, 4, 3, 5).reshape(DEPTH, D, 1536)
    return w


def kernel(x, ffn1_norm, ffn1_w_gate, ffn1_w_up, ffn1_w_down, mix_norm, w_in, w_proj_attn, w_proj_ret, w_out,
           ffn2_norm, ffn2_w_gate, ffn2_w_up, ffn2_w_down, final_norm):
    x = np.asarray(x, np.float32)
    gains = [ffn1_norm[0], mix_norm[0], ffn2_norm[0], ffn1_norm[1], mix_norm[1], ffn2_norm[1], final_norm]
    gcols = np.concatenate([np.asarray(g, np.float32).reshape(16, 128).T for g in gains], axis=1)
    wperm = _perm_w_in(np.asarray(w_in, np.float32))
    allw = {"ffn1_w_gate": ffn1_w_gate, "ffn1_w_up": ffn1_w_up, "ffn1_w_down": ffn1_w_down, "w_in": wperm,
            "w_proj_attn": w_proj_attn, "w_proj_ret": w_proj_ret, "w_out": w_out,
            "ffn2_w_gate": ffn2_w_gate, "ffn2_w_up": ffn2_w_up, "ffn2_w_down": ffn2_w_down}
    nc = build()
    shared = {key: np.ascontiguousarray(np.asarray(allw[nm][l], np.float32)) for key, (nm, l, _) in USED_W.items()}
    in_maps = []
    for c in range(NCORES):
        b, qd = c // 4, c % 4
        m = dict(shared)
        m["xT"] = np.ascontiguousarray(x[b, qd * T:(qd + 1) * T, :].T)
        if XOVR is not None and c == 0:
            m["xT"][:, :XOVR.shape[0]] = XOVR.T
        cc = _consts(qd)
        cc[:, C_GAIN:C_GAIN + 112] = gcols
        m["cst"] = cc.astype(np.float32)
        m["ropeA"] = _rope(qd, 128, 2)
        m["ropeR"] = _rope(qd, 256, 1)
        in_maps.append(m)
    res = run_bass_kernel_spmd(nc, in_maps, core_ids=list(range(NCORES)))
    kernel.last = res
    out = np.empty((2, 4 * T, D), np.float32)
    for c in range(NCORES):
        b, qd = c // 4, c % 4
        out[b, qd * T:(qd + 1) * T, :] = res.results[c]["yT"].T
    return out
```
